# Optimizing a Trainium2 kernel written in Bass

```python
import math
import numpy as np
import jax
import jax.numpy as jnp
from jax import lax

D_MODEL = 1024
BATCH = 32
SEQ = 256
DEPTH = 4
DEC_BATCH = 2
DEC_SEQ = 2048
PAST_LEN = 256

GRID_W = 64
HEAD_DIM = 64
N_HEADS_TOTAL = D_MODEL // HEAD_DIM
A_HEADS = 3 * N_HEADS_TOTAL // 8
A_KV_HEADS = A_HEADS // 3
B_HEADS = N_HEADS_TOTAL // 4
C_HEADS = N_HEADS_TOTAL - A_HEADS - B_HEADS
A_WIDTH = A_HEADS * HEAD_DIM
A_KV_WIDTH = A_KV_HEADS * HEAD_DIM
B_WIDTH = B_HEADS * HEAD_DIM
C_WIDTH = C_HEADS * HEAD_DIM
MIX_WIDTH = A_WIDTH + B_WIDTH + C_WIDTH
N_DIR = 2
CONV_K = 3
CHUNK = 64
Q_BLOCK = 128
WIN_R = 8
WIN_C = 16
ROPE_THETA = 10000.0
D_FF = 256 * math.ceil(8 * D_MODEL / (3 * 256))
N_MOD = 6
PROJ_SPLITS = (A_WIDTH, A_KV_WIDTH, A_KV_WIDTH, 3 * B_WIDTH, B_WIDTH,
               N_DIR * B_HEADS, N_DIR * B_HEADS, C_WIDTH, C_WIDTH, C_WIDTH)
IN_WIDTH = sum(PROJ_SPLITS)
EPS = 1e-6

kernel_name = 'hybrid_flow_trunk_step'


def rmsnorm(x, g):
    xf = x.astype(jnp.float32)
    y = xf * lax.rsqrt(jnp.mean(xf * xf, axis=-1, keepdims=True) + EPS)
    return (y * g.astype(jnp.float32)).astype(x.dtype)


def l2norm(x):
    return x * lax.rsqrt(jnp.sum(x * x, axis=-1, keepdims=True) + EPS)


def axial_rope(x):
    s, d = x.shape[1], x.shape[-1]
    quarter = d // 4
    inv = ROPE_THETA ** (-jnp.arange(quarter, dtype=jnp.float32) / quarter)
    t = jnp.arange(s)
    row = (t // GRID_W).astype(jnp.float32)
    col = (t % GRID_W).astype(jnp.float32)
    ang = jnp.concatenate([row[:, None] * inv, col[:, None] * inv], axis=-1)
    cos = jnp.cos(ang)[None, :, None, :]
    sin = jnp.sin(ang)[None, :, None, :]
    xf = x.astype(jnp.float32)
    x1, x2 = xf[..., : d // 2], xf[..., d // 2:]
    return jnp.concatenate([x1 * cos - x2 * sin, x1 * sin + x2 * cos], axis=-1).astype(x.dtype)


def modulation(cvec, w_mod, b_mod):
    m = jax.nn.silu(cvec) @ w_mod + b_mod
    return jnp.split(m[..., None, :], N_MOD, axis=-1)


def split_projection(z):
    idx = np.cumsum(PROJ_SPLITS)[:-1].tolist()
    return jnp.split(z, idx, axis=-1)


def blocked_attention(q, k, v):
    b, s, hq, d = q.shape
    hkv = k.shape[2]
    grp = hq // hkv
    nb = s // Q_BLOCK
    qb = jnp.moveaxis(q.reshape(b, nb, Q_BLOCK, hkv, grp, d), 1, 0)
    scale = d ** -0.5

    def one_block(q_blk):
        sc = jnp.einsum('bqhgd,bkhd->bhgqk', q_blk, k).astype(jnp.float32) * scale
        p = jax.nn.softmax(sc, axis=-1).astype(v.dtype)
        return jnp.einsum('bhgqk,bkhd->bqhgd', p, v)

    o = lax.map(one_block, qb)
    return jnp.moveaxis(o, 0, 1).reshape(b, s, hq * d)


def neighbourhood_attention(q, k, v, k_ctx, v_ctx, rpb):
    b, s, h, d = q.shape
    rows = s // GRID_W
    wr = min(WIN_R, rows)
    r = jnp.arange(rows)
    r_start = jnp.clip(r - wr // 2, 0, rows - wr)
    ridx = r_start[:, None] + jnp.arange(wr)[None, :]
    col = jnp.arange(GRID_W)
    c_start = jnp.clip(col - WIN_C // 2, 0, GRID_W - WIN_C)
    cmask = (col[None, :] >= c_start[:, None]) & (col[None, :] < c_start[:, None] + WIN_C)
    mask = jnp.broadcast_to(cmask[:, None, :], (GRID_W, wr, GRID_W)).reshape(GRID_W, wr * GRID_W)
    qg = q.reshape(b, rows, GRID_W, h, d)
    kg = k.reshape(b, rows, GRID_W, h, d)[:, ridx].reshape(b, rows, wr * GRID_W, h, d)
    vg = v.reshape(b, rows, GRID_W, h, d)[:, ridx].reshape(b, rows, wr * GRID_W, h, d)
    dr = ridx - r[:, None] + (WIN_R - 1)
    dc = jnp.clip(col[None, :] - col[:, None] + (WIN_C - 1), 0, 2 * WIN_C - 2)
    bias = rpb[:, dr[:, None, :, None], dc[None, :, None, :]]
    bias = bias.reshape(h, rows, GRID_W, wr * GRID_W).astype(jnp.float32)
    scale = d ** -0.5
    s_loc = jnp.einsum('brqhd,brkhd->bhrqk', qg, kg).astype(jnp.float32) * scale + bias[None]
    s_loc = jnp.where(mask, s_loc, -jnp.inf)
    s_ctx = jnp.einsum('brqhd,bkhd->bhrqk', qg, k_ctx).astype(jnp.float32) * scale
    p = jax.nn.softmax(jnp.concatenate([s_loc, s_ctx], axis=-1), axis=-1).astype(v.dtype)
    p_loc, p_ctx = p[..., : wr * GRID_W], p[..., wr * GRID_W:]
    o = jnp.einsum('bhrqk,brkhd->brqhd', p_loc, vg) + jnp.einsum('bhrqk,bkhd->brqhd', p_ctx, v_ctx)
    return o.reshape(b, s, h * d)


def gated_delta_chunked(q, k, v, g, beta, s0):
    b, L, h, d = q.shape
    n = L // CHUNK

    def blocks(t):
        t = t.reshape((b, n, CHUNK, h) + t.shape[3:])
        return jnp.moveaxis(jnp.swapaxes(t, 2, 3), 1, 0)

    qc, kc, vc, bc = blocks(q), blocks(k), blocks(v), blocks(beta)
    gc = jnp.cumsum(blocks(g), axis=-1)
    tril = jnp.tril(jnp.ones((CHUNK, CHUNK), dtype=bool))
    strict = jnp.tril(jnp.ones((CHUNK, CHUNK), dtype=bool), -1)
    decay = jnp.exp(jnp.where(tril, gc[..., :, None] - gc[..., None, :], -jnp.inf))
    kb = kc * bc[..., None]
    lmat = jnp.where(strict, jnp.einsum('nbhid,nbhjd->nbhij', kb, kc) * decay, 0.0)
    rhs = jnp.concatenate([vc * bc[..., None], kb * jnp.exp(gc)[..., None]], axis=-1)
    sol = lax.linalg.triangular_solve(lmat + jnp.eye(CHUNK, dtype=lmat.dtype), rhs,
                                      left_side=True, lower=True, unit_diagonal=True)
    u, w = jnp.split(sol, 2, axis=-1)
    a_intra = jnp.where(tril, jnp.einsum('nbhid,nbhjd->nbhij', qc, kc) * decay, 0.0)

    def step(s, xs):
        q_i, k_i, u_i, w_i, g_i, a_i = xs
        v_new = u_i - w_i @ s
        o_i = (q_i * jnp.exp(g_i)[..., None]) @ s + a_i @ v_new
        g_last = g_i[..., -1:]
        s = s * jnp.exp(g_last)[..., None] + jnp.einsum(
            'bhcd,bhce->bhde', k_i * jnp.exp(g_last - g_i)[..., None], v_new)
        return s, o_i

    s_fin, o = lax.scan(step, s0, (qc, kc, u, w, gc, a_intra))
    o = jnp.swapaxes(jnp.moveaxis(o, 0, 1), 2, 3).reshape(b, L, h, d)
    return o, s_fin


def deltanet_prep(b_qkv, b_beta, b_alpha, conv_w, a_log, dt_bias):
    b, L, _ = b_qkv.shape
    pad = CONV_K // 2
    xp = jnp.pad(b_qkv, ((0, 0), (pad, pad), (0, 0)))
    y = xp[:, 0:L] * conv_w[0]
    for j in range(1, CONV_K):
        y = y + xp[:, j:j + L] * conv_w[j]
    y = jax.nn.silu(y).astype(jnp.float32)
    q, k, v = jnp.split(y, 3, axis=-1)
    q = l2norm(q.reshape(b, L, B_HEADS, HEAD_DIM)) * (HEAD_DIM ** -0.5)
    k = l2norm(k.reshape(b, L, B_HEADS, HEAD_DIM))
    v = v.reshape(b, L, B_HEADS, HEAD_DIM)
    beta = jax.nn.sigmoid(b_beta.astype(jnp.float32)).reshape(b, L, N_DIR, B_HEADS)
    g = -jnp.exp(a_log.astype(jnp.float32)) * jax.nn.softplus(
        b_alpha.astype(jnp.float32).reshape(b, L, N_DIR, B_HEADS) + dt_bias.astype(jnp.float32))
    return q, k, v, beta, g


def bidir_delta(q, k, v, beta, g, s0_fwd, s0_bwd):
    o_f, s_f = gated_delta_chunked(q, k, v, g[:, :, 0], beta[:, :, 0], s0_fwd)
    o_b, s_b = gated_delta_chunked(q[:, ::-1], k[:, ::-1], v[:, ::-1],
                                   g[:, ::-1, 1], beta[:, ::-1, 1], s0_bwd)
    return o_f + o_b[:, ::-1], jnp.stack([s_f, s_b], axis=1)


def deltanet_out(o, b_g, g_onorm):
    b, L = o.shape[:2]
    gate = jax.nn.silu(b_g.astype(jnp.float32)).reshape(b, L, B_HEADS, HEAD_DIM)
    return (rmsnorm(o, g_onorm) * gate).reshape(b, L, B_WIDTH)


def merge_groups(o_a, o_b, o_c, g_out_a, g_out_c, w_out):
    y = jnp.concatenate([rmsnorm(o_a, g_out_a), o_b.astype(o_a.dtype), rmsnorm(o_c, g_out_c)], axis=-1)
    return y @ w_out


def context_mixer(h, w_in, g_qk_a, g_out_a, conv_w, a_log, dt_bias, g_onorm_b, g_out_c, w_out):
    b, L, _ = h.shape
    a_q, a_k, a_v, b_qkv, b_g, b_beta, b_alpha, c_q, c_k, c_v = split_projection(h @ w_in)
    qa = rmsnorm(a_q.reshape(b, L, A_HEADS, HEAD_DIM), g_qk_a[0])
    ka = rmsnorm(a_k.reshape(b, L, A_KV_HEADS, HEAD_DIM), g_qk_a[1])
    va = a_v.reshape(b, L, A_KV_HEADS, HEAD_DIM)
    o_a = blocked_attention(qa, ka, va)
    q, k, v, beta, g = deltanet_prep(b_qkv, b_beta, b_alpha, conv_w, a_log, dt_bias)
    zero = jnp.zeros((b, B_HEADS, HEAD_DIM, HEAD_DIM), jnp.float32)
    o_b, s_b = bidir_delta(q, k, v, beta, g, zero, zero)
    o_b = deltanet_out(o_b, b_g, g_onorm_b)
    qc = c_q.reshape(b, L, C_HEADS, HEAD_DIM)
    kc = c_k.reshape(b, L, C_HEADS, HEAD_DIM)
    vc = c_v.reshape(b, L, C_HEADS, HEAD_DIM)
    o_c = blocked_attention(qc, kc, vc)
    m = merge_groups(o_a, o_b, o_c, g_out_a, g_out_c, w_out)
    return m, (ka, va, s_b, kc, vc)


def latent_mixer(h, ka_ctx, va_ctx, s_ctx, kc_ctx, vc_ctx, w_in, g_qk_a, g_out_a, conv_w, a_log,
                 dt_bias, g_onorm_b, rpb, g_out_c, w_out):
    b, L, _ = h.shape
    dt = h.dtype
    a_q, a_k, a_v, b_qkv, b_g, b_beta, b_alpha, c_q, c_k, c_v = split_projection(h @ w_in)
    qa = axial_rope(rmsnorm(a_q.reshape(b, L, A_HEADS, HEAD_DIM), g_qk_a[0]))
    ka = axial_rope(rmsnorm(a_k.reshape(b, L, A_KV_HEADS, HEAD_DIM), g_qk_a[1]))
    va = a_v.reshape(b, L, A_KV_HEADS, HEAD_DIM)
    o_a = blocked_attention(qa, jnp.concatenate([ka, ka_ctx.astype(dt)], axis=1),
                            jnp.concatenate([va, va_ctx.astype(dt)], axis=1))
    q, k, v, beta, g = deltanet_prep(b_qkv, b_beta, b_alpha, conv_w, a_log, dt_bias)
    s_ctx = s_ctx.astype(jnp.float32)
    o_b, _ = bidir_delta(q, k, v, beta, g, s_ctx[:, 0], s_ctx[:, 1])
    o_b = deltanet_out(o_b, b_g, g_onorm_b)
    o_c = neighbourhood_attention(c_q.reshape(b, L, C_HEADS, HEAD_DIM), c_k.reshape(b, L, C_HEADS, HEAD_DIM),
                                  c_v.reshape(b, L, C_HEADS, HEAD_DIM), kc_ctx.astype(dt), vc_ctx.astype(dt), rpb)
    return merge_groups(o_a, o_b, o_c, g_out_a, g_out_c, w_out)


def sandwich_block(x, mod, g_norm, mixer, w_gu, w_down):
    sh1, sc1, gt1, sh2, sc2, gt2 = mod
    h = rmsnorm(x, g_norm[0]) * (1.0 + sc1) + sh1
    m, extra = mixer(h)
    x = x + gt1 * rmsnorm(m, g_norm[1])
    h = rmsnorm(x, g_norm[2]) * (1.0 + sc2) + sh2
    gate, up = jnp.split(h @ w_gu, 2, axis=-1)
    f = (jax.nn.silu(gate) * up) @ w_down
    x = x + gt2 * rmsnorm(f, g_norm[3])
    return x, extra


def setup_inputs(seed: int = 0) -> dict:
    key = jax.random.key(seed)
    ks = jax.random.split(key, 26)
    f32 = jnp.float32
    D = D_MODEL

    def nrm(k, shape, s):
        return s * jax.random.normal(k, shape, f32)

    a_log = jnp.log(jax.random.uniform(ks[20], (DEPTH, N_DIR, B_HEADS), f32, 1.0, 16.0))
    dtv = jnp.exp(jax.random.uniform(ks[21], (DEPTH, N_DIR, B_HEADS), f32, math.log(1e-3), math.log(0.1)))
    dt_bias = dtv + jnp.log(-jnp.expm1(-dtv))
    return {
        'x_prompt': nrm(ks[0], (BATCH, SEQ, D), 1.0),
        'x_sample': nrm(ks[1], (DEC_BATCH, DEC_SEQ, D), 1.0),
        'cache_a_k': nrm(ks[2], (DEC_BATCH, DEPTH, PAST_LEN, A_KV_HEADS, HEAD_DIM), 1.0),
        'cache_a_v': nrm(ks[3], (DEC_BATCH, DEPTH, PAST_LEN, A_KV_HEADS, HEAD_DIM), 1.0),
        'state_b': nrm(ks[4], (DEC_BATCH, DEPTH, N_DIR, B_HEADS, HEAD_DIM, HEAD_DIM), HEAD_DIM ** -0.5),
        'cache_c_k': nrm(ks[5], (DEC_BATCH, DEPTH, PAST_LEN, C_HEADS, HEAD_DIM), 1.0),
        'cache_c_v': nrm(ks[6], (DEC_BATCH, DEPTH, PAST_LEN, C_HEADS, HEAD_DIM), 1.0),
        'c': nrm(ks[7], (DEC_BATCH, D), 1.0),
        'c_ctx': nrm(ks[8], (D,), 1.0),
        'w_mod': nrm(ks[9], (DEPTH, D, N_MOD * D), 0.5 * D ** -0.5),
        'b_mod': nrm(ks[10], (DEPTH, N_MOD * D), 0.01),
        'g_norm': 1.0 + nrm(ks[11], (DEPTH, 4, D), 0.05),
        'w_in': nrm(ks[12], (DEPTH, D, IN_WIDTH), D ** -0.5),
        'g_qk_a': 1.0 + nrm(ks[13], (DEPTH, 2, HEAD_DIM), 0.05),
        'g_out_a': 1.0 + nrm(ks[14], (DEPTH, A_WIDTH), 0.05),
        'conv_w': nrm(ks[15], (DEPTH, CONV_K, 3 * B_WIDTH), CONV_K ** -0.5),
        'a_log': a_log,
        'dt_bias': dt_bias,
        'g_onorm_b': 1.0 + nrm(ks[16], (DEPTH, HEAD_DIM), 0.05),
        'rpb': nrm(ks[17], (DEPTH, C_HEADS, 2 * WIN_R - 1, 2 * WIN_C - 1), 0.1),
        'g_out_c': 1.0 + nrm(ks[18], (DEPTH, C_WIDTH), 0.05),
        'w_out': nrm(ks[19], (DEPTH, MIX_WIDTH, D), MIX_WIDTH ** -0.5),
        'w_gu': nrm(ks[22], (DEPTH, D, 2 * D_FF), D ** -0.5),
        'w_down': nrm(ks[23], (DEPTH, D_FF, D), D_FF ** -0.5),
    }


def reference(x_prompt, x_sample, cache_a_k, cache_a_v, state_b, cache_c_k, cache_c_v, c, c_ctx,
              w_mod, b_mod, g_norm, w_in, g_qk_a, g_out_a, conv_w, a_log, dt_bias, g_onorm_b, rpb,
              g_out_c, w_out, w_gu, w_down):
    xp = x_prompt
    xs = x_sample
    ka_l, va_l, sb_l, kc_l, vc_l = [], [], [], [], []
    for l in range(DEPTH):
        mod_p = modulation(c_ctx, w_mod[l], b_mod[l])
        xp, ctx_t = sandwich_block(
            xp, mod_p, g_norm[l],
            lambda h: context_mixer(h, w_in[l], g_qk_a[l], g_out_a[l], conv_w[l], a_log[l], dt_bias[l],
                                    g_onorm_b[l], g_out_c[l], w_out[l]),
            w_gu[l], w_down[l])
        ka_l.append(ctx_t[0])
        va_l.append(ctx_t[1])
        sb_l.append(ctx_t[2])
        kc_l.append(ctx_t[3])
        vc_l.append(ctx_t[4])
        mod_s = modulation(c, w_mod[l], b_mod[l])
        xs, _ = sandwich_block(
            xs, mod_s, g_norm[l],
            lambda h: (latent_mixer(h, cache_a_k[:, l], cache_a_v[:, l], state_b[:, l], cache_c_k[:, l],
                                    cache_c_v[:, l], w_in[l], g_qk_a[l], g_out_a[l], conv_w[l], a_log[l],
                                    dt_bias[l], g_onorm_b[l], rpb[l], g_out_c[l], w_out[l]), None),
            w_gu[l], w_down[l])
    new_cache_a_k = jnp.stack(ka_l, axis=1)
    new_cache_a_v = jnp.stack(va_l, axis=1)
    new_state_b = jnp.stack(sb_l, axis=1)
    new_cache_c_k = jnp.stack(kc_l, axis=1)
    new_cache_c_v = jnp.stack(vc_l, axis=1)
    return (xp, xs, new_cache_a_k, new_cache_a_v, new_state_b, new_cache_c_k, new_cache_c_v)
```

```python
import contextlib
import math
import os
import numpy as np
import concourse.bass as bass
import concourse.mybir as mybir
from concourse.bass_utils import run_bass_kernel_spmd

F32, BF16 = mybir.dt.float32, mybir.dt.bfloat16
AF = mybir.ActivationFunctionType
ALU = mybir.AluOpType
AX = mybir.AxisListType

D = 1024
DEPTH = 4
NPS = 4
TP = 256
TS = 2048
DFF = 2816
INW = 2832
EPS = 1e-6
C_AQ, C_AK, C_AV, C_BQKV, C_BG, C_BBETA, C_BALPHA, C_CQ, C_CK, C_CV = (
    0, 384, 512, 640, 1408, 1664, 1672, 1680, 2064, 2448)
NEG = -30000.0
POOLENG = os.environ.get('POOLENG', 'pool')

ENGS = ("pe", "act", "dve", "pool", "sp")


class Op:
    __slots__ = ("eng", "fn", "deps", "signal", "sem", "val", "dma", "epoch", "phase", "iname")

    def __init__(self, eng, fn, dma, epoch):
        self.eng = eng
        self.fn = fn
        self.deps = []
        self.signal = False
        self.sem = None
        self.val = 0
        self.dma = dma
        self.epoch = epoch
        self.phase = None
        self.iname = None


class Sched:
    def __init__(self, n_dma_sems=24, same_eng_sync=True):
        self.ops = {e: [] for e in ENGS}
        self.lastw = {}
        self.readers = {}
        self.n_dma_sems = n_dma_sems
        self.dma_rr = {e: 0 for e in ENGS}
        self.dma_last = {e: [None] * n_dma_sems for e in ENGS}
        self.epoch = 0
        self.final_ops = []
        self.same = same_eng_sync
        self.bar = {e: [] for e in ENGS}
        self.seq = []
        self.phase = "init"

    def _dep(self, op, other):
        if other is None or other is op:
            return
        if not other.dma and not op.dma and other.eng == op.eng:
            if op.eng == "pe" or not self.same:
                return
        op.deps.append(other)

    def barrier(self):
        lasts = [self.ops[e][-1] for e in ENGS if self.ops[e]]
        for e in ENGS:
            lasts += [d for d in self.dma_last[e] if d is not None]
        for e in ENGS:
            self.bar[e] = list(lasts)

    def add(self, eng, fn, reads=(), writes=(), dma=False):
        op = Op(eng, fn, dma, self.epoch)
        op.phase = self.phase
        if self.bar[eng]:
            for o in self.bar[eng]:
                if o.dma or dma or o.eng != eng:
                    op.deps.append(o)
            self.bar[eng] = []
        for r in reads:
            self._dep(op, self.lastw.get(r))
            if r[:2] in ("PF", "PT"):
                for rd in self.readers.get(r, ()):
                    if rd.eng != eng:
                        op.deps.append(rd)
        for w in writes:
            self._dep(op, self.lastw.get(w))
            for rd in self.readers.get(w, ()):
                self._dep(op, rd)
        for r in reads:
            self.readers.setdefault(r, []).append(op)
        for w in writes:
            self.lastw[w] = op
            self.readers[w] = []
        if dma:
            k = self.dma_rr[eng]
            self.dma_rr[eng] = (k + 1) % self.n_dma_sems
            prev = self.dma_last[eng][k]
            if prev is not None:
                op.deps.append(prev)
            self.dma_last[eng][k] = op
            op.sem = ("dma" + eng, k)
            op.signal = True
        self.ops[eng].append(op)
        self.seq.append(op)
        return op

    def emit(self, nc, stack):
        for e in ENGS:
            for op in self.ops[e]:
                for d in op.deps:
                    d.signal = True
        for op in self.final_ops:
            op.signal = True
        sems = {}
        counts = {}
        for op in self.seq:
            if not op.signal:
                continue
            if op.dma:
                key = op.sem
                counts[key] = counts.get(key, 0) + 16
            else:
                key = (op.eng, op.epoch)
                counts[key] = counts.get(key, 0) + 1
            op.sem = key
            op.val = counts[key]
            if key not in sems:
                sems[key] = stack.enter_context(nc.semaphore("s_%s_%s" % key))
        self.maxval = max(counts.values()) if counts else 0
        self.nsems = len(sems)
        block = stack.enter_context(nc.Block())
        engobj = {"pe": block.tensor, "act": block.scalar, "dve": block.vector,
                  "pool": block.gpsimd, "sp": block.sync}
        finals = list(self.final_ops)

        def make(e):
            ops = self.ops[e]

            def body(eng):
                waited = {}
                for op in ops:
                    need = {}
                    for d in op.deps:
                        if d.val > need.get(d.sem, 0):
                            need[d.sem] = d.val
                    for key, v in need.items():
                        if waited.get(key, 0) >= v:
                            continue
                        eng.wait_ge(sems[key], v)
                        waited[key] = v
                    ins = op.fn(eng)
                    try:
                        op.iname = ins.ins.name
                    except Exception:
                        pass
                    if op.signal:
                        ins.then_inc(sems[op.sem], 16 if op.dma else 1)
                if e == "sp":
                    for f in finals:
                        if waited.get(f.sem, 0) < f.val:
                            eng.wait_ge(sems[f.sem], f.val)
                            waited[f.sem] = f.val
            return body

        for e in ENGS:
            engobj[e](make(e))


class Arena:
    def __init__(self, ap, ncols):
        self.ap = ap
        self.n = ncols
        self.off = 0
        self.peak = 0

    def alloc(self, shape, dt=F32):
        n = int(np.prod(shape))
        cols = n if dt == F32 else (n + 1) // 2
        a = self.off
        self.off += cols
        self.peak = max(self.peak, self.off)
        assert self.off <= self.n, "arena overflow %d > %d" % (self.off, self.n)
        v = self.ap[:, a:a + cols]
        if dt != F32:
            v = v.bitcast(dt)
            if 2 * cols != n:
                v = v[:, 0:n]
        if len(shape) > 1:
            names = " ".join("d%d" % i for i in range(len(shape)))
            kw = {"d%d" % i: shape[i] for i in range(len(shape) - 1)}
            v = v.rearrange("p (%s) -> p %s" % (names, names), **kw)
        return v

    def mark(self):
        return self.off

    def release(self, m):
        self.off = m


class Job:
    pass


class KB:
    def __init__(self, nl=DEPTH, jobs=("p", "s"), dbg=(), same=True, stop=None):
        self.stop = stop
        self.nl = nl
        self.jobs = jobs
        self.dbg = set(dbg)
        self.nc = bass.Bass("TRN2", target_bir_lowering=False)
        self.S = Sched(same_eng_sync=same)
        self.st = contextlib.ExitStack()
        self.outs = []
        self.uid = 0

    def din(self, name, shape, dt=F32):
        return self.nc.dram_tensor(name, list(shape), dt, kind="ExternalInput").ap()

    def dout(self, name, shape, dt=F32):
        self.outs.append(name)
        return self.nc.dram_tensor(name, list(shape), dt, kind="ExternalOutput").ap()

    def add(self, eng, fn, r=(), w=(), dma=False):
        return self.S.add(eng, fn, reads=r, writes=w, dma=dma)

    def dma(self, q, out, in_, r=(), w=(), slow=False):
        if slow:
            return self.add(q, lambda e: e.dma_start(out=out, in_=in_, allow_slow_non_contiguous=True), r, w, True)
        return self.add(q, lambda e: e.dma_start(out=out, in_=in_), r, w, True)

    def store(self, out, in_, r=()):
        op = self.dma("sp", out, in_, r=r)
        self.S.final_ops.append(op)
        return op

    def mm(self, out, lhsT, rhs, start, stop, r=(), w=()):
        return self.add("pe", lambda e: e.matmul(out, lhsT=lhsT, rhs=rhs, start=start, stop=stop), r, w)

    def tr(self, out, in_, r=(), w=()):
        ident = self.ident_b if in_.dtype == BF16 else self.ident_f
        n = in_.shape[0]
        idn = ident[0:n, 0:n]
        return self.add("pe", lambda e: e.transpose(out=out, in_=in_, identity=idn), r, w)

    def act(self, out, in_, func, r=(), w=(), scale=None, bias=None, accum=None):
        kw = {}
        if scale is not None:
            kw["scale"] = scale
        if bias is not None:
            kw["bias"] = bias
        if accum is not None:
            kw["accum_out"] = accum
        return self.add("act", lambda e: e.activation(out=out, in_=in_, func=func, **kw), r, w)

    def tt(self, eng, out, in0, in1, op, r=(), w=()):
        return self.add(eng, lambda e: e.tensor_tensor(out=out, in0=in0, in1=in1, op=op), r, w)

    def ts(self, eng, out, in0, s1, s2, op0, op1=None, r=(), w=()):
        if op1 is None:
            return self.add(eng, lambda e: e.tensor_scalar(out=out, in0=in0, scalar1=s1, scalar2=None, op0=op0), r, w)
        return self.add(eng, lambda e: e.tensor_scalar(out=out, in0=in0, scalar1=s1, scalar2=s2, op0=op0, op1=op1), r, w)

    def stt(self, eng, out, in0, scalar, in1, op0, op1, r=(), w=()):
        return self.add(eng, lambda e: e.scalar_tensor_tensor(out=out, in0=in0, scalar=scalar, in1=in1, op0=op0, op1=op1), r, w)

    def cp(self, eng, out, in_, r=(), w=()):
        if eng == "act":
            return self.add(eng, lambda e: e.activation(out=out, in_=in_, func=AF.Copy), r, w)
        return self.add(eng, lambda e: e.tensor_copy(out=out, in_=in_), r, w)

    def red(self, eng, out, in_, r=(), w=()):
        return self.add(eng, lambda e: e.tensor_reduce(out=out, in_=in_, axis=AX.X, op=ALU.add), r, w)

    def memset(self, eng, out, val, w=()):
        return self.add(eng, lambda e: e.memset(out, val), (), w)

    def rstd(self, out, ssum, inv_n, r=(), w=()):
        self.act(out, ssum, AF.Ln, r=r, w=w, scale=inv_n, bias=EPS)
        self.act(out, out, AF.Exp, r=w, w=w, scale=-0.5)

    def tap(self, name, ap, r):
        if name not in self.dbg:
            return
        shape = list(ap.shape)
        d = self.dout("dbg_" + name, shape, ap.dtype)
        self.store(d, ap, r=r)

    def build(self):
        nc = self.nc
        nl = self.nl
        st = self.st
        self.xp = self.din("xp", [NPS * TP, D])
        self.xs = self.din("xs", [TS, D])
        self.cak = self.din("cak", [DEPTH, 256, 128])
        self.cav = self.din("cav", [DEPTH, 256, 128])
        self.sb0 = self.din("sb0", [DEPTH, 2, 4, 64, 64])
        self.cck = self.din("cck", [DEPTH, 256, 384])
        self.ccv = self.din("ccv", [DEPTH, 256, 384])
        self.cvec = self.din("cvec", [2, D])
        self.w_mod = self.din("w_mod", [DEPTH, D, 6 * D])
        self.b_mod = self.din("b_mod", [DEPTH, 6 * D])
        self.g_norm = self.din("g_norm", [DEPTH, 4, D])
        self.w_in = self.din("w_in", [DEPTH, D, INW])
        self.g_qk_a = self.din("g_qk_a", [DEPTH, 2, 64])
        self.g_out_a = self.din("g_out_a", [DEPTH, 384])
        self.conv_w = self.din("conv_w", [DEPTH, 3, 768])
        self.a_log = self.din("a_log", [DEPTH, 8])
        self.dt_bias = self.din("dt_bias", [DEPTH, 8])
        self.g_onorm_b = self.din("g_onorm_b", [DEPTH, 64])
        self.rpb = self.din("rpb", [DEPTH, 6, 15, 31])
        self.g_out_c = self.din("g_out_c", [DEPTH, 384])
        self.w_out = self.din("w_out", [DEPTH, D, D])
        self.w_gu = self.din("w_gu", [DEPTH, D, 2 * DFF])
        self.w_down = self.din("w_down", [DEPTH, DFF, D])
        self.cpack = self.din("cpack", [128, 7 * 128])
        self.ropet = self.din("ropet", [128, 16 * 64])
        self.qmask = self.din("qmask", [128, 14 * 128])
        self.rsel = self.din("rsel", [33, 102])
        self.ohc = self.din("ohc", [33, 4096])
        self.neghd = self.din("neghd", [128, 128])
        self.tabscr = self.nc.dram_tensor("tabscr", [102, 4096], BF16, kind="Internal").ap()
        self.yp = self.dout("yp", [NPS * TP, D])
        self.ys = self.dout("ys", [TS, D])
        self.oka = self.dout("oka", [NPS, DEPTH, TP, 128])
        self.ova = self.dout("ova", [NPS, DEPTH, TP, 128])
        self.osb = self.dout("osb", [NPS, DEPTH, 2, 4, 64, 64])
        self.okc = self.dout("okc", [NPS, DEPTH, TP, 384])
        self.ovc = self.dout("ovc", [NPS, DEPTH, TP, 384])

        NCOL = 53200
        arena_t = st.enter_context(nc.sbuf_tensor("arena", [128, NCOL], F32))
        self.A = Arena(arena_t[:], NCOL)
        A = self.A
        self.PF = [st.enter_context(nc.psum_tensor("pf%d" % i, [128, 512], F32))[:] for i in range(6)]
        self.PT = [st.enter_context(nc.psum_tensor("pt%d" % i, [128, 1024], BF16))[:] for i in range(2)]

        self.cF = A.alloc([7, 128])
        self.ident_f = self.cF[:, 0, :]
        self.tri = [self.cF[:, 1, :], self.cF[:, 2, :]]
        self.maft = [self.cF[:, 3, :], self.cF[:, 4, :]]
        self.ones_f = self.cF[:, 5, :]
        self.cB = A.alloc([3, 128], BF16)
        self.ident_b = self.cB[:, 0, :]
        self.ones_b = self.cB[:, 1, :]
        self.bd_b = self.cB[:, 2, :]
        self.rope = A.alloc([16, 64])
        self.negh = A.alloc([2, 64], BF16)
        self.dma("pool", self.negh, self.neghd.rearrange("p (a b) -> p a b", a=2), w=["negh"])
        self.dma("sp", self.cF, self.cpack.rearrange("p (a b) -> p a b", a=7), w=["cF"])
        self.dma("sp", self.rope, self.ropet.rearrange("p (a b) -> p a b", a=16), w=["rope"])
        self.cp("dve", self.cB[:, 0, :], self.cF[:, 0, :], r=["cF"], w=["cB"])
        self.cp("dve", self.cB[:, 1, :], self.cF[:, 5, :], r=["cF"], w=["cB"])
        self.cp("dve", self.cB[:, 2, :], self.cF[:, 6, :], r=["cF"], w=["cB"])
        self.CK = ["cF", "cB"]

        self.X = A.alloc([16, D])
        self.modraw = A.alloc([4, 8])
        self.modA = A.alloc([2, 8])
        self.gfm = A.alloc([2, 8])
        self.G1 = A.alloc([D])
        self.G2 = A.alloc([D])
        self.cfm = A.alloc([8])
        self.srep = A.alloc([8, 128], BF16)
        self.small = A.alloc([64])
        self.base_mark = A.mark()

        for jn in self.jobs:
            J = Job()
            J.name = jn
            if jn == "p":
                J.nt, J.T, J.nseq, J.ci, J.latent = 8, TP, NPS, 0, False
                J.xin, J.yout = self.xp, self.yp
            else:
                J.nt, J.T, J.nseq, J.ci, J.latent = 16, TS, 1, 1, True
                J.xin, J.yout = self.xs, self.ys
            self.run_job(J)

        self.S.emit(nc, st)
        return nc

    def run_job(self, J):
        S = self.S
        A = self.A
        S.barrier()
        xin = J.xin.rearrange("(t p) d -> p t d", p=128)
        for t in range(J.nt):
            self.dma("sp", self.X[:, t, :], xin[:, t, :], w=["X%d" % t])
        self.dma("sp", self.cfm, self.cvec[J.ci:J.ci + 1, :].rearrange("o (k p) -> p (o k)", p=128),
                 w=["cfm"], slow=True)
        sil = self.small[:, 0:8]
        self.act(sil, self.cfm, AF.Silu, r=["cfm"], w=["small"])
        self.cp("dve", self.srep, sil.unsqueeze(2).to_broadcast([128, 8, 128]), r=["small"], w=["srep"])
        for l in range(self.nl):
            S.epoch = (J.name, l)
            if self.stop == "load":
                break
            self.mod(l, J)
            if self.stop == "mod":
                break
            self.phase_b(l, J)
            if self.stop == "b":
                break
            self.phase_kvq(l, J)
            if self.stop == "kvq":
                break
            self.phase_f(l, J)
            self.tap("x2_%s%d" % (J.name, l), self.X[:, 0:J.nt, :], r=["X%d" % t for t in range(J.nt)])
        yout = J.yout.rearrange("(t p) d -> p t d", p=128)
        for t in range(J.nt):
            self.store(yout[:, t, :], self.X[:, t, :], r=["X%d" % t])

    def mod(self, l, J):
        self.S.phase = "mod"
        PF = self.PF
        self.S.barrier()
        self.A.release(self.base_mark)
        self.rowbuf = self.A.alloc([512])
        self.bbc = [self.A.alloc([512]) for _ in range(2)]
        self.wm = [self.A.alloc([8, 512], BF16) for _ in range(4)]
        wmv = self.w_mod[l].rearrange("(k p) n -> p k n", p=128)
        gn = self.g_norm[l]
        self.dma("sp", self.G1, gn[1:2, :].partition_broadcast(128), w=["G1"])
        self.dma("sp", self.G2, gn[3:4, :].partition_broadcast(128), w=["G2"])
        self.dma("sp", self.gfm[:, 0, :], gn[0:1, :].rearrange("o (k p) -> p (o k)", p=128), w=["gfm"], slow=True)
        self.dma("sp", self.gfm[:, 1, :], gn[2:3, :].rearrange("o (k p) -> p (o k)", p=128), w=["gfm"], slow=True)
        rawidx = {0: 0, 1: 1, 3: 2, 4: 3}
        for piece in range(12):
            s = piece % 2
            v, half = piece // 2, piece % 2
            ws = piece % 4
            self.dma("pool", self.wm[ws], wmv[:, :, piece * 512:(piece + 1) * 512], w=["wm%d" % ws])
            self.dma("sp", self.bbc[s], self.b_mod[l:l + 1, piece * 512:(piece + 1) * 512].partition_broadcast(128),
                     w=["bbc%d" % s])
            ps = PF[s]
            for k in range(8):
                self.mm(ps, self.srep[:, k, :], self.wm[ws][:, k, :], k == 0, k == 7,
                        r=["srep", "wm%d" % ws], w=["PF%d" % s])
            if v in (2, 5):
                G = self.G1 if v == 2 else self.G2
                gk = "G1" if v == 2 else "G2"
                gs = G[:, half * 512:(half + 1) * 512]
                self.tt("dve", self.bbc[s], ps, self.bbc[s], ALU.add, r=["PF%d" % s, "bbc%d" % s], w=["bbc%d" % s])
                self.tt("pool", gs, gs, self.bbc[s], ALU.mult, r=["bbc%d" % s, gk], w=[gk])
            elif os.environ.get("MODTEST") == "1":
                self.cp("dve", self.rowbuf[0:1, :], ps[0:1, :], r=["PF%d" % s], w=["rowbuf"])
            else:
                self.tt("dve", self.rowbuf[0:1, :], ps[0:1, :], self.bbc[s][0:1, :], ALU.add,
                        r=["PF%d" % s, "bbc%d" % s], w=["rowbuf"])
                pm = PF[2][:, 0:4]
                for j in range(4):
                    self.mm(pm[:, j:j + 1], self.rowbuf[0:1, j * 128:(j + 1) * 128], self.ones_f[0:1, 0:1],
                            True, True, r=["rowbuf", "cF"], w=["PF2"])
                self.cp("dve", self.modraw[:, rawidx[v], half * 4:(half + 1) * 4], pm, r=["PF2"], w=["modraw"])
        self.stt("dve", self.modA[:, 0, :], self.modraw[:, 1, :], 1.0, self.gfm[:, 0, :], ALU.add, ALU.mult,
                 r=["modraw", "gfm"], w=["modA"])
        self.stt("dve", self.modA[:, 1, :], self.modraw[:, 3, :], 1.0, self.gfm[:, 1, :], ALU.add, ALU.mult,
                 r=["modraw", "gfm"], w=["modA"])
        self.tap("G1_%s%d" % (J.name, l), self.G1, r=["G1"])
        self.tap("modraw_%s%d" % (J.name, l), self.modraw, r=["modraw"])

    def make_hT(self, J, tiles, which, dst, dkey, xn, tmpf):
        PT0 = self.PT[0]
        Avec = self.modA[:, which, :]
        shv = self.modraw[:, 0 if which == 0 else 2, :]
        for j, t in enumerate(tiles):
            xk = "X%d" % t
            ss = self.small[:, 8:9]
            rs = self.small[:, 9:10]
            self.act(xn, self.X[:, t, :], AF.Square, r=[xk], w=["xn", "small"], accum=ss)
            self.rstd(rs, ss, 1.0 / D, r=["small"], w=["small"])
            self.act(xn, self.X[:, t, :], AF.Copy, r=[xk, "small"], w=["xn"], scale=rs)
            for k in range(8):
                self.tr(PT0[:, k * 128:(k + 1) * 128], xn[:, k * 128:(k + 1) * 128], r=["xn", "cB"], w=["PT0"])
            pv = PT0.rearrange("p (k c) -> p k c", k=8)
            tv = tmpf.rearrange("p (k c) -> p k c", k=8)
            self.tt("dve", tv, pv, Avec.unsqueeze(2).to_broadcast([128, 8, 128]), ALU.mult,
                    r=["PT0", "modA"], w=["tmpf"])
            self.tt("pool", dst[:, :, j * 128:(j + 1) * 128], tv, shv.unsqueeze(2).to_broadcast([128, 8, 128]),
                    ALU.add, r=["tmpf", "modraw"], w=[dkey])

    def hT_pre(self, t, xn, xnk, sc0):
        xk = "X%d" % t
        ss = self.small[:, sc0:sc0 + 1]
        rs = self.small[:, sc0 + 1:sc0 + 2]
        self.act(xn, self.X[:, t, :], AF.Square, r=[xk], w=[xnk, "small"], accum=ss)
        self.rstd(rs, ss, 1.0 / D, r=["small"], w=["small"])
        self.act(xn, self.X[:, t, :], AF.Copy, r=[xk, "small"], w=[xnk], scale=rs)

    def hT_post(self, which, dst, dkey, j, xn, xnk, tmpf):
        PT0 = self.PT[0]
        Avec = self.modA[:, which, :]
        shv = self.modraw[:, 0 if which == 0 else 2, :]
        for k in range(8):
            self.tr(PT0[:, k * 128:(k + 1) * 128], xn[:, k * 128:(k + 1) * 128], r=[xnk, "cB"], w=["PT0"])
        for k in range(8):
            self.act(dst[:, k, j * 128:(j + 1) * 128], PT0[:, k * 128:(k + 1) * 128], AF.Identity,
                     r=["PT0", "modA", "modraw"], w=[dkey], scale=Avec[:, k:k + 1], bias=shv[:, k:k + 1])

    def load_w(self, dst, src2d, c0, c1, key, kchunks=8):
        v = src2d.rearrange("(k p) n -> p k n", p=128)
        return self.dma("pool", dst, v[:, :, c0:c1], w=[key])

    def phase_b(self, l, J):
        S = self.S
        A = self.A
        PF, PT = self.PF, self.PT
        S.barrier()
        A.release(self.base_mark)
        nt, T = J.nt, J.T
        ntq = T // 128
        self.YB = A.alloc([nt, 256], BF16)
        self.yb_mark = A.mark()
        WB = A.alloc([8, 1040], BF16)
        self.qm = A.alloc([14, 128], BF16)
        self.dma("pool", self.qm, self.qmask.rearrange("p (a b) -> p a b", a=14), w=["qm"])
        BT = A.alloc([6, T], BF16)
        GT = A.alloc([ntq, 256], BF16)
        OB = A.alloc([ntq, 256])
        bet = A.alloc([ntq, 8])
        nbet = A.alloc([ntq, 8])
        gl = A.alloc([ntq, 8])
        if J.nseq == 1:
            ctmp = WB.rearrange("p k c -> p (k c)").bitcast(F32)[:, 0:T]
        else:
            ctmp = A.alloc([T])
        hT0 = A.alloc([8, 128], BF16)
        xn = A.alloc([D], BF16)
        tmpf = A.alloc([D])
        cw = A.alloc([3, 6])
        dtb = A.alloc([8])
        nal = A.alloc([8])
        gon = A.alloc([64])
        if J.nseq == 1:
            wbf = WB.rearrange("p k c -> p (k c)").bitcast(F32)
            sq = wbf[:, 2048:2304].bitcast(BF16)
            rsb = wbf[:, 2304:2816]
        else:
            sq = A.alloc([512], BF16)
            rsb = A.alloc([512])
        def two(shape, dt=F32):
            return [A.alloc(shape, dt) for _ in range(2)]

        def one(shape, dt=F32):
            a = A.alloc(shape, dt)
            return [a, a]
        KVt = two([512], BF16)
        Gm = one([4, 128])
        dec = two([4, 128])
        decT = two([4, 128])
        if J.nseq == 1:
            hT1b = dec[0].rearrange("p h j -> p (h j)").bitcast(BF16).rearrange("p (k c) -> p k c", k=8)
            xn2 = dec[1].rearrange("p h j -> p (h j)").bitcast(BF16)
        else:
            hT1b = A.alloc([8, 128], BF16)
            xn2 = A.alloc([D], BF16)
        hT = [hT0, hT1b]
        xnb = [xn, xn2]
        Nb = two([4, 128], BF16)
        NTb = two([4, 128], BF16)
        Xb = two([4, 128], BF16)
        Yb = two([4, 128], BF16)
        Zb = two([4, 128], BF16)
        Zc2 = two([4, 128], BF16)
        Pm = one([4, 128], BF16)
        Qm = one([4, 128], BF16)
        ATb = two([4, 128], BF16)
        vb = two([4, 64], BF16)
        kbg = two([4, 64], BF16)
        ktil = two([4, 64], BF16)
        U = two([4, 64])
        WT = two([2, 128], BF16)
        vnew = two([4, 64], BF16)
        tmpo = two([4, 64])
        gsm = two([32])
        Sf = A.alloc([2, 64])
        Sb = A.alloc([2, 64], BF16)

        self.load_w(WB, self.w_in[l], C_BQKV, C_BQKV + 1040, "WB")
        for jc in range(3):
            self.dma("sp", cw[:, jc, :], self.conv_w[l, jc:jc + 1, :].rearrange("o (b p) -> p (o b)", p=128),
                     w=["cw"], slow=True)
        self.dma("sp", dtb, self.dt_bias[l:l + 1, :].partition_broadcast(128), w=["dtb"])
        self.dma("sp", nal, self.a_log[l:l + 1, :].partition_broadcast(128), w=["nal"])
        self.dma("sp", gon, self.g_onorm_b[l:l + 1, :].partition_broadcast(128), w=["gon"])
        self.ts("dve", gon, gon, 0.125, None, ALU.mult, r=["gon"], w=["gon"])
        self.act(nal, nal, AF.Exp, r=["nal"], w=["nal"])
        self.ts("dve", nal, nal, -1.0, None, ALU.mult, r=["nal"], w=["nal"])

        for s in range(J.nseq):
            t0 = s * ntq
            self.S.phase = "b1_proj"
            self.hT_pre(t0, xnb[0], "xnb0", 48)
            self.hT_post(0, hT[0], "hTb0", 0, xnb[0], "xnb0", tmpf)
            for j in range(ntq):
                t = t0 + j
                h = hT[j % 2]
                hk = "hTb%d" % (j % 2)
                q2 = (j + 1) % 2
                if j + 1 < ntq:
                    self.hT_pre(t + 1, xnb[q2], "xnb%d" % q2, 48 + 2 * q2)
                for b in range(6):
                    if b == 3 and j + 1 < ntq:
                        self.hT_post(0, hT[q2], "hTb%d" % q2, 0, xnb[q2], "xnb%d" % q2, tmpf)
                    ps = PF[b % 2][:, 0:128]
                    pk = "PF%d" % (b % 2)
                    for k in range(8):
                        self.mm(ps, WB[:, k, b * 128:(b + 1) * 128], h[:, k, :], k == 0, k == 7,
                                r=["WB", hk], w=[pk])
                    self.cp("act", BT[:, b, j * 128:(j + 1) * 128], ps, r=[pk], w=["BT"])
                pg = PF[2][:, 0:272]
                for k in range(8):
                    self.mm(pg, h[:, k, :], WB[:, k, 768:1040], k == 0, k == 7, r=["WB", hk], w=["PF2"])
                e1 = tmpf[:, 0:256]
                self.act(e1, pg[:, 0:256], AF.Exp, r=["PF2"], w=["tmpf"], scale=-1.0)
                self.ts("dve", e1, e1, 1.0, None, ALU.add, r=["tmpf"], w=["tmpf"])
                self.add("dve", lambda e, a=e1: e.reciprocal(out=a, in_=a), ["tmpf"], ["tmpf"])
                self.tt("dve", GT[:, j, :], e1, pg[:, 0:256], ALU.mult, r=["tmpf", "PF2"], w=["GT"])
                e2 = tmpf[:, 256:264]
                self.act(e2, pg[:, 256:264], AF.Exp, r=["PF2"], w=["tmpf"], scale=-1.0)
                self.ts("dve", e2, e2, 1.0, None, ALU.add, r=["tmpf"], w=["tmpf"])
                self.add("dve", lambda e, a=e2, o=bet[:, j, :]: e.reciprocal(out=o, in_=a), ["tmpf"], ["bet"])
                self.ts("dve", nbet[:, j, :], bet[:, j, :], -1.0, None, ALU.mult, r=["bet"], w=["nbet"])
                e3 = tmpf[:, 264:272]
                self.tt("dve", e3, pg[:, 264:272], dtb, ALU.add, r=["PF2", "dtb"], w=["tmpf"])
                self.act(e3, e3, AF.Exp, r=["tmpf"], w=["tmpf"])
                self.act(e3, e3, AF.Ln, r=["tmpf"], w=["tmpf"], bias=1.0)
                self.tt("dve", gl[:, j, :], e3, nal, ALU.mult, r=["tmpf", "nal"], w=["gl"])
            if os.environ.get("BSTOP") == "1":
                continue
            if J.nseq == 1:
                S.barrier()
            self.S.phase = "b2_conv"
            for b in range(6):
                src = BT[:, b, :]
                self.ts("dve", ctmp, src, cw[:, 1, b:b + 1], None, ALU.mult, r=["BT", "cw"], w=["ctmp"])
                self.stt("dve", ctmp[:, 1:T], src[:, 0:T - 1], cw[:, 0, b:b + 1], ctmp[:, 1:T], ALU.mult, ALU.add,
                         r=["BT", "cw", "ctmp"], w=["ctmp"])
                self.stt("dve", ctmp[:, 0:T - 1], src[:, 1:T], cw[:, 2, b:b + 1], ctmp[:, 0:T - 1], ALU.mult, ALU.add,
                         r=["BT", "cw", "ctmp"], w=["ctmp"])
                CW = min(512, T)
                for c0 in range(0, T, CW):
                    cs = slice(c0, c0 + CW)
                    ex = rsb[:, 0:CW]
                    self.act(ex, ctmp[:, cs], AF.Exp, r=["ctmp"], w=["rsb"], scale=-1.0)
                    self.ts("dve", ex, ex, 1.0, None, ALU.add, r=["rsb"], w=["rsb"])
                    self.add("dve", lambda e, a=ex: e.reciprocal(out=a, in_=a), ["rsb"], ["rsb"])
                    if b >= 4:
                        self.tt("pool", BT[:, b, cs], ctmp[:, cs], ex, ALU.mult, r=["ctmp", "rsb"], w=["BT"])
                    else:
                        self.tt("pool", ctmp[:, cs], ctmp[:, cs], ex, ALU.mult, r=["ctmp", "rsb"], w=["ctmp"])
                        self.tt("pool", sq[:, 0:CW], ctmp[:, cs], ctmp[:, cs], ALU.mult, r=["ctmp"], w=["sq"])
                        pn = PF[3][:, 0:CW]
                        self.mm(pn, self.bd_b, sq[:, 0:CW], True, True, r=["sq", "cB"], w=["PF3"])
                        self.act(ex, pn, AF.Ln, r=["PF3"], w=["rsb"], bias=EPS)
                        self.act(ex, ex, AF.Exp, r=["rsb"], w=["rsb"], scale=-0.5)
                        self.tt("dve", BT[:, b, cs], ctmp[:, cs], ex, ALU.mult, r=["ctmp", "rsb"], w=["BT"])
            if "bq" in self.dbg and s == 0:
                self.tap("bqkv", BT[:, :, 0:256], r=["BT"])
                self.tap("bgl", gl[:, 0:2, :], r=["gl"])
                self.tap("bbeta", bet[:, 0:2, :], r=["bet"])
            if os.environ.get("BSTOP") == "2":
                continue
            self.S.phase = "b3_chunks"
            for r in range(2):
                if J.latent:
                    self.dma("sp", Sf, self.sb0[l, r].rearrange("(j i) k v -> (i k) j v", i=2), w=["Sf"])
                else:
                    self.memset("pool", Sf, 0.0, w=["Sf"])
                self.cp("act", Sb, Sf, r=["Sf"], w=["Sb"])
                order = range(ntq) if r == 0 else range(ntq - 1, -1, -1)
                def chunk_gen(ci, c, p):
                    ba, bak = (PF[1], "PF1") if p == 0 else (PF[3], "PF3")
                    bb, bbk = (PF[2], "PF2") if p == 0 else (PF[4], "PF4")
                    sfx = "_%d" % p
                    cols = slice(c * 128, (c + 1) * 128)
                    for i, b in enumerate((2, 3, 4, 5)):
                        self.tr(PT[0][:, i * 128:(i + 1) * 128], BT[:, b, cols], r=["BT", "cB"], w=["PT0"])
                    self.cp("act", KVt[p], PT[0][:, 0:512], r=["PT0"], w=["KVt" + sfx])
                    ktok = KVt[p][:, 0:256].rearrange("p (h d) -> p h d", h=4)
                    vtok = KVt[p][:, 256:512].rearrange("p (h d) -> p h d", h=4)
                    g4 = gl[:, c, r * 4:(r + 1) * 4]
                    b4 = bet[:, c, r * 4:(r + 1) * 4]
                    nb4 = nbet[:, c, r * 4:(r + 1) * 4]
                    pg = PF[0][:, 0:8]
                    self.mm(pg[:, 0:4], self.tri[r], g4, True, True, r=["gl", "cF"], w=["PF0"])
                    self.mm(pg[:, 4:8], self.ones_f, g4, True, True, r=["gl", "cF"], w=["PF0"])
                    gs = gsm[p]
                    gk = "gsm" + sfx
                    gc, egc, eglast, dgl, ekt, bg = (gs[:, 0:4], gs[:, 4:8], gs[:, 8:12], gs[:, 12:16],
                                                     gs[:, 16:20], gs[:, 20:24])
                    self.cp("dve", gc, pg[:, 0:4], r=["PF0"], w=[gk])
                    self.act(egc, pg[:, 0:4], AF.Exp, r=["PF0"], w=[gk])
                    self.act(eglast, pg[:, 4:8], AF.Exp, r=["PF0"], w=[gk])
                    self.tt("dve", dgl, pg[:, 4:8], gc, ALU.subtract, r=["PF0", gk], w=[gk])
                    self.act(ekt, dgl, AF.Exp, r=[gk], w=[gk])
                    self.tt("dve", bg, b4, egc, ALU.mult, r=["bet", gk], w=[gk])
                    yield None
                    self.tt("dve", Gm[p], self.maft[r].unsqueeze(1).to_broadcast([128, 4, 128]),
                            g4.unsqueeze(2).to_broadcast([128, 4, 128]), ALU.mult, r=["cF", "gl"], w=["Gm"])
                    self.mm(ba, self.tri[r], Gm[p].rearrange("p h j -> p (h j)"), True, True,
                            r=["Gm", "cF"], w=[bak])
                    for h in range(4):
                        self.mm(bb[:, h * 128:(h + 1) * 128], Gm[p][:, h, :], self.tri[r], True, True,
                                r=["Gm", "cF"], w=[bbk])
                    d2 = dec[p].rearrange("p h j -> p (h j)")
                    dT2 = decT[p].rearrange("p h j -> p (h j)")
                    self.act(d2, ba, AF.Exp, r=[bak], w=["dec" + sfx])
                    self.act(dT2, bb, AF.Exp, r=[bbk], w=["decT" + sfx])
                    yield None
                    for h in range(4):
                        rows = slice((h % 2) * 64, (h % 2) * 64 + 64)
                        kT_h = BT[rows, 2 + h // 2, cols]
                        qT_h = BT[rows, h // 2, cols]
                        bank, bk = (ba, bak) if h % 2 == 0 else (bb, bbk)
                        self.mm(bank[:, (h // 2) * 128:(h // 2) * 128 + 128], kT_h, kT_h, True, True, r=["BT"], w=[bk])
                        self.mm(bank[:, 256 + (h // 2) * 128:256 + (h // 2) * 128 + 128], kT_h, qT_h, True, True,
                                r=["BT"], w=[bk])
                    yield None
                    mstrict = self.maft[r].unsqueeze(1).to_broadcast([128, 4, 128])
                    minclT = self.tri[r].unsqueeze(1).to_broadcast([128, 4, 128])
                    self.tt(POOLENG, dec[p], dec[p], mstrict, ALU.mult, r=["dec" + sfx, "cF"], w=["dec" + sfx])
                    self.tt(POOLENG, dec[p], dec[p], nb4.unsqueeze(2).to_broadcast([128, 4, 128]), ALU.mult,
                            r=["dec" + sfx, "nbet"], w=["dec" + sfx])
                    for par, (bank, bk) in enumerate(((ba, bak), (bb, bbk))):
                        self.tt("dve", Nb[p][:, par::2, :], bank[:, 0:256].rearrange("p (h j) -> p h j", h=2),
                                dec[p][:, par::2, :], ALU.mult, r=[bk, "dec" + sfx], w=["Nb" + sfx])
                    self.tt(POOLENG, decT[p], decT[p], minclT, ALU.mult, r=["decT" + sfx, "cF"], w=["decT" + sfx])
                    for par, (bank, bk) in enumerate(((ba, bak), (bb, bbk))):
                        self.tt("dve", ATb[p][:, par::2, :], bank[:, 256:512].rearrange("p (h j) -> p h j", h=2),
                                decT[p][:, par::2, :], ALU.mult, r=[bk, "decT" + sfx], w=["ATb" + sfx])
                    yield None
                    for h in range(4):
                        self.tr(PT[1][:, h * 128:(h + 1) * 128], Nb[p][:, h, :], r=["Nb" + sfx, "cB"], w=["PT1"])
                    self.cp("act", NTb[p].rearrange("p h j -> p (h j)"), PT[1][:, 0:512], r=["PT1"], w=["NTb" + sfx])
                    QT_ = lambda lv: self.qm[:, (0 if r == 0 else 7) + lv, :].unsqueeze(1).to_broadcast([128, 4, 128])
                    QZ_ = lambda lv: self.qm[:, (7 if r == 0 else 0) + lv, :].unsqueeze(1).to_broadcast([128, 4, 128])
                    idb = self.ident_b.unsqueeze(1).to_broadcast([128, 4, 128])
                    Tc, Tn_, tck, tnk = Xb[p], Yb[p], "Xb" + sfx, "Yb" + sfx
                    Zc, Zn_, zck, znk = Zb[p], Zc2[p], "Zb" + sfx, "Zc2" + sfx
                    self.tt("pool", Pm[p], Nb[p], QT_(0), ALU.mult, r=["Nb" + sfx, "qm"], w=["Pm"])
                    self.tt("pool", Tc, Pm[p], idb, ALU.add, r=["Pm", "cB"], w=[tck])
                    self.tt("pool", Qm[p], NTb[p], QZ_(0), ALU.mult, r=["NTb" + sfx, "qm"], w=["Qm"])
                    self.tt("pool", Zc, Qm[p], idb, ALU.add, r=["Qm", "cB"], w=[zck])
                    yield None
                    for lv in range(1, 7):
                        for h in range(4):
                            self.mm(ba[:, h * 128:(h + 1) * 128], NTb[p][:, h, :], Tc[:, h, :], True, True,
                                    r=["NTb" + sfx, tck], w=[bak])
                        for h in range(4):
                            self.mm(bb[:, h * 128:(h + 1) * 128], Nb[p][:, h, :], Zc[:, h, :], True, True,
                                    r=["Nb" + sfx, zck], w=[bbk])
                        self.tt("dve", Pm[p], ba.rearrange("p (h j) -> p h j", h=4), QT_(lv), ALU.mult,
                                r=[bak, "qm"], w=["Pm"])
                        self.tt("dve", Qm[p], bb.rearrange("p (h j) -> p h j", h=4), QZ_(lv), ALU.mult,
                                r=[bbk, "qm"], w=["Qm"])
                        if lv < 6:
                            for h in range(4):
                                self.mm(ba[:, h * 128:(h + 1) * 128], Zc[:, h, :], Pm[p][:, h, :], True, True,
                                        r=[zck, "Pm"], w=[bak])
                        for h in range(4):
                            self.mm(bb[:, h * 128:(h + 1) * 128], Tc[:, h, :], Qm[p][:, h, :], True, True,
                                    r=[tck, "Qm"], w=[bbk])
                        if lv < 6:
                            self.tt("dve", Tn_, ba.rearrange("p (h j) -> p h j", h=4), Tc, ALU.add,
                                    r=[bak, tck], w=[tnk])
                        self.tt("dve", Zn_, bb.rearrange("p (h j) -> p h j", h=4), Zc, ALU.add,
                                r=[bbk, zck], w=[znk])
                        Tc, Tn_, tck, tnk = Tn_, Tc, tnk, tck
                        Zc, Zn_, zck, znk = Zn_, Zc, znk, zck
                        yield None
                    Zf, zfk = Zc, zck
                    yield None
                    self.tt("pool", vb[p], vtok, b4.unsqueeze(2).to_broadcast([128, 4, 64]), ALU.mult,
                            r=["KVt" + sfx, "bet"], w=["vb" + sfx])
                    self.tt("pool", kbg[p], ktok, bg.unsqueeze(2).to_broadcast([128, 4, 64]), ALU.mult,
                            r=["KVt" + sfx, gk], w=["kbg" + sfx])
                    self.tt("pool", ktil[p], ktok, ekt.unsqueeze(2).to_broadcast([128, 4, 64]), ALU.mult,
                            r=["KVt" + sfx, gk], w=["ktil" + sfx])
                    for h in range(4):
                        self.mm(PF[0][:, h * 64:(h + 1) * 64], Zf[:, h, :], vb[p][:, h, :], True, True,
                                r=[zfk, "vb" + sfx], w=["PF0"])
                    for h in range(4):
                        rows = slice((h % 2) * 64, (h % 2) * 64 + 64)
                        self.mm(PF[5][rows, (h // 2) * 128:(h // 2) * 128 + 128], kbg[p][:, h, :], Zf[:, h, :],
                                True, True, r=[zfk, "kbg" + sfx], w=["PF5"])
                    self.cp("act", U[p].rearrange("p h d -> p (h d)"), PF[0][:, 0:256], r=["PF0"], w=["U" + sfx])
                    self.cp("act", WT[p].rearrange("p h j -> p (h j)"), PF[5][:, 0:256], r=["PF5"], w=["WT" + sfx])
                    egl2 = gs[:, 24:26]
                    self.cp("dve", egl2[0:64, :], eglast[0:64, 0:4:2], r=[gk], w=[gk])
                    self.cp("dve", egl2[64:128, :], eglast[64:128, 1:4:2], r=[gk], w=[gk])
                    if s == 0 and r == 1 and ci == 0 and "chk" in self.dbg:
                        self.dbg |= {"c_dec", "c_N", "c_AT", "c_Z", "c_U", "c_WT", "c_gs", "c_ktil", "c_vb"}
                        self.tap("c_dec", dec[p], r=["dec" + sfx])
                        self.tap("c_AT", ATb[p], r=["ATb" + sfx])
                        self.tap("c_Z", Zf, r=[zfk])
                        self.tap("c_U", U[p], r=["U" + sfx])
                        self.tap("c_WT", WT[p], r=["WT" + sfx])
                        self.tap("c_gs", gs[:, 0:26], r=[gk])
                        self.tap("c_ktil", ktil[p], r=["ktil" + sfx])
                        self.tap("c_vb", vb[p], r=["vb" + sfx])
                    yield "REC"
                    for h in range(4):
                        rows = slice((h % 2) * 64, (h % 2) * 64 + 64)
                        bank, bk = (PF[4][:, 0:128], "PF4") if h % 2 == 0 else (PF[5][:, 256:384], "PF5")
                        self.mm(bank[:, (h // 2) * 64:(h // 2) * 64 + 64], WT[p][rows, h // 2, :], Sb[rows, h // 2, :],
                                True, True, r=["WT" + sfx, "Sb"], w=[bk])
                    for par, (bank, bk) in enumerate(((PF[4][:, 0:128], "PF4"), (PF[5][:, 256:384], "PF5"))):
                        self.tt("dve", vnew[p][:, par::2, :], U[p][:, par::2, :],
                                bank.rearrange("p (h d) -> p h d", h=2), ALU.subtract,
                                r=["U" + sfx, bk], w=["vnew" + sfx])
                    for h in range(4):
                        rows = slice((h % 2) * 64, (h % 2) * 64 + 64)
                        bank, bk = (PF[0][:, 0:128], "PF0") if h % 2 == 0 else (PF[1][:, 0:128], "PF1")
                        self.mm(bank[:, (h // 2) * 64:(h // 2) * 64 + 64], BT[rows, h // 2, cols], Sb[rows, h // 2, :],
                                True, True, r=["BT", "Sb"], w=[bk])
                    for h in range(4):
                        self.mm(PF[5][:, h * 64:(h + 1) * 64], ATb[p][:, h, :], vnew[p][:, h, :], True, True,
                                r=["ATb" + sfx, "vnew" + sfx], w=["PF5"])
                    for par, (bank, bk) in enumerate(((PF[0][:, 0:128], "PF0"), (PF[1][:, 0:128], "PF1"))):
                        self.tt("dve", tmpo[p][:, par::2, :], bank.rearrange("p (h d) -> p h d", h=2),
                                egc[:, par::2].unsqueeze(2).to_broadcast([128, 2, 64]), ALU.mult,
                                r=[bk, gk], w=["tmpo" + sfx])
                    ob = OB[:, c, :].rearrange("p (h d) -> p h d", h=4)
                    if r == 0:
                        self.tt("dve", ob, PF[5][:, 0:256].rearrange("p (h d) -> p h d", h=4), tmpo[p], ALU.add,
                                r=["PF5", "tmpo" + sfx], w=["OB"])
                    else:
                        self.tt("dve", tmpo[p], PF[5][:, 0:256].rearrange("p (h d) -> p h d", h=4), tmpo[p], ALU.add,
                                r=["PF5", "tmpo" + sfx], w=["tmpo" + sfx])
                        self.tt("pool", ob, ob, tmpo[p], ALU.add, r=["tmpo" + sfx, "OB"], w=["OB"])
                    for h in range(4):
                        rows = slice((h % 2) * 64, (h % 2) * 64 + 64)
                        self.mm(PF[3][rows, (h // 2) * 64:(h // 2) * 64 + 64], ktil[p][:, h, :], vnew[p][:, h, :],
                                True, True, r=["ktil" + sfx, "vnew" + sfx], w=["PF3"])
                    self.tt("pool", Sf, Sf, egl2.unsqueeze(2).to_broadcast([128, 2, 64]), ALU.mult,
                            r=["Sf", gk], w=["Sf"])
                    self.tt("dve", Sf, Sf, PF[3][:, 0:128].rearrange("p (h d) -> p h d", h=2), ALU.add,
                            r=["Sf", "PF3"], w=["Sf"])
                    self.cp("act", Sb, Sf, r=["Sf"], w=["Sb"])
                order = list(order)
                for k0 in range(0, len(order), 2):
                    gens = [chunk_gen(k0 + i, order[k0 + i], i) for i in range(min(2, len(order) - k0))]
                    live = list(gens)
                    while live:
                        for g in list(live):
                            if next(g) == "REC":
                                live.remove(g)
                    for g in gens:
                        for _ in g:
                            pass
                if not J.latent:
                    self.store(self.osb[s, l, r].rearrange("(j i) k v -> (i k) j v", i=2), Sf, r=["Sf"])
            self.S.phase = "b4_out"
            for j0 in range(0, ntq, 4):
                nj = min(4, ntq - j0)
                ob = OB[:, j0:j0 + nj, :]
                ob4 = ob.rearrange("p t (h d) -> p (t h) d", h=4)
                ss = self.small[:, 16:16 + nj * 4]
                rs = self.small[:, 32:32 + nj * 4]
                sq4 = tmpf[:, 0:nj * 256].rearrange("p (a d) -> p a d", d=64)
                self.tt("pool", sq4, ob4, ob4, ALU.mult, r=["OB"], w=["tmpf"])
                self.red("dve", ss, sq4, r=["tmpf"], w=["small"])
                self.rstd(rs, ss, 1.0 / (64.0 * 64.0), r=["small"], w=["small"])
                self.tt("dve", sq4, ob4, rs.unsqueeze(2).to_broadcast([128, nj * 4, 64]), ALU.mult,
                        r=["OB", "small"], w=["tmpf"])
                self.tt("pool", sq4, sq4, gon.unsqueeze(1).to_broadcast([128, nj * 4, 64]), ALU.mult,
                        r=["tmpf", "gon"], w=["tmpf"])
                self.tt("pool", self.YB[:, t0 + j0:t0 + j0 + nj, :].rearrange("p t c -> p (t c)"),
                        tmpf[:, 0:nj * 256], GT[:, j0:j0 + nj, :].rearrange("p t c -> p (t c)"), ALU.mult,
                        r=["tmpf", "GT"], w=["YB"])
            if s == 0:
                self.tap("ob_%s%d" % (J.name, l), OB[:, 0:2, :], r=["OB"])
        self.tap("yb_%s%d" % (J.name, l), self.YB, r=["YB"])

    def rope_apply(self, eng, dst, src, t, H, scr, skey, r, w):
        cos = self.rope[:, t, 0:32].unsqueeze(1).to_broadcast([128, H, 32])
        sin = self.rope[:, t, 32:64].unsqueeze(1).to_broadcast([128, H, 32])
        x1, x2 = src[:, :, 0:32], src[:, :, 32:64]
        sc = scr[:, 0:H * 64].rearrange("p (h d) -> p h d", h=H)
        t1, t2 = sc[:, :, 0:32], sc[:, :, 32:64]
        self.tt(eng, t1, x1, cos, ALU.mult, r=r + ["rope"], w=[skey])
        self.tt(eng, t2, x2, sin, ALU.mult, r=r + ["rope"], w=[skey])
        self.tt(eng, dst[:, :, 0:32], t1, t2, ALU.subtract, r=[skey], w=w)
        self.tt(eng, t1, x1, sin, ALU.mult, r=r + ["rope"] + w, w=[skey])
        self.tt(eng, t2, x2, cos, ALU.mult, r=r + ["rope"], w=[skey])
        self.tt(eng, dst[:, :, 32:64], t1, t2, ALU.add, r=[skey], w=w)

    def head_norm(self, src, H, gbc, dst, scr, skey, r, w, sm0):
        sq = scr[:, 0:H * 64].rearrange("p (h d) -> p h d", h=H)
        ss = self.small[:, sm0:sm0 + H]
        rs = self.small[:, sm0 + 8:sm0 + 8 + H]
        self.tt("pool", sq, src, src, ALU.mult, r=r, w=[skey])
        self.red("dve", ss, sq, r=[skey], w=["small"])
        self.rstd(rs, ss, 1.0 / 64.0, r=["small"], w=["small"])
        self.tt("dve", sq, src, rs.unsqueeze(2).to_broadcast([128, H, 64]), ALU.mult, r=r + ["small"], w=[skey])
        self.tt("pool", dst, sq, gbc.unsqueeze(1).to_broadcast([128, H, 64]), ALU.mult, r=[skey, "gqk"], w=w)

    def attn(self, nq, qT, qkey, kts, O, h65, st):
        PF = self.PF
        nj = nq // 128
        n = len(kts)
        slots = []
        SB = (2, 3, 0, 1)
        NE = len(st["E"])
        LA = 3

        def scores(i):
            kT, V, bias, rk = kts[i]
            c = st["cnt"]
            st["cnt"] += 1
            bi_ = SB[c % 4]
            sp = PF[bi_][:, 0:nq]
            spk = "PF%d" % bi_
            self.mm(sp, kT, qT, True, bias is None, r=rk + [qkey], w=[spk])
            if bias is not None:
                for bi, (qb, bap) in enumerate(bias):
                    self.mm(sp[:, qb * 64:(qb + 1) * 64], self.ident_b, bap, False, bi == len(bias) - 1,
                            r=["BB2", "cB"], w=[spk])
            slots.append((c, sp, spk))

        for i in range(min(LA, n)):
            scores(i)
        for i in range(n):
            if i + LA < n:
                scores(i + LA)
            kT, V, bias, rk = kts[i]
            c, sp, spk = slots[i]
            E = st["E"][c % NE][:, 0:nq]
            ek = "E%d" % (c % NE)
            self.act(E, sp, AF.Exp, r=[spk], w=[ek], scale=0.125)
            for jj in range(nj):
                self.mm(O[jj][0][:, h65 * 65:h65 * 65 + 65], E[:, jj * 128:(jj + 1) * 128], V, i == 0,
                        i == n - 1, r=[ek] + rk, w=[O[jj][1]])

    def attn_out(self, Ops, okey, H, gbc, gkey, ydst, ykey, oscr, oskey, sm0):
        ov = Ops[:, 0:H * 65].rearrange("p (h d) -> p h d", h=H)
        rden = self.small[:, sm0:sm0 + H]
        self.add("dve", lambda e: e.reciprocal(out=rden.unsqueeze(2), in_=ov[:, :, 64:65]), [okey], ["small"])
        o3 = oscr[:, 0:H * 64].rearrange("p (h d) -> p h d", h=H)
        self.tt("dve", o3, ov[:, :, 0:64], rden.unsqueeze(2).to_broadcast([128, H, 64]), ALU.mult,
                r=[okey, "small"], w=[oskey])
        ss = self.small[:, sm0 + 8:sm0 + 9]
        rs = self.small[:, sm0 + 9:sm0 + 10]
        junk = self.junkb[:, 0:H * 64]
        self.act(junk, oscr[:, 0:H * 64], AF.Square, r=[oskey], w=["junkb", "small"], accum=ss)
        self.rstd(rs, ss, 1.0 / (H * 64.0), r=["small"], w=["small"])
        self.stt("dve", ydst, oscr[:, 0:H * 64], rs, gbc, ALU.mult, ALU.mult, r=[oskey, "small", gkey], w=[ykey])

    def residual_epilogue(self, t, Gbc, gkey, tmp, tkey):
        PF = self.PF
        ss = self.small[:, 40:42]
        rs = self.small[:, 42:43]
        junk = self.junkb[:, 0:512]
        self.act(junk, PF[0], AF.Square, r=["PF0"], w=["junkb", "small"], accum=ss[:, 0:1])
        self.act(junk, PF[1], AF.Square, r=["PF1"], w=["junkb", "small"], accum=ss[:, 1:2])
        self.tt("dve", ss[:, 0:1], ss[:, 0:1], ss[:, 1:2], ALU.add, r=["small"], w=["small"])
        self.rstd(rs, ss[:, 0:1], 1.0 / D, r=["small"], w=["small"])
        self.stt("dve", tmp[:, 0:512], PF[0], rs, Gbc[:, 0:512], ALU.mult, ALU.mult, r=["PF0", "small", gkey], w=[tkey])
        self.stt("dve", tmp[:, 512:1024], PF[1], rs, Gbc[:, 512:1024], ALU.mult, ALU.mult,
                 r=["PF1", "small", gkey], w=[tkey])
        self.tt("pool", self.X[:, t, :], self.X[:, t, :], tmp, ALU.add, r=[tkey, "X%d" % t], w=["X%d" % t])

    def phase_kvq(self, l, J):
        S = self.S
        A = self.A
        PF, PT = self.PF, self.PT
        S.barrier()
        A.release(self.yb_mark)
        nt, T = J.nt, J.T
        ntq = T // 128
        nctx = 2 if J.latent else 0
        nkt_seq = ntq + nctx
        NKT = J.nseq * nkt_seq
        Wkv = A.alloc([8, 1024], BF16)
        Wq = A.alloc([8, 768], BF16)
        Wo = Wkv
        kTA = A.alloc([NKT * 128], BF16)
        VA = A.alloc([NKT, 2, 65], BF16)
        kTC = A.alloc([3, NKT * 128], BF16)
        VC = A.alloc([NKT, 6, 65], BF16)
        gqk = A.alloc([2, 64])
        goa = A.alloc([384])
        goc = A.alloc([384])
        hTq = A.alloc([8, 256], BF16)
        hT1 = hTq[:, :, 0:128]
        xn = A.alloc([D], BF16)
        tmpf = A.alloc([D])
        self.junkb = A.alloc([512], BF16)
        ZA0 = A.alloc([256])
        ZA = [ZA0, ZA0]
        ZC0 = A.alloc([768])
        ZC = [ZC0, ZC0]
        knf0 = A.alloc([128])
        knf = [knf0, knf0]
        scr = A.alloc([768])
        kab = A.alloc([128], BF16)
        kcb = A.alloc([384], BF16)
        ZQ = A.alloc([384])
        qab = A.alloc([3, 2, 64], BF16)
        qT_all = A.alloc([6, 256], BF16)
        qTA = qT_all[:, 0:3, :]
        qTC = qT_all[:, 3:6, :]
        xnb = [xn, qT_all.rearrange("p a b -> p (a b)")[:, 0:D]]
        hbuf = [hTq[:, :, 0:128], hTq[:, :, 128:256]]
        Eb = [A.alloc([256], BF16) for _ in range(6)]
        oscr = A.alloc([384])
        qnf = oscr
        ya = [A.alloc([384], BF16) for _ in range(2)]
        yc = [A.alloc([384], BF16) for _ in range(2)]
        yT = A.alloc([8, 128], BF16)
        tmpx = tmpf

        win = self.w_in[l]
        wv = win.rearrange("(k p) n -> p k n", p=128)
        self.dma("pool", Wkv[:, :, 0:256], wv[:, :, C_AK:C_AK + 256], w=["Wkv"])
        self.dma("pool", Wkv[:, :, 256:1024], wv[:, :, C_CK:C_CK + 768], w=["Wkv"])
        self.dma("pool", Wq[:, :, 0:384], wv[:, :, C_AQ:C_AQ + 384], w=["Wq"])
        self.dma("pool", Wq[:, :, 384:768], wv[:, :, C_CQ:C_CQ + 384], w=["Wq"])
        self.dma("sp", gqk.rearrange("p a d -> p (a d)"),
                 self.g_qk_a[l:l + 1].rearrange("o a d -> o (a d)").partition_broadcast(128), w=["gqk"])
        self.dma("sp", goa, self.g_out_a[l:l + 1, :].partition_broadcast(128), w=["goa"])
        self.dma("sp", goc, self.g_out_c[l:l + 1, :].partition_broadcast(128), w=["goc"])
        if J.latent:
            self.kv_scr, self.kv_tmpf = scr, tmpf
            self.build_bias(l)
        self.memset("pool", VA[:, :, :, 64:65], 1.0, w=["VA"])
        self.memset("pool", VC[:, :, :, 64:65], 1.0, w=["VC"])

        if os.environ.get("KSTOP") == "0":
            return
        self.S.phase = "kv"
        self.hT_pre(0, xnb[0], "xnb0", 48)
        self.hT_post(0, hbuf[0], "hTq0", 0, xnb[0], "xnb0", tmpf)
        for t in range(nt):
            s, j = t // ntq, t % ntq
            kt = s * nkt_seq + j
            p2 = t % 2
            hT1 = hbuf[p2]
            for k in range(8):
                self.mm(PF[0], hT1[:, k, :], Wkv[:, k, 0:512], k == 0, k == 7, r=["hTq%d" % p2, "Wkv"], w=["PF0"])
            for k in range(8):
                self.mm(PF[1], hT1[:, k, :], Wkv[:, k, 512:1024], k == 0, k == 7, r=["hTq%d" % p2, "Wkv"], w=["PF1"])
            if t + 1 < nt:
                q2 = (t + 1) % 2
                self.hT_pre(t + 1, xnb[q2], "xnb%d" % q2, 48 + 2 * q2)
                self.hT_post(0, hbuf[q2], "hTq%d" % q2, 0, xnb[q2], "xnb%d" % q2, tmpf)
            if os.environ.get("KSTOP") == "0a":
                continue
            za, zc = ZA[p2], ZC[p2]
            zak, zck = "ZA", "ZC"
            self.cp("act", za, PF[0][:, 0:256], r=["PF0"], w=[zak])
            self.cp(os.environ.get("CPENG", "dve"), zc[:, 0:256], PF[0][:, 256:512], r=["PF0"], w=[zck])
            self.cp("act", zc[:, 256:768], PF[1], r=["PF1"], w=[zck])
            if os.environ.get("KSTOP") == "0d":
                continue
            kn = knf[p2]
            knk = "knf"
            kn3 = kn.rearrange("p (h d) -> p h d", h=2)
            self.head_norm(za[:, 0:128].rearrange("p (h d) -> p h d", h=2), 2, gqk[:, 1, :], kn3, scr, "scr",
                           [zak], [knk], 24)
            if os.environ.get("KSTOP") == "0e":
                continue
            if not J.latent and os.environ.get("KSTOP") == "0c":
                self.cp("pool", kab, kn, r=[knk], w=["kab"])
                continue
            if not J.latent:
                tok = slice(j * 128, (j + 1) * 128)
                self.store(self.oka[s, l, tok, :], kn, r=[knk])
                self.store(self.ova[s, l, tok, :], za[:, 128:256], r=[zak])
                self.store(self.okc[s, l, tok, :], zc[:, 0:384], r=[zck])
                self.store(self.ovc[s, l, tok, :], zc[:, 384:768], r=[zck])
                self.cp("pool", kab, kn, r=[knk], w=["kab"])
            else:
                self.rope_apply("pool", kab.rearrange("p (h d) -> p h d", h=2), kn3, j, 2, scr, "scr", [knk], ["kab"])
            if os.environ.get("KSTOP") == "0b":
                continue
            self.tr(PT[1][:, 0:128], kab, r=["kab", "cB"], w=["PT1"])
            self.cp("pool", VA[:, kt, :, 0:64], za[:, 128:256].rearrange("p (h d) -> p h d", h=2), r=[zak], w=["VA"])
            self.cp("pool", kcb, zc[:, 0:384], r=[zck], w=["kcb"])
            for p in range(3):
                self.tr(PT[1][:, 128 + p * 128:256 + p * 128], kcb[:, p * 128:(p + 1) * 128], r=["kcb", "cB"], w=["PT1"])
            self.cp("act", kTA[:, kt * 128:(kt + 1) * 128], PT[1][:, 0:128], r=["PT1"], w=["kTA"])
            self.cp("dve", kTC[:, :, kt * 128:(kt + 1) * 128], PT[1][:, 128:512].rearrange("p (a b) -> p a b", a=3),
                    r=["PT1"], w=["kTC"])
            self.cp("pool", VC[:, kt, :, 0:64], zc[:, 384:768].rearrange("p (h d) -> p h d", h=6), r=[zck], w=["VC"])
        if J.latent:
            for j in range(2):
                kt = ntq + j
                tok = slice(j * 128, (j + 1) * 128)
                self.dma("pool", kab, self.cak[l, tok, :], w=["kab"])
                self.dma("pool", kcb, self.cck[l, tok, :], w=["kcb"])
                self.dma("pool", VA[:, kt, :, 0:64], self.cav[l, tok, :].rearrange("p (h d) -> p h d", h=2), w=["VA"])
                self.dma("pool", VC[:, kt, :, 0:64], self.ccv[l, tok, :].rearrange("p (h d) -> p h d", h=6), w=["VC"])
                self.tr(PT[1][:, 0:128], kab, r=["kab", "cB"], w=["PT1"])
                for p in range(3):
                    self.tr(PT[1][:, 128 + p * 128:256 + p * 128], kcb[:, p * 128:(p + 1) * 128], r=["kcb", "cB"],
                            w=["PT1"])
                self.cp("act", kTA[:, kt * 128:(kt + 1) * 128], PT[1][:, 0:128], r=["PT1"], w=["kTA"])
                self.cp("dve", kTC[:, :, kt * 128:(kt + 1) * 128],
                        PT[1][:, 128:512].rearrange("p (a b) -> p a b", a=3), r=["PT1"], w=["kTC"])
        self.tap("kTA_%s%d" % (J.name, l), kTA, r=["kTA"])
        self.load_w(Wo, self.w_out[l], 0, D, "Wkv")

        if os.environ.get("KSTOP") == "1":
            return
        S.barrier()
        ast = {"cnt": 0, "E": Eb}
        for g in range(nt // 2):
            tl = [2 * g, 2 * g + 1]
            s = tl[0] // ntq
            self.make_hT(J, tl, 0, hTq, "hTq", xn, tmpf)
            self.S.phase = "q_proj"
            for jj, t in enumerate(tl):
                for k in range(8):
                    self.mm(PF[0][:, 0:384], hTq[:, k, jj * 128:(jj + 1) * 128], Wq[:, k, 0:384], k == 0, k == 7,
                            r=["hTq", "Wq"], w=["PF0"])
                self.cp("act", ZQ, PF[0][:, 0:384], r=["PF0"], w=["ZQ"])
                q3 = qnf.rearrange("p (h d) -> p h d", h=6)
                self.head_norm(ZQ.rearrange("p (h d) -> p h d", h=6), 6, gqk[:, 0, :], q3, scr, "scr", ["ZQ"], ["oscr"], 24)
                qdst = qab.rearrange("q p g d -> q g p d")
                qsrc = qnf.rearrange("q (g p d) -> q g p d", g=2, p=3)
                if J.latent:
                    for gg in range(2):
                        self.rope_apply("pool", qdst[:, gg], qsrc[:, gg], t % ntq, 3, scr, "scr", ["oscr"], ["qab"])
                else:
                    for gg in range(2):
                        self.cp("pool", qdst[:, gg], qsrc[:, gg], r=["oscr"], w=["qab"])
                for p in range(3):
                    self.tr(PT[1][:, jj * 384 + p * 128:jj * 384 + (p + 1) * 128],
                            qab[:, p].rearrange("q g d -> q (g d)"), r=["qab", "cB"], w=["PT1"])
                self.cp("act", qTA[:, :, jj * 128:(jj + 1) * 128],
                        PT[1][:, jj * 384:(jj + 1) * 384].rearrange("p (a b) -> p a b", a=3), r=["PT1"], w=["qTA"])
            for p in range(3):
                ps, pk = (PF[1][:, (p % 2) * 256:(p % 2) * 256 + 256], "PF1") if p < 2 else (PF[0][:, 0:256], "PF0")
                for k in range(8):
                    self.mm(ps, Wq[:, k, 384 + p * 128:384 + (p + 1) * 128], hTq[:, k, :], k == 0, k == 7,
                            r=["hTq", "Wq"], w=[pk])
                self.cp("dve" if p % 2 else "act", qTC[:, p, :], ps, r=[pk], w=["qTC"])
            if os.environ.get("KSTOP") == "2":
                continue
            self.S.phase = "attnA"
            O = [(PF[4], "PF4"), (PF[5], "PF5")]
            for h in range(6):
                p, gk_ = h % 3, h // 3
                rows = slice(gk_ * 64, gk_ * 64 + 64)
                kts = []
                for i in range(nkt_seq):
                    kt = s * nkt_seq + i
                    kts.append((kTA[rows, kt * 128:(kt + 1) * 128], VA[:, kt, gk_, :], None, ["kTA", "VA"]))
                self.attn(256, qTA[rows, p, :], "qTA", kts, O, h, ast)
            for jj in range(2):
                self.attn_out(O[jj][0], O[jj][1], 6, goa, "goa", ya[jj], "ya%d" % jj, oscr, "oscr", 24)
            if os.environ.get("KSTOP") == "3":
                continue
            self.S.phase = "attnC"
            if not J.latent:
                for h in range(6):
                    p = h // 2
                    rows = slice((h % 2) * 64, (h % 2) * 64 + 64)
                    kts = []
                    for i in range(nkt_seq):
                        kt = s * nkt_seq + i
                        kts.append((kTC[rows, p, kt * 128:(kt + 1) * 128], VC[:, kt, h, :], None, ["kTC", "VC"]))
                    self.attn(256, qTC[rows, p, :], "qTC", kts, O, h, ast)
            else:
                self.latent_c(l, J, tl, kTC, VC, qTC, O, ast)
            for jj in range(2):
                self.attn_out(O[jj][0], O[jj][1], 6, goc, "goc", yc[jj], "yc%d" % jj, oscr, "oscr", 24)
            if g == 0:
                self.tap("ya_%s%d" % (J.name, l), ya[0], r=["ya0"])
                self.tap("yc_%s%d" % (J.name, l), yc[0], r=["yc0"])
            if os.environ.get("KSTOP") == "4":
                continue
            self.S.phase = "merge"
            for jj, t in enumerate(tl):
                for kk in range(8):
                    if kk < 3:
                        src, rk = ya[jj][:, kk * 128:(kk + 1) * 128], "ya%d" % jj
                    elif kk < 5:
                        src, rk = self.YB[:, t, (kk - 3) * 128:(kk - 2) * 128], "YB"
                    else:
                        src, rk = yc[jj][:, (kk - 5) * 128:(kk - 4) * 128], "yc%d" % jj
                    self.tr(PT[0][:, kk * 128:(kk + 1) * 128], src, r=[rk, "cB"], w=["PT0"])
                self.cp("act", yT.rearrange("p k c -> p (k c)"), PT[0], r=["PT0"], w=["yT"])
                for hf in range(2):
                    for k in range(8):
                        self.mm(PF[hf], yT[:, k, :], Wo[:, k, hf * 512:(hf + 1) * 512], k == 0, k == 7,
                                r=["yT", "Wkv"], w=["PF%d" % hf])
                self.residual_epilogue(t, self.G1, "G1", tmpx, "tmpf")
        self.tap("x1_%s%d" % (J.name, l), self.X[:, 0:nt, :], r=["X%d" % t for t in range(nt)])

    def build_bias(self, l):
        A = self.A
        PF = self.PF
        BB2 = A.alloc([6, 17, 64], BF16)
        self.BB2 = BB2
        rT = self.kv_scr[:, 512:614]
        ohs = [self.kv_tmpf[:, 0:512], self.kv_tmpf[:, 512:1024]]
        tb = [self.kv_scr[:, 0:256].bitcast(BF16), self.kv_scr[:, 256:512].bitcast(BF16)]
        self.dma("sp", rT[0:33, :], self.rsel, w=["scr"])
        r3 = rT[0:31, :].rearrange("c (h d) -> c h d", h=6)
        for h in range(6):
            self.dma("sp", r3[:, h, 1:16], self.rpb[l, h].rearrange("d c -> c d"), r=["scr"], w=["scr"], slow=True)
        for ch in range(8):
            b2 = ch % 2
            self.dma("sp", ohs[b2][0:33, :], self.ohc[:, ch * 512:(ch + 1) * 512], w=["tmpf"])
            self.mm(PF[0][0:102, :], rT[0:33, :], ohs[b2][0:33, :], True, True, r=["scr", "tmpf"], w=["PF0"])
            self.act(tb[b2][0:102, :], PF[0][0:102, :], AF.Copy, r=["PF0"], w=["scr"], scale=8.0)
            self.dma("sp", self.tabscr[:, ch * 512:(ch + 1) * 512], tb[b2][0:102, :], r=["scr"], w=["tabscr"])
        tv = self.tabscr.rearrange("(h d) (k q) -> k h d q", h=6, k=64)
        for h in range(6):
            self.dma("sp", BB2[0:64, h, :, :], tv[:, h, :, :], r=["tabscr"], w=["BB2"])
            self.dma("sp", BB2[64:128, h, 0:16, :], tv[:, h, 1:17, :], r=["tabscr"], w=["BB2"])
            self.dma("sp", BB2[64:128, h, 16:17, :], tv[:, h, 16:17, :], r=["tabscr"], w=["BB2"])

    def latent_c(self, l, J, tl, kTC, VC, qTC, O, ast):
        clip = lambda v: min(max(v, 0), 24)
        for jj, t in enumerate(tl):
            s0, s1 = clip(2 * t - 4), clip(2 * t + 1 - 4)
            kt_lo, kt_hi = s0 // 2, (s1 + 7) // 2
            for h in range(6):
                p = h // 2
                rows = slice((h % 2) * 64, (h % 2) * 64 + 64)
                kts = []
                for kt in range(kt_lo, kt_hi + 1):
                    dl = kt - t
                    bl = []
                    for qb in range(2):
                        d = 2 * dl + 8 - qb
                        assert 0 <= d <= 16, d
                        bl.append((qb, self.BB2[:, h, d, :]))
                        sq_ = clip(2 * t + qb - 4)
                        for kb in range(2):
                            krow = 2 * kt + kb
                            if not (sq_ <= krow <= sq_ + 7):
                                bl.append((qb, self.negh[:, kb, :]))
                    kts.append((kTC[rows, p, kt * 128:(kt + 1) * 128], VC[:, kt, h, :], bl, ["kTC", "VC"]))
                for c in (16, 17):
                    kts.append((kTC[rows, p, c * 128:(c + 1) * 128], VC[:, c, h, :], None, ["kTC", "VC"]))
                self.attn(128, qTC[rows, p, jj * 128:(jj + 1) * 128], "qTC", kts, [O[jj]], h, ast)

    def phase_f(self, l, J):
        S = self.S
        A = self.A
        PF, PT = self.PF, self.PT
        S.barrier()
        A.release(self.base_mark)
        nt = J.nt
        GF = 4
        Wd = A.alloc([22, D], BF16)
        hT2s = [A.alloc([8, GF * 128], BF16) for _ in range(2)]
        actT = A.alloc([22, GF * 128], BF16)
        ring = [A.alloc([8, 256], BF16) for _ in range(3)]
        xn4 = [A.alloc([D], BF16) for _ in range(GF)]
        tmpf = A.alloc([D])
        self.junkb = A.alloc([512], BF16)
        sg = [A.alloc([512]) for _ in range(2)]
        tmpx = A.alloc([D])
        self.S.phase = "ffn"
        self.dma("pool", Wd, self.w_down[l].rearrange("(c p) n -> p c n", p=128), w=["Wd"])
        wg = self.w_gu[l].rearrange("(k p) n -> p k n", p=128)
        ng = nt // GF
        for j in range(GF):
            self.hT_pre(j, xn4[j], "xn4_%d" % j, 48 + 2 * j)
            self.hT_post(1, hT2s[0], "hT2_0", j, xn4[j], "xn4_%d" % j, tmpf)
        for g in range(ng):
            tl = list(range(g * GF, (g + 1) * GF))
            hT2 = hT2s[g % 2]
            hk = "hT2_%d" % (g % 2)
            nxt = list(range((g + 1) * GF, (g + 2) * GF)) if g + 1 < ng else []
            for c in range(22):
                if nxt and c == 1:
                    for j, t in enumerate(nxt):
                        self.hT_pre(t, xn4[j], "xn4_%d" % j, 48 + 2 * j)
                if nxt and c in (8, 11, 14, 17):
                    j = (c - 8) // 3
                    self.hT_post(1, hT2s[(g + 1) % 2], "hT2_%d" % ((g + 1) % 2), j, xn4[j], "xn4_%d" % j, tmpf)
                sl = c % 3
                rk = "ring%d" % sl
                self.dma("pool", ring[sl][:, :, 0:128], wg[:, :, c * 128:(c + 1) * 128], w=[rk])
                self.dma("pool", ring[sl][:, :, 128:256], wg[:, :, DFF + c * 128:DFF + (c + 1) * 128], w=[rk])
                pg, pu = PF[2 + (c % 2) * 2], PF[3 + (c % 2) * 2]
                pgk, puk = "PF%d" % (2 + (c % 2) * 2), "PF%d" % (3 + (c % 2) * 2)
                for k in range(8):
                    self.mm(pg, ring[sl][:, k, 0:128], hT2[:, k, :], k == 0, k == 7, r=[rk, hk], w=[pgk])
                for k in range(8):
                    self.mm(pu, ring[sl][:, k, 128:256], hT2[:, k, :], k == 0, k == 7, r=[rk, hk], w=[puk])
                s2 = sg[c % 2]
                self.act(s2, pg, AF.Silu, r=[pgk], w=["sg%d" % (c % 2)])
                self.tt("dve", actT[:, c, :], s2, pu, ALU.mult, r=["sg%d" % (c % 2), puk], w=["actT"])
            for jj, t in enumerate(tl):
                for hf in range(2):
                    for c in range(22):
                        self.mm(PF[hf], actT[:, c, jj * 128:(jj + 1) * 128], Wd[:, c, hf * 512:(hf + 1) * 512],
                                c == 0, c == 21, r=["actT", "Wd"], w=["PF%d" % hf])
                self.residual_epilogue(t, self.G2, "G2", tmpx, "tmpx")


def make_consts():
    k = np.arange(128)[:, None]
    i = np.arange(128)[None, :]
    cp = np.zeros((128, 7, 128), np.float32)
    cp[:, 0] = (k == i)
    cp[:, 1] = (k <= i)
    cp[:, 2] = (k >= i)
    cp[:, 3] = (k > i)
    cp[:, 4] = (k < i)
    cp[:, 5] = 1.0
    cp[:, 6] = ((k // 64) == (i // 64))
    t = np.arange(TS)
    row = (t // 64).astype(np.float32)
    col = (t % 64).astype(np.float32)
    inv = (10000.0 ** (-np.arange(16, dtype=np.float32) / 16)).astype(np.float32)
    ang = np.concatenate([row[:, None] * inv, col[:, None] * inv], axis=-1).astype(np.float32)
    tab = np.concatenate([np.cos(ang), np.sin(ang)], axis=-1).astype(np.float32)
    ropet = tab.reshape(16, 128, 64).transpose(1, 0, 2).reshape(128, 16 * 64)
    qm = np.zeros((128, 14, 128), np.float32)
    for lv in range(7):
        b = 2 ** lv
        ll = ((k // (2 * b)) == (i // (2 * b))) & ((k % (2 * b)) >= b) & ((i % (2 * b)) < b)
        qm[:, lv] = ll
        qm[:, 7 + lv] = ll.T
    rsel = np.zeros((33, 6, 17), np.float32)
    rsel[31, :, 1:16] = 1.0
    rsel[32, :, 0] = 1.0
    rsel[32, :, 16] = 1.0
    kc = np.arange(64)[:, None]
    qc = np.arange(64)[None, :]
    cst = np.clip(qc - 8, 0, 48)
    inwin = (kc >= cst) & (kc < cst + 16)
    oh = np.zeros((33, 64, 64), np.float32)
    dcc = kc - qc + 15
    for c in range(31):
        oh[c] = ((dcc == c) & inwin)
    oh[31] = np.where(inwin, 0.0, NEG / 8.0)
    oh[32] = NEG / 8.0
    negh = np.zeros((128, 2, 64), np.float32)
    negh[0:64, 0, :] = NEG * 8.0
    negh[64:128, 1, :] = NEG * 8.0
    return (cp.reshape(128, 7 * 128), np.ascontiguousarray(ropet), qm.reshape(128, 14 * 128),
            rsel.reshape(33, 102), oh.reshape(33, 4096), negh.reshape(128, 128))


_CACHE = {}


def get_nc(nl=DEPTH, jobs=("p", "s"), dbg=(), same=True, stop=None):
    key = (nl, tuple(jobs), tuple(sorted(dbg)), same, stop)
    if key not in _CACHE:
        kb = KB(nl=nl, jobs=jobs, dbg=dbg, same=same, stop=stop)
        nc = kb.build()
        _CACHE[key] = (nc, kb)
    return _CACHE[key]


def make_in_maps(inp):
    f = lambda a: np.ascontiguousarray(np.asarray(a, dtype=np.float32))
    cpack, ropet, qmask, rsel, ohc, neghd = make_consts()
    shared = {
        "w_mod": f(inp["w_mod"]), "b_mod": f(inp["b_mod"]), "g_norm": f(inp["g_norm"]), "w_in": f(inp["w_in"]),
        "g_qk_a": f(inp["g_qk_a"]), "g_out_a": f(inp["g_out_a"]), "conv_w": f(inp["conv_w"]),
        "a_log": f(inp["a_log"]).reshape(DEPTH, 8), "dt_bias": f(inp["dt_bias"]).reshape(DEPTH, 8),
        "g_onorm_b": f(inp["g_onorm_b"]), "rpb": f(inp["rpb"]), "g_out_c": f(inp["g_out_c"]),
        "w_out": f(inp["w_out"]), "w_gu": f(inp["w_gu"]), "w_down": f(inp["w_down"]),
        "cpack": cpack, "ropet": ropet, "qmask": qmask, "rsel": rsel, "ohc": ohc, "neghd": neghd,
    }
    maps = []
    for c in range(8):
        b = c // 4
        m = dict(shared)
        m["xp"] = f(inp["x_prompt"][NPS * c:NPS * (c + 1)]).reshape(NPS * TP, D)
        m["xs"] = f(inp["x_sample"][b])
        m["cak"] = f(inp["cache_a_k"][b]).reshape(DEPTH, 256, 128)
        m["cav"] = f(inp["cache_a_v"][b]).reshape(DEPTH, 256, 128)
        m["sb0"] = f(inp["state_b"][b])
        m["cck"] = f(inp["cache_c_k"][b]).reshape(DEPTH, 256, 384)
        m["ccv"] = f(inp["cache_c_v"][b]).reshape(DEPTH, 256, 384)
        m["cvec"] = np.stack([f(inp["c_ctx"]), f(inp["c"][b])], 0)
        maps.append(m)
    return maps


def kernel(**inputs):
    nc, kb = get_nc()
    maps = make_in_maps(inputs)
    res = run_bass_kernel_spmd(nc, maps, core_ids=list(range(8)))
    R = res.results
    yp = np.concatenate([R[c]["yp"].reshape(NPS, TP, D) for c in range(8)], 0)
    ys = np.stack([R[0]["ys"], R[4]["ys"]], 0)
    ka = np.concatenate([R[c]["oka"].reshape(NPS, DEPTH, TP, 2, 64) for c in range(8)], 0)
    va = np.concatenate([R[c]["ova"].reshape(NPS, DEPTH, TP, 2, 64) for c in range(8)], 0)
    sb = np.concatenate([R[c]["osb"] for c in range(8)], 0)
    kc = np.concatenate([R[c]["okc"].reshape(NPS, DEPTH, TP, 6, 64) for c in range(8)], 0)
    vc = np.concatenate([R[c]["ovc"].reshape(NPS, DEPTH, TP, 6, 64) for c in range(8)], 0)
    return (yp.astype(np.float32), ys.astype(np.float32), ka.astype(np.float32), va.astype(np.float32),
            sb.astype(np.float32), kc.astype(np.float32), vc.astype(np.float32))
```

```python
import contextlib
import math
import os
import numpy as np
import concourse.bass as bass
import concourse.mybir as mybir
from concourse.bass_utils import run_bass_kernel_spmd

F32, BF16 = mybir.dt.float32, mybir.dt.bfloat16
AF = mybir.ActivationFunctionType
ALU = mybir.AluOpType
AX = mybir.AxisListType

D = 1024
DEPTH = 4
NPS = 4
TP = 256
TS = 2048
DFF = 2816
INW = 2832
EPS = 1e-6
C_AQ, C_AK, C_AV, C_BQKV, C_BG, C_BBETA, C_BALPHA, C_CQ, C_CK, C_CV = (
    0, 384, 512, 640, 1408, 1664, 1672, 1680, 2064, 2448)
NEG = -30000.0
_DBG = {}
POOLENG = _DBG.get('POOLENG', 'pool')

ENGS = ("pe", "act", "dve", "pool", "sp")


class Op:
    __slots__ = ("eng", "fn", "deps", "signal", "sem", "val", "dma", "epoch", "phase", "iname")

    def __init__(self, eng, fn, dma, epoch):
        self.eng = eng
        self.fn = fn
        self.deps = []
        self.signal = False
        self.sem = None
        self.val = 0
        self.dma = dma
        self.epoch = epoch
        self.phase = None
        self.iname = None


class Sched:
    def __init__(self, n_dma_sems=24, same_eng_sync=True):
        self.ops = {e: [] for e in ENGS}
        self.lastw = {}
        self.readers = {}
        self.n_dma_sems = n_dma_sems
        self.dma_rr = {e: 0 for e in ENGS}
        self.dma_last = {e: [None] * n_dma_sems for e in ENGS}
        self.epoch = 0
        self.final_ops = []
        self.same = same_eng_sync
        self.bar = {e: [] for e in ENGS}
        self.seq = []
        self.phase = "init"

    def _dep(self, op, other):
        if other is None or other is op:
            return
        if not other.dma and not op.dma and other.eng == op.eng:
            if op.eng == "pe" or not self.same:
                return
        op.deps.append(other)

    def barrier(self):
        lasts = [self.ops[e][-1] for e in ENGS if self.ops[e]]
        for e in ENGS:
            lasts += [d for d in self.dma_last[e] if d is not None]
        for e in ENGS:
            self.bar[e] = list(lasts)

    def add(self, eng, fn, reads=(), writes=(), dma=False):
        op = Op(eng, fn, dma, self.epoch)
        op.phase = self.phase
        if self.bar[eng]:
            for o in self.bar[eng]:
                if o.dma or dma or o.eng != eng:
                    op.deps.append(o)
            self.bar[eng] = []
        for r in reads:
            self._dep(op, self.lastw.get(r))
            if r[:2] in ("PF", "PT"):
                for rd in self.readers.get(r, ()):
                    if rd.eng != eng:
                        op.deps.append(rd)
        for w in writes:
            self._dep(op, self.lastw.get(w))
            for rd in self.readers.get(w, ()):
                self._dep(op, rd)
        for r in reads:
            self.readers.setdefault(r, []).append(op)
        for w in writes:
            self.lastw[w] = op
            self.readers[w] = []
        if dma:
            k = self.dma_rr[eng]
            self.dma_rr[eng] = (k + 1) % self.n_dma_sems
            prev = self.dma_last[eng][k]
            if prev is not None:
                op.deps.append(prev)
            self.dma_last[eng][k] = op
            op.sem = ("dma" + eng, k)
            op.signal = True
        self.ops[eng].append(op)
        self.seq.append(op)
        return op

    def emit(self, nc, stack):
        for e in ENGS:
            for op in self.ops[e]:
                for d in op.deps:
                    d.signal = True
        for op in self.final_ops:
            op.signal = True
        sems = {}
        counts = {}
        for op in self.seq:
            if not op.signal:
                continue
            if op.dma:
                key = op.sem
                counts[key] = counts.get(key, 0) + 16
            else:
                key = (op.eng, op.epoch)
                counts[key] = counts.get(key, 0) + 1
            op.sem = key
            op.val = counts[key]
            if key not in sems:
                sems[key] = stack.enter_context(nc.semaphore("s_%s_%s" % key))
        self.maxval = max(counts.values()) if counts else 0
        self.nsems = len(sems)
        block = stack.enter_context(nc.Block())
        engobj = {"pe": block.tensor, "act": block.scalar, "dve": block.vector,
                  "pool": block.gpsimd, "sp": block.sync}
        finals = list(self.final_ops)

        def make(e):
            ops = self.ops[e]

            def body(eng):
                waited = {}
                for op in ops:
                    need = {}
                    for d in op.deps:
                        if d.val > need.get(d.sem, 0):
                            need[d.sem] = d.val
                    for key, v in need.items():
                        if waited.get(key, 0) >= v:
                            continue
                        eng.wait_ge(sems[key], v)
                        waited[key] = v
                    ins = op.fn(eng)
                    try:
                        op.iname = ins.ins.name
                    except Exception:
                        pass
                    if op.signal:
                        ins.then_inc(sems[op.sem], 16 if op.dma else 1)
                if e == "sp":
                    for f in finals:
                        if waited.get(f.sem, 0) < f.val:
                            eng.wait_ge(sems[f.sem], f.val)
                            waited[f.sem] = f.val
            return body

        for e in ENGS:
            engobj[e](make(e))


class Arena:
    def __init__(self, ap, ncols):
        self.ap = ap
        self.n = ncols
        self.off = 0
        self.peak = 0

    def alloc(self, shape, dt=F32):
        n = int(np.prod(shape))
        cols = n if dt == F32 else (n + 1) // 2
        a = self.off
        self.off += cols
        self.peak = max(self.peak, self.off)
        assert self.off <= self.n, "arena overflow %d > %d" % (self.off, self.n)
        v = self.ap[:, a:a + cols]
        if dt != F32:
            v = v.bitcast(dt)
            if 2 * cols != n:
                v = v[:, 0:n]
        if len(shape) > 1:
            names = " ".join("d%d" % i for i in range(len(shape)))
            kw = {"d%d" % i: shape[i] for i in range(len(shape) - 1)}
            v = v.rearrange("p (%s) -> p %s" % (names, names), **kw)
        return v

    def mark(self):
        return self.off

    def release(self, m):
        self.off = m


class Job:
    pass


class KB:
    def __init__(self, nl=DEPTH, jobs=("p", "s"), dbg=(), same=True, stop=None):
        self.stop = stop
        self.nl = nl
        self.jobs = jobs
        self.dbg = set(dbg)
        self.nc = bass.Bass("TRN2", target_bir_lowering=False)
        self.S = Sched(same_eng_sync=same)
        self.st = contextlib.ExitStack()
        self.outs = []
        self.uid = 0

    def din(self, name, shape, dt=F32):
        return self.nc.dram_tensor(name, list(shape), dt, kind="ExternalInput").ap()

    def dout(self, name, shape, dt=F32):
        self.outs.append(name)
        return self.nc.dram_tensor(name, list(shape), dt, kind="ExternalOutput").ap()

    def add(self, eng, fn, r=(), w=(), dma=False):
        return self.S.add(eng, fn, reads=r, writes=w, dma=dma)

    def dma(self, q, out, in_, r=(), w=(), slow=False):
        if slow:
            return self.add(q, lambda e: e.dma_start(out=out, in_=in_, allow_slow_non_contiguous=True), r, w, True)
        return self.add(q, lambda e: e.dma_start(out=out, in_=in_), r, w, True)

    def store(self, out, in_, r=()):
        op = self.dma("sp", out, in_, r=r)
        self.S.final_ops.append(op)
        return op

    def mm(self, out, lhsT, rhs, start, stop, r=(), w=()):
        return self.add("pe", lambda e: e.matmul(out, lhsT=lhsT, rhs=rhs, start=start, stop=stop), r, w)

    def tr(self, out, in_, r=(), w=()):
        ident = self.ident_b if in_.dtype == BF16 else self.ident_f
        n = in_.shape[0]
        idn = ident[0:n, 0:n]
        return self.add("pe", lambda e: e.transpose(out=out, in_=in_, identity=idn), r, w)

    def act(self, out, in_, func, r=(), w=(), scale=None, bias=None, accum=None):
        kw = {}
        if scale is not None:
            kw["scale"] = scale
        if bias is not None:
            kw["bias"] = bias
        if accum is not None:
            kw["accum_out"] = accum
        return self.add("act", lambda e: e.activation(out=out, in_=in_, func=func, **kw), r, w)

    def tt(self, eng, out, in0, in1, op, r=(), w=()):
        return self.add(eng, lambda e: e.tensor_tensor(out=out, in0=in0, in1=in1, op=op), r, w)

    def ts(self, eng, out, in0, s1, s2, op0, op1=None, r=(), w=()):
        if op1 is None:
            return self.add(eng, lambda e: e.tensor_scalar(out=out, in0=in0, scalar1=s1, scalar2=None, op0=op0), r, w)
        return self.add(eng, lambda e: e.tensor_scalar(out=out, in0=in0, scalar1=s1, scalar2=s2, op0=op0, op1=op1), r, w)

    def stt(self, eng, out, in0, scalar, in1, op0, op1, r=(), w=()):
        return self.add(eng, lambda e: e.scalar_tensor_tensor(out=out, in0=in0, scalar=scalar, in1=in1, op0=op0, op1=op1), r, w)

    def cp(self, eng, out, in_, r=(), w=()):
        if eng == "act":
            return self.add(eng, lambda e: e.activation(out=out, in_=in_, func=AF.Copy), r, w)
        return self.add(eng, lambda e: e.tensor_copy(out=out, in_=in_), r, w)

    def red(self, eng, out, in_, r=(), w=()):
        return self.add(eng, lambda e: e.tensor_reduce(out=out, in_=in_, axis=AX.X, op=ALU.add), r, w)

    def memset(self, eng, out, val, w=()):
        return self.add(eng, lambda e: e.memset(out, val), (), w)

    def rstd(self, out, ssum, inv_n, r=(), w=()):
        self.act(out, ssum, AF.Ln, r=r, w=w, scale=inv_n, bias=EPS)
        self.act(out, out, AF.Exp, r=w, w=w, scale=-0.5)

    def tap(self, name, ap, r):
        if name not in self.dbg:
            return
        shape = list(ap.shape)
        d = self.dout("dbg_" + name, shape, ap.dtype)
        self.store(d, ap, r=r)

    def build(self):
        nc = self.nc
        nl = self.nl
        st = self.st
        self.xp = self.din("xp", [NPS * TP, D])
        self.xs = self.din("xs", [TS, D])
        self.cak = self.din("cak", [DEPTH, 256, 128])
        self.cav = self.din("cav", [DEPTH, 256, 128])
        self.sb0 = self.din("sb0", [DEPTH, 2, 4, 64, 64])
        self.cck = self.din("cck", [DEPTH, 256, 384])
        self.ccv = self.din("ccv", [DEPTH, 256, 384])
        self.cvec = self.din("cvec", [2, D])
        self.w_mod = self.din("w_mod", [DEPTH, D, 6 * D])
        self.b_mod = self.din("b_mod", [DEPTH, 6 * D])
        self.g_norm = self.din("g_norm", [DEPTH, 4, D])
        self.w_in = self.din("w_in", [DEPTH, D, INW])
        self.g_qk_a = self.din("g_qk_a", [DEPTH, 2, 64])
        self.g_out_a = self.din("g_out_a", [DEPTH, 384])
        self.conv_w = self.din("conv_w", [DEPTH, 3, 768])
        self.a_log = self.din("a_log", [DEPTH, 8])
        self.dt_bias = self.din("dt_bias", [DEPTH, 8])
        self.g_onorm_b = self.din("g_onorm_b", [DEPTH, 64])
        self.rpb = self.din("rpb", [DEPTH, 6, 15, 31])
        self.g_out_c = self.din("g_out_c", [DEPTH, 384])
        self.w_out = self.din("w_out", [DEPTH, D, D])
        self.w_gu = self.din("w_gu", [DEPTH, D, 2 * DFF])
        self.w_down = self.din("w_down", [DEPTH, DFF, D])
        self.cpack = self.din("cpack", [128, 7 * 128])
        self.ropet = self.din("ropet", [128, 16 * 64])
        self.qmask = self.din("qmask", [128, 14 * 128])
        self.rsel = self.din("rsel", [33, 102])
        self.ohc = self.din("ohc", [33, 4096])
        self.neghd = self.din("neghd", [128, 128])
        self.tabscr = self.nc.dram_tensor("tabscr", [102, 4096], BF16, kind="Internal").ap()
        self.yp = self.dout("yp", [NPS * TP, D])
        self.ys = self.dout("ys", [TS, D])
        self.oka = self.dout("oka", [NPS, DEPTH, TP, 128])
        self.ova = self.dout("ova", [NPS, DEPTH, TP, 128])
        self.osb = self.dout("osb", [NPS, DEPTH, 2, 4, 64, 64])
        self.okc = self.dout("okc", [NPS, DEPTH, TP, 384])
        self.ovc = self.dout("ovc", [NPS, DEPTH, TP, 384])

        NCOL = 53200
        arena_t = st.enter_context(nc.sbuf_tensor("arena", [128, NCOL], F32))
        self.A = Arena(arena_t[:], NCOL)
        A = self.A
        self.PF = [st.enter_context(nc.psum_tensor("pf%d" % i, [128, 512], F32))[:] for i in range(6)]
        self.PT = [st.enter_context(nc.psum_tensor("pt%d" % i, [128, 1024], BF16))[:] for i in range(2)]

        self.cF = A.alloc([7, 128])
        self.ident_f = self.cF[:, 0, :]
        self.tri = [self.cF[:, 1, :], self.cF[:, 2, :]]
        self.maft = [self.cF[:, 3, :], self.cF[:, 4, :]]
        self.ones_f = self.cF[:, 5, :]
        self.cB = A.alloc([3, 128], BF16)
        self.ident_b = self.cB[:, 0, :]
        self.ones_b = self.cB[:, 1, :]
        self.bd_b = self.cB[:, 2, :]
        self.rope = A.alloc([16, 64])
        self.negh = A.alloc([2, 64], BF16)
        self.dma("pool", self.negh, self.neghd.rearrange("p (a b) -> p a b", a=2), w=["negh"])
        self.dma("sp", self.cF, self.cpack.rearrange("p (a b) -> p a b", a=7), w=["cF"])
        self.dma("sp", self.rope, self.ropet.rearrange("p (a b) -> p a b", a=16), w=["rope"])
        self.cp("dve", self.cB[:, 0, :], self.cF[:, 0, :], r=["cF"], w=["cB"])
        self.cp("dve", self.cB[:, 1, :], self.cF[:, 5, :], r=["cF"], w=["cB"])
        self.cp("dve", self.cB[:, 2, :], self.cF[:, 6, :], r=["cF"], w=["cB"])
        self.CK = ["cF", "cB"]

        self.X = A.alloc([16, D])
        self.modraw = A.alloc([4, 8])
        self.modA = A.alloc([2, 8])
        self.gfm = A.alloc([2, 8])
        self.G1 = A.alloc([D])
        self.G2 = A.alloc([D])
        self.cfm = A.alloc([8])
        self.srep = A.alloc([8, 128], BF16)
        self.small = A.alloc([64])
        self.base_mark = A.mark()

        for jn in self.jobs:
            J = Job()
            J.name = jn
            if jn == "p":
                J.nt, J.T, J.nseq, J.ci, J.latent = 8, TP, NPS, 0, False
                J.xin, J.yout = self.xp, self.yp
            else:
                J.nt, J.T, J.nseq, J.ci, J.latent = 16, TS, 1, 1, True
                J.xin, J.yout = self.xs, self.ys
            self.run_job(J)

        self.S.emit(nc, st)
        return nc

    def run_job(self, J):
        S = self.S
        A = self.A
        S.barrier()
        xin = J.xin.rearrange("(t p) d -> p t d", p=128)
        for t in range(J.nt):
            self.dma("sp", self.X[:, t, :], xin[:, t, :], w=["X%d" % t])
        self.dma("sp", self.cfm, self.cvec[J.ci:J.ci + 1, :].rearrange("o (k p) -> p (o k)", p=128),
                 w=["cfm"], slow=True)
        sil = self.small[:, 0:8]
        self.act(sil, self.cfm, AF.Silu, r=["cfm"], w=["small"])
        self.cp("dve", self.srep, sil.unsqueeze(2).to_broadcast([128, 8, 128]), r=["small"], w=["srep"])
        for l in range(self.nl):
            S.epoch = (J.name, l)
            if self.stop == "load":
                break
            self.mod(l, J)
            if self.stop == "mod":
                break
            self.phase_b(l, J)
            if self.stop == "b":
                break
            self.phase_kvq(l, J)
            if self.stop == "kvq":
                break
            self.phase_f(l, J)
            self.tap("x2_%s%d" % (J.name, l), self.X[:, 0:J.nt, :], r=["X%d" % t for t in range(J.nt)])
        yout = J.yout.rearrange("(t p) d -> p t d", p=128)
        for t in range(J.nt):
            self.store(yout[:, t, :], self.X[:, t, :], r=["X%d" % t])

    def mod(self, l, J):
        self.S.phase = "mod"
        PF = self.PF
        self.S.barrier()
        self.A.release(self.base_mark)
        self.rowbuf = self.A.alloc([512])
        self.bbc = [self.A.alloc([512]) for _ in range(2)]
        self.wm = [self.A.alloc([8, 512], BF16) for _ in range(4)]
        wmv = self.w_mod[l].rearrange("(k p) n -> p k n", p=128)
        gn = self.g_norm[l]
        self.dma("sp", self.G1, gn[1:2, :].partition_broadcast(128), w=["G1"])
        self.dma("sp", self.G2, gn[3:4, :].partition_broadcast(128), w=["G2"])
        self.dma("sp", self.gfm[:, 0, :], gn[0:1, :].rearrange("o (k p) -> p (o k)", p=128), w=["gfm"], slow=True)
        self.dma("sp", self.gfm[:, 1, :], gn[2:3, :].rearrange("o (k p) -> p (o k)", p=128), w=["gfm"], slow=True)
        rawidx = {0: 0, 1: 1, 3: 2, 4: 3}
        for piece in range(12):
            s = piece % 2
            v, half = piece // 2, piece % 2
            ws = piece % 4
            self.dma("pool", self.wm[ws], wmv[:, :, piece * 512:(piece + 1) * 512], w=["wm%d" % ws])
            self.dma("sp", self.bbc[s], self.b_mod[l:l + 1, piece * 512:(piece + 1) * 512].partition_broadcast(128),
                     w=["bbc%d" % s])
            ps = PF[s]
            for k in range(8):
                self.mm(ps, self.srep[:, k, :], self.wm[ws][:, k, :], k == 0, k == 7,
                        r=["srep", "wm%d" % ws], w=["PF%d" % s])
            if v in (2, 5):
                G = self.G1 if v == 2 else self.G2
                gk = "G1" if v == 2 else "G2"
                gs = G[:, half * 512:(half + 1) * 512]
                self.tt("dve", self.bbc[s], ps, self.bbc[s], ALU.add, r=["PF%d" % s, "bbc%d" % s], w=["bbc%d" % s])
                self.tt("pool", gs, gs, self.bbc[s], ALU.mult, r=["bbc%d" % s, gk], w=[gk])
            elif _DBG.get("MODTEST") == "1":
                self.cp("dve", self.rowbuf[0:1, :], ps[0:1, :], r=["PF%d" % s], w=["rowbuf"])
            else:
                self.tt("dve", self.rowbuf[0:1, :], ps[0:1, :], self.bbc[s][0:1, :], ALU.add,
                        r=["PF%d" % s, "bbc%d" % s], w=["rowbuf"])
                pm = PF[2][:, 0:4]
                for j in range(4):
                    self.mm(pm[:, j:j + 1], self.rowbuf[0:1, j * 128:(j + 1) * 128], self.ones_f[0:1, 0:1],
                            True, True, r=["rowbuf", "cF"], w=["PF2"])
                self.cp("dve", self.modraw[:, rawidx[v], half * 4:(half + 1) * 4], pm, r=["PF2"], w=["modraw"])
        self.stt("dve", self.modA[:, 0, :], self.modraw[:, 1, :], 1.0, self.gfm[:, 0, :], ALU.add, ALU.mult,
                 r=["modraw", "gfm"], w=["modA"])
        self.stt("dve", self.modA[:, 1, :], self.modraw[:, 3, :], 1.0, self.gfm[:, 1, :], ALU.add, ALU.mult,
                 r=["modraw", "gfm"], w=["modA"])
        self.tap("G1_%s%d" % (J.name, l), self.G1, r=["G1"])
        self.tap("modraw_%s%d" % (J.name, l), self.modraw, r=["modraw"])

    def make_hT(self, J, tiles, which, dst, dkey, xn, tmpf):
        PT0 = self.PT[0]
        Avec = self.modA[:, which, :]
        shv = self.modraw[:, 0 if which == 0 else 2, :]
        for j, t in enumerate(tiles):
            xk = "X%d" % t
            ss = self.small[:, 8:9]
            rs = self.small[:, 9:10]
            self.act(xn, self.X[:, t, :], AF.Square, r=[xk], w=["xn", "small"], accum=ss)
            self.rstd(rs, ss, 1.0 / D, r=["small"], w=["small"])
            self.act(xn, self.X[:, t, :], AF.Copy, r=[xk, "small"], w=["xn"], scale=rs)
            for k in range(8):
                self.tr(PT0[:, k * 128:(k + 1) * 128], xn[:, k * 128:(k + 1) * 128], r=["xn", "cB"], w=["PT0"])
            pv = PT0.rearrange("p (k c) -> p k c", k=8)
            tv = tmpf.rearrange("p (k c) -> p k c", k=8)
            self.tt("dve", tv, pv, Avec.unsqueeze(2).to_broadcast([128, 8, 128]), ALU.mult,
                    r=["PT0", "modA"], w=["tmpf"])
            self.tt("pool", dst[:, :, j * 128:(j + 1) * 128], tv, shv.unsqueeze(2).to_broadcast([128, 8, 128]),
                    ALU.add, r=["tmpf", "modraw"], w=[dkey])

    def hT_pre(self, t, xn, xnk, sc0):
        xk = "X%d" % t
        ss = self.small[:, sc0:sc0 + 1]
        rs = self.small[:, sc0 + 1:sc0 + 2]
        self.act(xn, self.X[:, t, :], AF.Square, r=[xk], w=[xnk, "small"], accum=ss)
        self.rstd(rs, ss, 1.0 / D, r=["small"], w=["small"])
        self.act(xn, self.X[:, t, :], AF.Copy, r=[xk, "small"], w=[xnk], scale=rs)

    def hT_post(self, which, dst, dkey, j, xn, xnk, tmpf):
        PT0 = self.PT[0]
        Avec = self.modA[:, which, :]
        shv = self.modraw[:, 0 if which == 0 else 2, :]
        for k in range(8):
            self.tr(PT0[:, k * 128:(k + 1) * 128], xn[:, k * 128:(k + 1) * 128], r=[xnk, "cB"], w=["PT0"])
        for k in range(8):
            self.act(dst[:, k, j * 128:(j + 1) * 128], PT0[:, k * 128:(k + 1) * 128], AF.Identity,
                     r=["PT0", "modA", "modraw"], w=[dkey], scale=Avec[:, k:k + 1], bias=shv[:, k:k + 1])

    def load_w(self, dst, src2d, c0, c1, key, kchunks=8):
        v = src2d.rearrange("(k p) n -> p k n", p=128)
        return self.dma("pool", dst, v[:, :, c0:c1], w=[key])

    def phase_b(self, l, J):
        S = self.S
        A = self.A
        PF, PT = self.PF, self.PT
        S.barrier()
        A.release(self.base_mark)
        nt, T = J.nt, J.T
        ntq = T // 128
        self.YB = A.alloc([nt, 256], BF16)
        self.yb_mark = A.mark()
        WB = A.alloc([8, 1040], BF16)
        self.qm = A.alloc([14, 128], BF16)
        self.dma("pool", self.qm, self.qmask.rearrange("p (a b) -> p a b", a=14), w=["qm"])
        BT = A.alloc([6, T], BF16)
        GT = A.alloc([ntq, 256], BF16)
        OB = A.alloc([ntq, 256])
        bet = A.alloc([ntq, 8])
        nbet = A.alloc([ntq, 8])
        gl = A.alloc([ntq, 8])
        if J.nseq == 1:
            ctmp = WB.rearrange("p k c -> p (k c)").bitcast(F32)[:, 0:T]
        else:
            ctmp = A.alloc([T])
        hT0 = A.alloc([8, 128], BF16)
        xn = A.alloc([D], BF16)
        tmpf = A.alloc([D])
        cw = A.alloc([3, 6])
        dtb = A.alloc([8])
        nal = A.alloc([8])
        gon = A.alloc([64])
        if J.nseq == 1:
            wbf = WB.rearrange("p k c -> p (k c)").bitcast(F32)
            sq = wbf[:, 2048:2304].bitcast(BF16)
            rsb = wbf[:, 2304:2816]
        else:
            sq = A.alloc([512], BF16)
            rsb = A.alloc([512])
        def two(shape, dt=F32):
            return [A.alloc(shape, dt) for _ in range(2)]

        def one(shape, dt=F32):
            a = A.alloc(shape, dt)
            return [a, a]
        KVt = two([512], BF16)
        Gm = one([4, 128])
        dec = two([4, 128])
        decT = two([4, 128])
        if J.nseq == 1:
            hT1b = dec[0].rearrange("p h j -> p (h j)").bitcast(BF16).rearrange("p (k c) -> p k c", k=8)
            xn2 = dec[1].rearrange("p h j -> p (h j)").bitcast(BF16)
        else:
            hT1b = A.alloc([8, 128], BF16)
            xn2 = A.alloc([D], BF16)
        hT = [hT0, hT1b]
        xnb = [xn, xn2]
        Nb = two([4, 128], BF16)
        NTb = two([4, 128], BF16)
        Xb = two([4, 128], BF16)
        Yb = two([4, 128], BF16)
        Zb = two([4, 128], BF16)
        Zc2 = two([4, 128], BF16)
        Pm = one([4, 128], BF16)
        Qm = one([4, 128], BF16)
        ATb = two([4, 128], BF16)
        vb = two([4, 64], BF16)
        kbg = two([4, 64], BF16)
        ktil = two([4, 64], BF16)
        U = two([4, 64])
        WT = two([2, 128], BF16)
        vnew = two([4, 64], BF16)
        tmpo = two([4, 64])
        gsm = two([32])
        Sf = A.alloc([2, 64])
        Sb = A.alloc([2, 64], BF16)

        self.load_w(WB, self.w_in[l], C_BQKV, C_BQKV + 1040, "WB")
        for jc in range(3):
            self.dma("sp", cw[:, jc, :], self.conv_w[l, jc:jc + 1, :].rearrange("o (b p) -> p (o b)", p=128),
                     w=["cw"], slow=True)
        self.dma("sp", dtb, self.dt_bias[l:l + 1, :].partition_broadcast(128), w=["dtb"])
        self.dma("sp", nal, self.a_log[l:l + 1, :].partition_broadcast(128), w=["nal"])
        self.dma("sp", gon, self.g_onorm_b[l:l + 1, :].partition_broadcast(128), w=["gon"])
        self.ts("dve", gon, gon, 0.125, None, ALU.mult, r=["gon"], w=["gon"])
        self.act(nal, nal, AF.Exp, r=["nal"], w=["nal"])
        self.ts("dve", nal, nal, -1.0, None, ALU.mult, r=["nal"], w=["nal"])

        for s in range(J.nseq):
            t0 = s * ntq
            self.S.phase = "b1_proj"
            self.hT_pre(t0, xnb[0], "xnb0", 48)
            self.hT_post(0, hT[0], "hTb0", 0, xnb[0], "xnb0", tmpf)
            for j in range(ntq):
                t = t0 + j
                h = hT[j % 2]
                hk = "hTb%d" % (j % 2)
                q2 = (j + 1) % 2
                if j + 1 < ntq:
                    self.hT_pre(t + 1, xnb[q2], "xnb%d" % q2, 48 + 2 * q2)
                for b in range(6):
                    if b == 3 and j + 1 < ntq:
                        self.hT_post(0, hT[q2], "hTb%d" % q2, 0, xnb[q2], "xnb%d" % q2, tmpf)
                    ps = PF[b % 2][:, 0:128]
                    pk = "PF%d" % (b % 2)
                    for k in range(8):
                        self.mm(ps, WB[:, k, b * 128:(b + 1) * 128], h[:, k, :], k == 0, k == 7,
                                r=["WB", hk], w=[pk])
                    self.cp("act", BT[:, b, j * 128:(j + 1) * 128], ps, r=[pk], w=["BT"])
                pg = PF[2][:, 0:272]
                for k in range(8):
                    self.mm(pg, h[:, k, :], WB[:, k, 768:1040], k == 0, k == 7, r=["WB", hk], w=["PF2"])
                e1 = tmpf[:, 0:256]
                self.act(e1, pg[:, 0:256], AF.Exp, r=["PF2"], w=["tmpf"], scale=-1.0)
                self.ts("dve", e1, e1, 1.0, None, ALU.add, r=["tmpf"], w=["tmpf"])
                self.add("dve", lambda e, a=e1: e.reciprocal(out=a, in_=a), ["tmpf"], ["tmpf"])
                self.tt("dve", GT[:, j, :], e1, pg[:, 0:256], ALU.mult, r=["tmpf", "PF2"], w=["GT"])
                e2 = tmpf[:, 256:264]
                self.act(e2, pg[:, 256:264], AF.Exp, r=["PF2"], w=["tmpf"], scale=-1.0)
                self.ts("dve", e2, e2, 1.0, None, ALU.add, r=["tmpf"], w=["tmpf"])
                self.add("dve", lambda e, a=e2, o=bet[:, j, :]: e.reciprocal(out=o, in_=a), ["tmpf"], ["bet"])
                self.ts("dve", nbet[:, j, :], bet[:, j, :], -1.0, None, ALU.mult, r=["bet"], w=["nbet"])
                e3 = tmpf[:, 264:272]
                self.tt("dve", e3, pg[:, 264:272], dtb, ALU.add, r=["PF2", "dtb"], w=["tmpf"])
                self.act(e3, e3, AF.Exp, r=["tmpf"], w=["tmpf"])
                self.act(e3, e3, AF.Ln, r=["tmpf"], w=["tmpf"], bias=1.0)
                self.tt("dve", gl[:, j, :], e3, nal, ALU.mult, r=["tmpf", "nal"], w=["gl"])
            if _DBG.get("BSTOP") == "1":
                continue
            if J.nseq == 1:
                S.barrier()
            self.S.phase = "b2_conv"
            for b in range(6):
                src = BT[:, b, :]
                self.ts("dve", ctmp, src, cw[:, 1, b:b + 1], None, ALU.mult, r=["BT", "cw"], w=["ctmp"])
                self.stt("dve", ctmp[:, 1:T], src[:, 0:T - 1], cw[:, 0, b:b + 1], ctmp[:, 1:T], ALU.mult, ALU.add,
                         r=["BT", "cw", "ctmp"], w=["ctmp"])
                self.stt("dve", ctmp[:, 0:T - 1], src[:, 1:T], cw[:, 2, b:b + 1], ctmp[:, 0:T - 1], ALU.mult, ALU.add,
                         r=["BT", "cw", "ctmp"], w=["ctmp"])
                CW = min(512, T)
                for c0 in range(0, T, CW):
                    cs = slice(c0, c0 + CW)
                    ex = rsb[:, 0:CW]
                    self.act(ex, ctmp[:, cs], AF.Exp, r=["ctmp"], w=["rsb"], scale=-1.0)
                    self.ts("dve", ex, ex, 1.0, None, ALU.add, r=["rsb"], w=["rsb"])
                    self.add("dve", lambda e, a=ex: e.reciprocal(out=a, in_=a), ["rsb"], ["rsb"])
                    if b >= 4:
                        self.tt("pool", BT[:, b, cs], ctmp[:, cs], ex, ALU.mult, r=["ctmp", "rsb"], w=["BT"])
                    else:
                        self.tt("pool", ctmp[:, cs], ctmp[:, cs], ex, ALU.mult, r=["ctmp", "rsb"], w=["ctmp"])
                        self.tt("pool", sq[:, 0:CW], ctmp[:, cs], ctmp[:, cs], ALU.mult, r=["ctmp"], w=["sq"])
                        pn = PF[3][:, 0:CW]
                        self.mm(pn, self.bd_b, sq[:, 0:CW], True, True, r=["sq", "cB"], w=["PF3"])
                        self.act(ex, pn, AF.Ln, r=["PF3"], w=["rsb"], bias=EPS)
                        self.act(ex, ex, AF.Exp, r=["rsb"], w=["rsb"], scale=-0.5)
                        self.tt("dve", BT[:, b, cs], ctmp[:, cs], ex, ALU.mult, r=["ctmp", "rsb"], w=["BT"])
            if "bq" in self.dbg and s == 0:
                self.tap("bqkv", BT[:, :, 0:256], r=["BT"])
                self.tap("bgl", gl[:, 0:2, :], r=["gl"])
                self.tap("bbeta", bet[:, 0:2, :], r=["bet"])
            if _DBG.get("BSTOP") == "2":
                continue
            self.S.phase = "b3_chunks"
            for r in range(2):
                if J.latent:
                    self.dma("sp", Sf, self.sb0[l, r].rearrange("(j i) k v -> (i k) j v", i=2), w=["Sf"])
                else:
                    self.memset("pool", Sf, 0.0, w=["Sf"])
                self.cp("act", Sb, Sf, r=["Sf"], w=["Sb"])
                order = range(ntq) if r == 0 else range(ntq - 1, -1, -1)
                def chunk_gen(ci, c, p):
                    ba, bak = (PF[1], "PF1") if p == 0 else (PF[3], "PF3")
                    bb, bbk = (PF[2], "PF2") if p == 0 else (PF[4], "PF4")
                    sfx = "_%d" % p
                    cols = slice(c * 128, (c + 1) * 128)
                    for i, b in enumerate((2, 3, 4, 5)):
                        self.tr(PT[0][:, i * 128:(i + 1) * 128], BT[:, b, cols], r=["BT", "cB"], w=["PT0"])
                    self.cp("act", KVt[p], PT[0][:, 0:512], r=["PT0"], w=["KVt" + sfx])
                    ktok = KVt[p][:, 0:256].rearrange("p (h d) -> p h d", h=4)
                    vtok = KVt[p][:, 256:512].rearrange("p (h d) -> p h d", h=4)
                    g4 = gl[:, c, r * 4:(r + 1) * 4]
                    b4 = bet[:, c, r * 4:(r + 1) * 4]
                    nb4 = nbet[:, c, r * 4:(r + 1) * 4]
                    pg = PF[0][:, 0:8]
                    self.mm(pg[:, 0:4], self.tri[r], g4, True, True, r=["gl", "cF"], w=["PF0"])
                    self.mm(pg[:, 4:8], self.ones_f, g4, True, True, r=["gl", "cF"], w=["PF0"])
                    gs = gsm[p]
                    gk = "gsm" + sfx
                    gc, egc, eglast, dgl, ekt, bg = (gs[:, 0:4], gs[:, 4:8], gs[:, 8:12], gs[:, 12:16],
                                                     gs[:, 16:20], gs[:, 20:24])
                    self.cp("dve", gc, pg[:, 0:4], r=["PF0"], w=[gk])
                    self.act(egc, pg[:, 0:4], AF.Exp, r=["PF0"], w=[gk])
                    self.act(eglast, pg[:, 4:8], AF.Exp, r=["PF0"], w=[gk])
                    self.tt("dve", dgl, pg[:, 4:8], gc, ALU.subtract, r=["PF0", gk], w=[gk])
                    self.act(ekt, dgl, AF.Exp, r=[gk], w=[gk])
                    self.tt("dve", bg, b4, egc, ALU.mult, r=["bet", gk], w=[gk])
                    yield None
                    self.tt("dve", Gm[p], self.maft[r].unsqueeze(1).to_broadcast([128, 4, 128]),
                            g4.unsqueeze(2).to_broadcast([128, 4, 128]), ALU.mult, r=["cF", "gl"], w=["Gm"])
                    self.mm(ba, self.tri[r], Gm[p].rearrange("p h j -> p (h j)"), True, True,
                            r=["Gm", "cF"], w=[bak])
                    for h in range(4):
                        self.mm(bb[:, h * 128:(h + 1) * 128], Gm[p][:, h, :], self.tri[r], True, True,
                                r=["Gm", "cF"], w=[bbk])
                    d2 = dec[p].rearrange("p h j -> p (h j)")
                    dT2 = decT[p].rearrange("p h j -> p (h j)")
                    self.act(d2, ba, AF.Exp, r=[bak], w=["dec" + sfx])
                    self.act(dT2, bb, AF.Exp, r=[bbk], w=["decT" + sfx])
                    yield None
                    for h in range(4):
                        rows = slice((h % 2) * 64, (h % 2) * 64 + 64)
                        kT_h = BT[rows, 2 + h // 2, cols]
                        qT_h = BT[rows, h // 2, cols]
                        bank, bk = (ba, bak) if h % 2 == 0 else (bb, bbk)
                        self.mm(bank[:, (h // 2) * 128:(h // 2) * 128 + 128], kT_h, kT_h, True, True, r=["BT"], w=[bk])
                        self.mm(bank[:, 256 + (h // 2) * 128:256 + (h // 2) * 128 + 128], kT_h, qT_h, True, True,
                                r=["BT"], w=[bk])
                    yield None
                    mstrict = self.maft[r].unsqueeze(1).to_broadcast([128, 4, 128])
                    minclT = self.tri[r].unsqueeze(1).to_broadcast([128, 4, 128])
                    self.tt(POOLENG, dec[p], dec[p], mstrict, ALU.mult, r=["dec" + sfx, "cF"], w=["dec" + sfx])
                    self.tt(POOLENG, dec[p], dec[p], nb4.unsqueeze(2).to_broadcast([128, 4, 128]), ALU.mult,
                            r=["dec" + sfx, "nbet"], w=["dec" + sfx])
                    for par, (bank, bk) in enumerate(((ba, bak), (bb, bbk))):
                        self.tt("dve", Nb[p][:, par::2, :], bank[:, 0:256].rearrange("p (h j) -> p h j", h=2),
                                dec[p][:, par::2, :], ALU.mult, r=[bk, "dec" + sfx], w=["Nb" + sfx])
                    self.tt(POOLENG, decT[p], decT[p], minclT, ALU.mult, r=["decT" + sfx, "cF"], w=["decT" + sfx])
                    for par, (bank, bk) in enumerate(((ba, bak), (bb, bbk))):
                        self.tt("dve", ATb[p][:, par::2, :], bank[:, 256:512].rearrange("p (h j) -> p h j", h=2),
                                decT[p][:, par::2, :], ALU.mult, r=[bk, "decT" + sfx], w=["ATb" + sfx])
                    yield None
                    for h in range(4):
                        self.tr(PT[1][:, h * 128:(h + 1) * 128], Nb[p][:, h, :], r=["Nb" + sfx, "cB"], w=["PT1"])
                    self.cp("act", NTb[p].rearrange("p h j -> p (h j)"), PT[1][:, 0:512], r=["PT1"], w=["NTb" + sfx])
                    QT_ = lambda lv: self.qm[:, (0 if r == 0 else 7) + lv, :].unsqueeze(1).to_broadcast([128, 4, 128])
                    QZ_ = lambda lv: self.qm[:, (7 if r == 0 else 0) + lv, :].unsqueeze(1).to_broadcast([128, 4, 128])
                    idb = self.ident_b.unsqueeze(1).to_broadcast([128, 4, 128])
                    Tc, Tn_, tck, tnk = Xb[p], Yb[p], "Xb" + sfx, "Yb" + sfx
                    Zc, Zn_, zck, znk = Zb[p], Zc2[p], "Zb" + sfx, "Zc2" + sfx
                    self.tt("pool", Pm[p], Nb[p], QT_(0), ALU.mult, r=["Nb" + sfx, "qm"], w=["Pm"])
                    self.tt("pool", Tc, Pm[p], idb, ALU.add, r=["Pm", "cB"], w=[tck])
                    self.tt("pool", Qm[p], NTb[p], QZ_(0), ALU.mult, r=["NTb" + sfx, "qm"], w=["Qm"])
                    self.tt("pool", Zc, Qm[p], idb, ALU.add, r=["Qm", "cB"], w=[zck])
                    yield None
                    for lv in range(1, 7):
                        for h in range(4):
                            self.mm(ba[:, h * 128:(h + 1) * 128], NTb[p][:, h, :], Tc[:, h, :], True, True,
                                    r=["NTb" + sfx, tck], w=[bak])
                        for h in range(4):
                            self.mm(bb[:, h * 128:(h + 1) * 128], Nb[p][:, h, :], Zc[:, h, :], True, True,
                                    r=["Nb" + sfx, zck], w=[bbk])
                        self.tt("dve", Pm[p], ba.rearrange("p (h j) -> p h j", h=4), QT_(lv), ALU.mult,
                                r=[bak, "qm"], w=["Pm"])
                        self.tt("dve", Qm[p], bb.rearrange("p (h j) -> p h j", h=4), QZ_(lv), ALU.mult,
                                r=[bbk, "qm"], w=["Qm"])
                        if lv < 6:
                            for h in range(4):
                                self.mm(ba[:, h * 128:(h + 1) * 128], Zc[:, h, :], Pm[p][:, h, :], True, True,
                                        r=[zck, "Pm"], w=[bak])
                        for h in range(4):
                            self.mm(bb[:, h * 128:(h + 1) * 128], Tc[:, h, :], Qm[p][:, h, :], True, True,
                                    r=[tck, "Qm"], w=[bbk])
                        if lv < 6:
                            self.tt("dve", Tn_, ba.rearrange("p (h j) -> p h j", h=4), Tc, ALU.add,
                                    r=[bak, tck], w=[tnk])
                        self.tt("dve", Zn_, bb.rearrange("p (h j) -> p h j", h=4), Zc, ALU.add,
                                r=[bbk, zck], w=[znk])
                        Tc, Tn_, tck, tnk = Tn_, Tc, tnk, tck
                        Zc, Zn_, zck, znk = Zn_, Zc, znk, zck
                        yield None
                    Zf, zfk = Zc, zck
                    yield None
                    self.tt("pool", vb[p], vtok, b4.unsqueeze(2).to_broadcast([128, 4, 64]), ALU.mult,
                            r=["KVt" + sfx, "bet"], w=["vb" + sfx])
                    self.tt("pool", kbg[p], ktok, bg.unsqueeze(2).to_broadcast([128, 4, 64]), ALU.mult,
                            r=["KVt" + sfx, gk], w=["kbg" + sfx])
                    self.tt("pool", ktil[p], ktok, ekt.unsqueeze(2).to_broadcast([128, 4, 64]), ALU.mult,
                            r=["KVt" + sfx, gk], w=["ktil" + sfx])
                    for h in range(4):
                        self.mm(PF[0][:, h * 64:(h + 1) * 64], Zf[:, h, :], vb[p][:, h, :], True, True,
                                r=[zfk, "vb" + sfx], w=["PF0"])
                    for h in range(4):
                        rows = slice((h % 2) * 64, (h % 2) * 64 + 64)
                        self.mm(PF[5][rows, (h // 2) * 128:(h // 2) * 128 + 128], kbg[p][:, h, :], Zf[:, h, :],
                                True, True, r=[zfk, "kbg" + sfx], w=["PF5"])
                    self.cp("act", U[p].rearrange("p h d -> p (h d)"), PF[0][:, 0:256], r=["PF0"], w=["U" + sfx])
                    self.cp("act", WT[p].rearrange("p h j -> p (h j)"), PF[5][:, 0:256], r=["PF5"], w=["WT" + sfx])
                    egl2 = gs[:, 24:26]
                    self.cp("dve", egl2[0:64, :], eglast[0:64, 0:4:2], r=[gk], w=[gk])
                    self.cp("dve", egl2[64:128, :], eglast[64:128, 1:4:2], r=[gk], w=[gk])
                    if s == 0 and r == 1 and ci == 0 and "chk" in self.dbg:
                        self.dbg |= {"c_dec", "c_N", "c_AT", "c_Z", "c_U", "c_WT", "c_gs", "c_ktil", "c_vb"}
                        self.tap("c_dec", dec[p], r=["dec" + sfx])
                        self.tap("c_AT", ATb[p], r=["ATb" + sfx])
                        self.tap("c_Z", Zf, r=[zfk])
                        self.tap("c_U", U[p], r=["U" + sfx])
                        self.tap("c_WT", WT[p], r=["WT" + sfx])
                        self.tap("c_gs", gs[:, 0:26], r=[gk])
                        self.tap("c_ktil", ktil[p], r=["ktil" + sfx])
                        self.tap("c_vb", vb[p], r=["vb" + sfx])
                    yield "REC"
                    for h in range(4):
                        rows = slice((h % 2) * 64, (h % 2) * 64 + 64)
                        bank, bk = (PF[4][:, 0:128], "PF4") if h % 2 == 0 else (PF[5][:, 256:384], "PF5")
                        self.mm(bank[:, (h // 2) * 64:(h // 2) * 64 + 64], WT[p][rows, h // 2, :], Sb[rows, h // 2, :],
                                True, True, r=["WT" + sfx, "Sb"], w=[bk])
                    for par, (bank, bk) in enumerate(((PF[4][:, 0:128], "PF4"), (PF[5][:, 256:384], "PF5"))):
                        self.tt("dve", vnew[p][:, par::2, :], U[p][:, par::2, :],
                                bank.rearrange("p (h d) -> p h d", h=2), ALU.subtract,
                                r=["U" + sfx, bk], w=["vnew" + sfx])
                    for h in range(4):
                        rows = slice((h % 2) * 64, (h % 2) * 64 + 64)
                        bank, bk = (PF[0][:, 0:128], "PF0") if h % 2 == 0 else (PF[1][:, 0:128], "PF1")
                        self.mm(bank[:, (h // 2) * 64:(h // 2) * 64 + 64], BT[rows, h // 2, cols], Sb[rows, h // 2, :],
                                True, True, r=["BT", "Sb"], w=[bk])
                    for h in range(4):
                        self.mm(PF[5][:, h * 64:(h + 1) * 64], ATb[p][:, h, :], vnew[p][:, h, :], True, True,
                                r=["ATb" + sfx, "vnew" + sfx], w=["PF5"])
                    for par, (bank, bk) in enumerate(((PF[0][:, 0:128], "PF0"), (PF[1][:, 0:128], "PF1"))):
                        self.tt("dve", tmpo[p][:, par::2, :], bank.rearrange("p (h d) -> p h d", h=2),
                                egc[:, par::2].unsqueeze(2).to_broadcast([128, 2, 64]), ALU.mult,
                                r=[bk, gk], w=["tmpo" + sfx])
                    ob = OB[:, c, :].rearrange("p (h d) -> p h d", h=4)
                    if r == 0:
                        self.tt("dve", ob, PF[5][:, 0:256].rearrange("p (h d) -> p h d", h=4), tmpo[p], ALU.add,
                                r=["PF5", "tmpo" + sfx], w=["OB"])
                    else:
                        self.tt("dve", tmpo[p], PF[5][:, 0:256].rearrange("p (h d) -> p h d", h=4), tmpo[p], ALU.add,
                                r=["PF5", "tmpo" + sfx], w=["tmpo" + sfx])
                        self.tt("pool", ob, ob, tmpo[p], ALU.add, r=["tmpo" + sfx, "OB"], w=["OB"])
                    for h in range(4):
                        rows = slice((h % 2) * 64, (h % 2) * 64 + 64)
                        self.mm(PF[3][rows, (h // 2) * 64:(h // 2) * 64 + 64], ktil[p][:, h, :], vnew[p][:, h, :],
                                True, True, r=["ktil" + sfx, "vnew" + sfx], w=["PF3"])
                    self.tt("pool", Sf, Sf, egl2.unsqueeze(2).to_broadcast([128, 2, 64]), ALU.mult,
                            r=["Sf", gk], w=["Sf"])
                    self.tt("dve", Sf, Sf, PF[3][:, 0:128].rearrange("p (h d) -> p h d", h=2), ALU.add,
                            r=["Sf", "PF3"], w=["Sf"])
                    self.cp("act", Sb, Sf, r=["Sf"], w=["Sb"])
                order = list(order)
                for k0 in range(0, len(order), 2):
                    gens = [chunk_gen(k0 + i, order[k0 + i], i) for i in range(min(2, len(order) - k0))]
                    live = list(gens)
                    while live:
                        for g in list(live):
                            if next(g) == "REC":
                                live.remove(g)
                    for g in gens:
                        for _ in g:
                            pass
                if not J.latent:
                    self.store(self.osb[s, l, r].rearrange("(j i) k v -> (i k) j v", i=2), Sf, r=["Sf"])
            self.S.phase = "b4_out"
            for j0 in range(0, ntq, 4):
                nj = min(4, ntq - j0)
                ob = OB[:, j0:j0 + nj, :]
                ob4 = ob.rearrange("p t (h d) -> p (t h) d", h=4)
                ss = self.small[:, 16:16 + nj * 4]
                rs = self.small[:, 32:32 + nj * 4]
                sq4 = tmpf[:, 0:nj * 256].rearrange("p (a d) -> p a d", d=64)
                self.tt("pool", sq4, ob4, ob4, ALU.mult, r=["OB"], w=["tmpf"])
                self.red("dve", ss, sq4, r=["tmpf"], w=["small"])
                self.rstd(rs, ss, 1.0 / (64.0 * 64.0), r=["small"], w=["small"])
                self.tt("dve", sq4, ob4, rs.unsqueeze(2).to_broadcast([128, nj * 4, 64]), ALU.mult,
                        r=["OB", "small"], w=["tmpf"])
                self.tt("pool", sq4, sq4, gon.unsqueeze(1).to_broadcast([128, nj * 4, 64]), ALU.mult,
                        r=["tmpf", "gon"], w=["tmpf"])
                self.tt("pool", self.YB[:, t0 + j0:t0 + j0 + nj, :].rearrange("p t c -> p (t c)"),
                        tmpf[:, 0:nj * 256], GT[:, j0:j0 + nj, :].rearrange("p t c -> p (t c)"), ALU.mult,
                        r=["tmpf", "GT"], w=["YB"])
            if s == 0:
                self.tap("ob_%s%d" % (J.name, l), OB[:, 0:2, :], r=["OB"])
        self.tap("yb_%s%d" % (J.name, l), self.YB, r=["YB"])

    def rope_apply(self, eng, dst, src, t, H, scr, skey, r, w):
        cos = self.rope[:, t, 0:32].unsqueeze(1).to_broadcast([128, H, 32])
        sin = self.rope[:, t, 32:64].unsqueeze(1).to_broadcast([128, H, 32])
        x1, x2 = src[:, :, 0:32], src[:, :, 32:64]
        sc = scr[:, 0:H * 64].rearrange("p (h d) -> p h d", h=H)
        t1, t2 = sc[:, :, 0:32], sc[:, :, 32:64]
        self.tt(eng, t1, x1, cos, ALU.mult, r=r + ["rope"], w=[skey])
        self.tt(eng, t2, x2, sin, ALU.mult, r=r + ["rope"], w=[skey])
        self.tt(eng, dst[:, :, 0:32], t1, t2, ALU.subtract, r=[skey], w=w)
        self.tt(eng, t1, x1, sin, ALU.mult, r=r + ["rope"] + w, w=[skey])
        self.tt(eng, t2, x2, cos, ALU.mult, r=r + ["rope"], w=[skey])
        self.tt(eng, dst[:, :, 32:64], t1, t2, ALU.add, r=[skey], w=w)

    def head_norm(self, src, H, gbc, dst, scr, skey, r, w, sm0):
        sq = scr[:, 0:H * 64].rearrange("p (h d) -> p h d", h=H)
        ss = self.small[:, sm0:sm0 + H]
        rs = self.small[:, sm0 + 8:sm0 + 8 + H]
        self.tt("pool", sq, src, src, ALU.mult, r=r, w=[skey])
        self.red("dve", ss, sq, r=[skey], w=["small"])
        self.rstd(rs, ss, 1.0 / 64.0, r=["small"], w=["small"])
        self.tt("dve", sq, src, rs.unsqueeze(2).to_broadcast([128, H, 64]), ALU.mult, r=r + ["small"], w=[skey])
        self.tt("pool", dst, sq, gbc.unsqueeze(1).to_broadcast([128, H, 64]), ALU.mult, r=[skey, "gqk"], w=w)

    def attn(self, nq, qT, qkey, kts, O, h65, st):
        PF = self.PF
        nj = nq // 128
        n = len(kts)
        slots = []
        SB = (2, 3, 0, 1)
        NE = len(st["E"])
        LA = 3

        def scores(i):
            kT, V, bias, rk = kts[i]
            c = st["cnt"]
            st["cnt"] += 1
            bi_ = SB[c % 4]
            sp = PF[bi_][:, 0:nq]
            spk = "PF%d" % bi_
            self.mm(sp, kT, qT, True, bias is None, r=rk + [qkey], w=[spk])
            if bias is not None:
                for bi, (qb, bap) in enumerate(bias):
                    self.mm(sp[:, qb * 64:(qb + 1) * 64], self.ident_b, bap, False, bi == len(bias) - 1,
                            r=["BB2", "cB"], w=[spk])
            slots.append((c, sp, spk))

        for i in range(min(LA, n)):
            scores(i)
        for i in range(n):
            if i + LA < n:
                scores(i + LA)
            kT, V, bias, rk = kts[i]
            c, sp, spk = slots[i]
            E = st["E"][c % NE][:, 0:nq]
            ek = "E%d" % (c % NE)
            self.act(E, sp, AF.Exp, r=[spk], w=[ek], scale=0.125)
            for jj in range(nj):
                self.mm(O[jj][0][:, h65 * 65:h65 * 65 + 65], E[:, jj * 128:(jj + 1) * 128], V, i == 0,
                        i == n - 1, r=[ek] + rk, w=[O[jj][1]])

    def attn_out(self, Ops, okey, H, gbc, gkey, ydst, ykey, oscr, oskey, sm0):
        ov = Ops[:, 0:H * 65].rearrange("p (h d) -> p h d", h=H)
        rden = self.small[:, sm0:sm0 + H]
        self.add("dve", lambda e: e.reciprocal(out=rden.unsqueeze(2), in_=ov[:, :, 64:65]), [okey], ["small"])
        o3 = oscr[:, 0:H * 64].rearrange("p (h d) -> p h d", h=H)
        self.tt("dve", o3, ov[:, :, 0:64], rden.unsqueeze(2).to_broadcast([128, H, 64]), ALU.mult,
                r=[okey, "small"], w=[oskey])
        ss = self.small[:, sm0 + 8:sm0 + 9]
        rs = self.small[:, sm0 + 9:sm0 + 10]
        junk = self.junkb[:, 0:H * 64]
        self.act(junk, oscr[:, 0:H * 64], AF.Square, r=[oskey], w=["junkb", "small"], accum=ss)
        self.rstd(rs, ss, 1.0 / (H * 64.0), r=["small"], w=["small"])
        self.stt("dve", ydst, oscr[:, 0:H * 64], rs, gbc, ALU.mult, ALU.mult, r=[oskey, "small", gkey], w=[ykey])

    def residual_epilogue(self, t, Gbc, gkey, tmp, tkey):
        PF = self.PF
        ss = self.small[:, 40:42]
        rs = self.small[:, 42:43]
        junk = self.junkb[:, 0:512]
        self.act(junk, PF[0], AF.Square, r=["PF0"], w=["junkb", "small"], accum=ss[:, 0:1])
        self.act(junk, PF[1], AF.Square, r=["PF1"], w=["junkb", "small"], accum=ss[:, 1:2])
        self.tt("dve", ss[:, 0:1], ss[:, 0:1], ss[:, 1:2], ALU.add, r=["small"], w=["small"])
        self.rstd(rs, ss[:, 0:1], 1.0 / D, r=["small"], w=["small"])
        self.stt("dve", tmp[:, 0:512], PF[0], rs, Gbc[:, 0:512], ALU.mult, ALU.mult, r=["PF0", "small", gkey], w=[tkey])
        self.stt("dve", tmp[:, 512:1024], PF[1], rs, Gbc[:, 512:1024], ALU.mult, ALU.mult,
                 r=["PF1", "small", gkey], w=[tkey])
        self.tt("pool", self.X[:, t, :], self.X[:, t, :], tmp, ALU.add, r=[tkey, "X%d" % t], w=["X%d" % t])

    def phase_kvq(self, l, J):
        S = self.S
        A = self.A
        PF, PT = self.PF, self.PT
        S.barrier()
        A.release(self.yb_mark)
        nt, T = J.nt, J.T
        ntq = T // 128
        nctx = 2 if J.latent else 0
        nkt_seq = ntq + nctx
        NKT = J.nseq * nkt_seq
        Wkv = A.alloc([8, 1024], BF16)
        Wq = A.alloc([8, 768], BF16)
        Wo = Wkv
        kTA = A.alloc([NKT * 128], BF16)
        VA = A.alloc([NKT, 2, 65], BF16)
        kTC = A.alloc([3, NKT * 128], BF16)
        VC = A.alloc([NKT, 6, 65], BF16)
        gqk = A.alloc([2, 64])
        goa = A.alloc([384])
        goc = A.alloc([384])
        hTq = A.alloc([8, 256], BF16)
        hT1 = hTq[:, :, 0:128]
        xn = A.alloc([D], BF16)
        tmpf = A.alloc([D])
        self.junkb = A.alloc([512], BF16)
        ZA0 = A.alloc([256])
        ZA = [ZA0, ZA0]
        ZC0 = A.alloc([768])
        ZC = [ZC0, ZC0]
        knf0 = A.alloc([128])
        knf = [knf0, knf0]
        scr = A.alloc([768])
        kab = A.alloc([128], BF16)
        kcb = A.alloc([384], BF16)
        ZQ = A.alloc([384])
        qab = A.alloc([3, 2, 64], BF16)
        qT_all = A.alloc([6, 256], BF16)
        qTA = qT_all[:, 0:3, :]
        qTC = qT_all[:, 3:6, :]
        xnb = [xn, qT_all.rearrange("p a b -> p (a b)")[:, 0:D]]
        hbuf = [hTq[:, :, 0:128], hTq[:, :, 128:256]]
        Eb = [A.alloc([256], BF16) for _ in range(6)]
        oscr = A.alloc([384])
        qnf = oscr
        ya = [A.alloc([384], BF16) for _ in range(2)]
        yc = [A.alloc([384], BF16) for _ in range(2)]
        yT = A.alloc([8, 128], BF16)
        tmpx = tmpf

        win = self.w_in[l]
        wv = win.rearrange("(k p) n -> p k n", p=128)
        self.dma("pool", Wkv[:, :, 0:256], wv[:, :, C_AK:C_AK + 256], w=["Wkv"])
        self.dma("pool", Wkv[:, :, 256:1024], wv[:, :, C_CK:C_CK + 768], w=["Wkv"])
        self.dma("pool", Wq[:, :, 0:384], wv[:, :, C_AQ:C_AQ + 384], w=["Wq"])
        self.dma("pool", Wq[:, :, 384:768], wv[:, :, C_CQ:C_CQ + 384], w=["Wq"])
        self.dma("sp", gqk.rearrange("p a d -> p (a d)"),
                 self.g_qk_a[l:l + 1].rearrange("o a d -> o (a d)").partition_broadcast(128), w=["gqk"])
        self.dma("sp", goa, self.g_out_a[l:l + 1, :].partition_broadcast(128), w=["goa"])
        self.dma("sp", goc, self.g_out_c[l:l + 1, :].partition_broadcast(128), w=["goc"])
        if J.latent:
            self.kv_scr, self.kv_tmpf = scr, tmpf
            self.build_bias(l)
        self.memset("pool", VA[:, :, :, 64:65], 1.0, w=["VA"])
        self.memset("pool", VC[:, :, :, 64:65], 1.0, w=["VC"])

        if _DBG.get("KSTOP") == "0":
            return
        self.S.phase = "kv"
        self.hT_pre(0, xnb[0], "xnb0", 48)
        self.hT_post(0, hbuf[0], "hTq0", 0, xnb[0], "xnb0", tmpf)
        for t in range(nt):
            s, j = t // ntq, t % ntq
            kt = s * nkt_seq + j
            p2 = t % 2
            hT1 = hbuf[p2]
            for k in range(8):
                self.mm(PF[0], hT1[:, k, :], Wkv[:, k, 0:512], k == 0, k == 7, r=["hTq%d" % p2, "Wkv"], w=["PF0"])
            for k in range(8):
                self.mm(PF[1], hT1[:, k, :], Wkv[:, k, 512:1024], k == 0, k == 7, r=["hTq%d" % p2, "Wkv"], w=["PF1"])
            if t + 1 < nt:
                q2 = (t + 1) % 2
                self.hT_pre(t + 1, xnb[q2], "xnb%d" % q2, 48 + 2 * q2)
                self.hT_post(0, hbuf[q2], "hTq%d" % q2, 0, xnb[q2], "xnb%d" % q2, tmpf)
            if _DBG.get("KSTOP") == "0a":
                continue
            za, zc = ZA[p2], ZC[p2]
            zak, zck = "ZA", "ZC"
            self.cp("act", za, PF[0][:, 0:256], r=["PF0"], w=[zak])
            self.cp(_DBG.get("CPENG", "dve"), zc[:, 0:256], PF[0][:, 256:512], r=["PF0"], w=[zck])
            self.cp("act", zc[:, 256:768], PF[1], r=["PF1"], w=[zck])
            if _DBG.get("KSTOP") == "0d":
                continue
            kn = knf[p2]
            knk = "knf"
            kn3 = kn.rearrange("p (h d) -> p h d", h=2)
            self.head_norm(za[:, 0:128].rearrange("p (h d) -> p h d", h=2), 2, gqk[:, 1, :], kn3, scr, "scr",
                           [zak], [knk], 24)
            if _DBG.get("KSTOP") == "0e":
                continue
            if not J.latent and _DBG.get("KSTOP") == "0c":
                self.cp("pool", kab, kn, r=[knk], w=["kab"])
                continue
            if not J.latent:
                tok = slice(j * 128, (j + 1) * 128)
                self.store(self.oka[s, l, tok, :], kn, r=[knk])
                self.store(self.ova[s, l, tok, :], za[:, 128:256], r=[zak])
                self.store(self.okc[s, l, tok, :], zc[:, 0:384], r=[zck])
                self.store(self.ovc[s, l, tok, :], zc[:, 384:768], r=[zck])
                self.cp("pool", kab, kn, r=[knk], w=["kab"])
            else:
                self.rope_apply("pool", kab.rearrange("p (h d) -> p h d", h=2), kn3, j, 2, scr, "scr", [knk], ["kab"])
            if _DBG.get("KSTOP") == "0b":
                continue
            self.tr(PT[1][:, 0:128], kab, r=["kab", "cB"], w=["PT1"])
            self.cp("pool", VA[:, kt, :, 0:64], za[:, 128:256].rearrange("p (h d) -> p h d", h=2), r=[zak], w=["VA"])
            self.cp("pool", kcb, zc[:, 0:384], r=[zck], w=["kcb"])
            for p in range(3):
                self.tr(PT[1][:, 128 + p * 128:256 + p * 128], kcb[:, p * 128:(p + 1) * 128], r=["kcb", "cB"], w=["PT1"])
            self.cp("act", kTA[:, kt * 128:(kt + 1) * 128], PT[1][:, 0:128], r=["PT1"], w=["kTA"])
            self.cp("dve", kTC[:, :, kt * 128:(kt + 1) * 128], PT[1][:, 128:512].rearrange("p (a b) -> p a b", a=3),
                    r=["PT1"], w=["kTC"])
            self.cp("pool", VC[:, kt, :, 0:64], zc[:, 384:768].rearrange("p (h d) -> p h d", h=6), r=[zck], w=["VC"])
        if J.latent:
            for j in range(2):
                kt = ntq + j
                tok = slice(j * 128, (j + 1) * 128)
                self.dma("pool", kab, self.cak[l, tok, :], w=["kab"])
                self.dma("pool", kcb, self.cck[l, tok, :], w=["kcb"])
                self.dma("pool", VA[:, kt, :, 0:64], self.cav[l, tok, :].rearrange("p (h d) -> p h d", h=2), w=["VA"])
                self.dma("pool", VC[:, kt, :, 0:64], self.ccv[l, tok, :].rearrange("p (h d) -> p h d", h=6), w=["VC"])
                self.tr(PT[1][:, 0:128], kab, r=["kab", "cB"], w=["PT1"])
                for p in range(3):
                    self.tr(PT[1][:, 128 + p * 128:256 + p * 128], kcb[:, p * 128:(p + 1) * 128], r=["kcb", "cB"],
                            w=["PT1"])
                self.cp("act", kTA[:, kt * 128:(kt + 1) * 128], PT[1][:, 0:128], r=["PT1"], w=["kTA"])
                self.cp("dve", kTC[:, :, kt * 128:(kt + 1) * 128],
                        PT[1][:, 128:512].rearrange("p (a b) -> p a b", a=3), r=["PT1"], w=["kTC"])
        self.tap("kTA_%s%d" % (J.name, l), kTA, r=["kTA"])
        self.load_w(Wo, self.w_out[l], 0, D, "Wkv")

        if _DBG.get("KSTOP") == "1":
            return
        S.barrier()
        ast = {"cnt": 0, "E": Eb}
        for g in range(nt // 2):
            tl = [2 * g, 2 * g + 1]
            s = tl[0] // ntq
            self.make_hT(J, tl, 0, hTq, "hTq", xn, tmpf)
            self.S.phase = "q_proj"
            for jj, t in enumerate(tl):
                for k in range(8):
                    self.mm(PF[0][:, 0:384], hTq[:, k, jj * 128:(jj + 1) * 128], Wq[:, k, 0:384], k == 0, k == 7,
                            r=["hTq", "Wq"], w=["PF0"])
                self.cp("act", ZQ, PF[0][:, 0:384], r=["PF0"], w=["ZQ"])
                q3 = qnf.rearrange("p (h d) -> p h d", h=6)
                self.head_norm(ZQ.rearrange("p (h d) -> p h d", h=6), 6, gqk[:, 0, :], q3, scr, "scr", ["ZQ"], ["oscr"], 24)
                qdst = qab.rearrange("q p g d -> q g p d")
                qsrc = qnf.rearrange("q (g p d) -> q g p d", g=2, p=3)
                if J.latent:
                    for gg in range(2):
                        self.rope_apply("pool", qdst[:, gg], qsrc[:, gg], t % ntq, 3, scr, "scr", ["oscr"], ["qab"])
                else:
                    for gg in range(2):
                        self.cp("pool", qdst[:, gg], qsrc[:, gg], r=["oscr"], w=["qab"])
                for p in range(3):
                    self.tr(PT[1][:, jj * 384 + p * 128:jj * 384 + (p + 1) * 128],
                            qab[:, p].rearrange("q g d -> q (g d)"), r=["qab", "cB"], w=["PT1"])
                self.cp("act", qTA[:, :, jj * 128:(jj + 1) * 128],
                        PT[1][:, jj * 384:(jj + 1) * 384].rearrange("p (a b) -> p a b", a=3), r=["PT1"], w=["qTA"])
            for p in range(3):
                ps, pk = (PF[1][:, (p % 2) * 256:(p % 2) * 256 + 256], "PF1") if p < 2 else (PF[0][:, 0:256], "PF0")
                for k in range(8):
                    self.mm(ps, Wq[:, k, 384 + p * 128:384 + (p + 1) * 128], hTq[:, k, :], k == 0, k == 7,
                            r=["hTq", "Wq"], w=[pk])
                self.cp("dve" if p % 2 else "act", qTC[:, p, :], ps, r=[pk], w=["qTC"])
            if _DBG.get("KSTOP") == "2":
                continue
            self.S.phase = "attnA"
            O = [(PF[4], "PF4"), (PF[5], "PF5")]
            for h in range(6):
                p, gk_ = h % 3, h // 3
                rows = slice(gk_ * 64, gk_ * 64 + 64)
                kts = []
                for i in range(nkt_seq):
                    kt = s * nkt_seq + i
                    kts.append((kTA[rows, kt * 128:(kt + 1) * 128], VA[:, kt, gk_, :], None, ["kTA", "VA"]))
                self.attn(256, qTA[rows, p, :], "qTA", kts, O, h, ast)
            for jj in range(2):
                self.attn_out(O[jj][0], O[jj][1], 6, goa, "goa", ya[jj], "ya%d" % jj, oscr, "oscr", 24)
            if _DBG.get("KSTOP") == "3":
                continue
            self.S.phase = "attnC"
            if not J.latent:
                for h in range(6):
                    p = h // 2
                    rows = slice((h % 2) * 64, (h % 2) * 64 + 64)
                    kts = []
                    for i in range(nkt_seq):
                        kt = s * nkt_seq + i
                        kts.append((kTC[rows, p, kt * 128:(kt + 1) * 128], VC[:, kt, h, :], None, ["kTC", "VC"]))
                    self.attn(256, qTC[rows, p, :], "qTC", kts, O, h, ast)
            else:
                self.latent_c(l, J, tl, kTC, VC, qTC, O, ast)
            for jj in range(2):
                self.attn_out(O[jj][0], O[jj][1], 6, goc, "goc", yc[jj], "yc%d" % jj, oscr, "oscr", 24)
            if g == 0:
                self.tap("ya_%s%d" % (J.name, l), ya[0], r=["ya0"])
                self.tap("yc_%s%d" % (J.name, l), yc[0], r=["yc0"])
            if _DBG.get("KSTOP") == "4":
                continue
            self.S.phase = "merge"
            for jj, t in enumerate(tl):
                for kk in range(8):
                    if kk < 3:
                        src, rk = ya[jj][:, kk * 128:(kk + 1) * 128], "ya%d" % jj
                    elif kk < 5:
                        src, rk = self.YB[:, t, (kk - 3) * 128:(kk - 2) * 128], "YB"
                    else:
                        src, rk = yc[jj][:, (kk - 5) * 128:(kk - 4) * 128], "yc%d" % jj
                    self.tr(PT[0][:, kk * 128:(kk + 1) * 128], src, r=[rk, "cB"], w=["PT0"])
                self.cp("act", yT.rearrange("p k c -> p (k c)"), PT[0], r=["PT0"], w=["yT"])
                for hf in range(2):
                    for k in range(8):
                        self.mm(PF[hf], yT[:, k, :], Wo[:, k, hf * 512:(hf + 1) * 512], k == 0, k == 7,
                                r=["yT", "Wkv"], w=["PF%d" % hf])
                self.residual_epilogue(t, self.G1, "G1", tmpx, "tmpf")
        self.tap("x1_%s%d" % (J.name, l), self.X[:, 0:nt, :], r=["X%d" % t for t in range(nt)])

    def build_bias(self, l):
        A = self.A
        PF = self.PF
        BB2 = A.alloc([6, 17, 64], BF16)
        self.BB2 = BB2
        rT = self.kv_scr[:, 512:614]
        ohs = [self.kv_tmpf[:, 0:512], self.kv_tmpf[:, 512:1024]]
        tb = [self.kv_scr[:, 0:256].bitcast(BF16), self.kv_scr[:, 256:512].bitcast(BF16)]
        self.dma("sp", rT[0:33, :], self.rsel, w=["scr"])
        r3 = rT[0:31, :].rearrange("c (h d) -> c h d", h=6)
        for h in range(6):
            self.dma("sp", r3[:, h, 1:16], self.rpb[l, h].rearrange("d c -> c d"), r=["scr"], w=["scr"], slow=True)
        for ch in range(8):
            b2 = ch % 2
            self.dma("sp", ohs[b2][0:33, :], self.ohc[:, ch * 512:(ch + 1) * 512], w=["tmpf"])
            self.mm(PF[0][0:102, :], rT[0:33, :], ohs[b2][0:33, :], True, True, r=["scr", "tmpf"], w=["PF0"])
            self.act(tb[b2][0:102, :], PF[0][0:102, :], AF.Copy, r=["PF0"], w=["scr"], scale=8.0)
            self.dma("sp", self.tabscr[:, ch * 512:(ch + 1) * 512], tb[b2][0:102, :], r=["scr"], w=["tabscr"])
        tv = self.tabscr.rearrange("(h d) (k q) -> k h d q", h=6, k=64)
        for h in range(6):
            self.dma("sp", BB2[0:64, h, :, :], tv[:, h, :, :], r=["tabscr"], w=["BB2"])
            self.dma("sp", BB2[64:128, h, 0:16, :], tv[:, h, 1:17, :], r=["tabscr"], w=["BB2"])
            self.dma("sp", BB2[64:128, h, 16:17, :], tv[:, h, 16:17, :], r=["tabscr"], w=["BB2"])

    def latent_c(self, l, J, tl, kTC, VC, qTC, O, ast):
        clip = lambda v: min(max(v, 0), 24)
        for jj, t in enumerate(tl):
            s0, s1 = clip(2 * t - 4), clip(2 * t + 1 - 4)
            kt_lo, kt_hi = s0 // 2, (s1 + 7) // 2
            for h in range(6):
                p = h // 2
                rows = slice((h % 2) * 64, (h % 2) * 64 + 64)
                kts = []
                for kt in range(kt_lo, kt_hi + 1):
                    dl = kt - t
                    bl = []
                    for qb in range(2):
                        d = 2 * dl + 8 - qb
                        assert 0 <= d <= 16, d
                        bl.append((qb, self.BB2[:, h, d, :]))
                        sq_ = clip(2 * t + qb - 4)
                        for kb in range(2):
                            krow = 2 * kt + kb
                            if not (sq_ <= krow <= sq_ + 7):
                                bl.append((qb, self.negh[:, kb, :]))
                    kts.append((kTC[rows, p, kt * 128:(kt + 1) * 128], VC[:, kt, h, :], bl, ["kTC", "VC"]))
                for c in (16, 17):
                    kts.append((kTC[rows, p, c * 128:(c + 1) * 128], VC[:, c, h, :], None, ["kTC", "VC"]))
                self.attn(128, qTC[rows, p, jj * 128:(jj + 1) * 128], "qTC", kts, [O[jj]], h, ast)

    def phase_f(self, l, J):
        S = self.S
        A = self.A
        PF, PT = self.PF, self.PT
        S.barrier()
        A.release(self.base_mark)
        nt = J.nt
        GF = 4
        Wd = A.alloc([22, D], BF16)
        hT2s = [A.alloc([8, GF * 128], BF16) for _ in range(2)]
        actT = A.alloc([22, GF * 128], BF16)
        ring = [A.alloc([8, 256], BF16) for _ in range(3)]
        xn4 = [A.alloc([D], BF16) for _ in range(GF)]
        tmpf = A.alloc([D])
        self.junkb = A.alloc([512], BF16)
        sg = [A.alloc([512]) for _ in range(2)]
        tmpx = A.alloc([D])
        self.S.phase = "ffn"
        self.dma("pool", Wd, self.w_down[l].rearrange("(c p) n -> p c n", p=128), w=["Wd"])
        wg = self.w_gu[l].rearrange("(k p) n -> p k n", p=128)
        ng = nt // GF
        for j in range(GF):
            self.hT_pre(j, xn4[j], "xn4_%d" % j, 48 + 2 * j)
            self.hT_post(1, hT2s[0], "hT2_0", j, xn4[j], "xn4_%d" % j, tmpf)
        for g in range(ng):
            tl = list(range(g * GF, (g + 1) * GF))
            hT2 = hT2s[g % 2]
            hk = "hT2_%d" % (g % 2)
            nxt = list(range((g + 1) * GF, (g + 2) * GF)) if g + 1 < ng else []
            for c in range(22):
                if nxt and c == 1:
                    for j, t in enumerate(nxt):
                        self.hT_pre(t, xn4[j], "xn4_%d" % j, 48 + 2 * j)
                if nxt and c in (8, 11, 14, 17):
                    j = (c - 8) // 3
                    self.hT_post(1, hT2s[(g + 1) % 2], "hT2_%d" % ((g + 1) % 2), j, xn4[j], "xn4_%d" % j, tmpf)
                sl = c % 3
                rk = "ring%d" % sl
                self.dma("pool", ring[sl][:, :, 0:128], wg[:, :, c * 128:(c + 1) * 128], w=[rk])
                self.dma("pool", ring[sl][:, :, 128:256], wg[:, :, DFF + c * 128:DFF + (c + 1) * 128], w=[rk])
                pg, pu = PF[2 + (c % 2) * 2], PF[3 + (c % 2) * 2]
                pgk, puk = "PF%d" % (2 + (c % 2) * 2), "PF%d" % (3 + (c % 2) * 2)
                for k in range(8):
                    self.mm(pg, ring[sl][:, k, 0:128], hT2[:, k, :], k == 0, k == 7, r=[rk, hk], w=[pgk])
                for k in range(8):
                    self.mm(pu, ring[sl][:, k, 128:256], hT2[:, k, :], k == 0, k == 7, r=[rk, hk], w=[puk])
                s2 = sg[c % 2]
                self.act(s2, pg, AF.Silu, r=[pgk], w=["sg%d" % (c % 2)])
                self.tt("dve", actT[:, c, :], s2, pu, ALU.mult, r=["sg%d" % (c % 2), puk], w=["actT"])
            for jj, t in enumerate(tl):
                for hf in range(2):
                    for c in range(22):
                        self.mm(PF[hf], actT[:, c, jj * 128:(jj + 1) * 128], Wd[:, c, hf * 512:(hf + 1) * 512],
                                c == 0, c == 21, r=["actT", "Wd"], w=["PF%d" % hf])
                self.residual_epilogue(t, self.G2, "G2", tmpx, "tmpx")


def make_consts():
    k = np.arange(128)[:, None]
    i = np.arange(128)[None, :]
    cp = np.zeros((128, 7, 128), np.float32)
    cp[:, 0] = (k == i)
    cp[:, 1] = (k <= i)
    cp[:, 2] = (k >= i)
    cp[:, 3] = (k > i)
    cp[:, 4] = (k < i)
    cp[:, 5] = 1.0
    cp[:, 6] = ((k // 64) == (i // 64))
    t = np.arange(TS)
    row = (t // 64).astype(np.float32)
    col = (t % 64).astype(np.float32)
    inv = (10000.0 ** (-np.arange(16, dtype=np.float32) / 16)).astype(np.float32)
    ang = np.concatenate([row[:, None] * inv, col[:, None] * inv], axis=-1).astype(np.float32)
    tab = np.concatenate([np.cos(ang), np.sin(ang)], axis=-1).astype(np.float32)
    ropet = tab.reshape(16, 128, 64).transpose(1, 0, 2).reshape(128, 16 * 64)
    qm = np.zeros((128, 14, 128), np.float32)
    for lv in range(7):
        b = 2 ** lv
        ll = ((k // (2 * b)) == (i // (2 * b))) & ((k % (2 * b)) >= b) & ((i % (2 * b)) < b)
        qm[:, lv] = ll
        qm[:, 7 + lv] = ll.T
    rsel = np.zeros((33, 6, 17), np.float32)
    rsel[31, :, 1:16] = 1.0
    rsel[32, :, 0] = 1.0
    rsel[32, :, 16] = 1.0
    kc = np.arange(64)[:, None]
    qc = np.arange(64)[None, :]
    cst = np.clip(qc - 8, 0, 48)
    inwin = (kc >= cst) & (kc < cst + 16)
    oh = np.zeros((33, 64, 64), np.float32)
    dcc = kc - qc + 15
    for c in range(31):
        oh[c] = ((dcc == c) & inwin)
    oh[31] = np.where(inwin, 0.0, NEG / 8.0)
    oh[32] = NEG / 8.0
    negh = np.zeros((128, 2, 64), np.float32)
    negh[0:64, 0, :] = NEG * 8.0
    negh[64:128, 1, :] = NEG * 8.0
    return (cp.reshape(128, 7 * 128), np.ascontiguousarray(ropet), qm.reshape(128, 14 * 128),
            rsel.reshape(33, 102), oh.reshape(33, 4096), negh.reshape(128, 128))


_CACHE = {}


def get_nc(nl=DEPTH, jobs=("p", "s"), dbg=(), same=True, stop=None):
    key = (nl, tuple(jobs), tuple(sorted(dbg)), same, stop)
    if key not in _CACHE:
        kb = KB(nl=nl, jobs=jobs, dbg=dbg, same=same, stop=stop)
        nc = kb.build()
        _CACHE[key] = (nc, kb)
    return _CACHE[key]


def make_in_maps(inp):
    f = lambda a: np.ascontiguousarray(np.asarray(a, dtype=np.float32))
    cpack, ropet, qmask, rsel, ohc, neghd = make_consts()
    shared = {
        "w_mod": f(inp["w_mod"]), "b_mod": f(inp["b_mod"]), "g_norm": f(inp["g_norm"]), "w_in": f(inp["w_in"]),
        "g_qk_a": f(inp["g_qk_a"]), "g_out_a": f(inp["g_out_a"]), "conv_w": f(inp["conv_w"]),
        "a_log": f(inp["a_log"]).reshape(DEPTH, 8), "dt_bias": f(inp["dt_bias"]).reshape(DEPTH, 8),
        "g_onorm_b": f(inp["g_onorm_b"]), "rpb": f(inp["rpb"]), "g_out_c": f(inp["g_out_c"]),
        "w_out": f(inp["w_out"]), "w_gu": f(inp["w_gu"]), "w_down": f(inp["w_down"]),
        "cpack": cpack, "ropet": ropet, "qmask": qmask, "rsel": rsel, "ohc": ohc, "neghd": neghd,
    }
    maps = []
    for c in range(8):
        b = c // 4
        m = dict(shared)
        m["xp"] = f(inp["x_prompt"][NPS * c:NPS * (c + 1)]).reshape(NPS * TP, D)
        m["xs"] = f(inp["x_sample"][b])
        m["cak"] = f(inp["cache_a_k"][b]).reshape(DEPTH, 256, 128)
        m["cav"] = f(inp["cache_a_v"][b]).reshape(DEPTH, 256, 128)
        m["sb0"] = f(inp["state_b"][b])
        m["cck"] = f(inp["cache_c_k"][b]).reshape(DEPTH, 256, 384)
        m["ccv"] = f(inp["cache_c_v"][b]).reshape(DEPTH, 256, 384)
        m["cvec"] = np.stack([f(inp["c_ctx"]), f(inp["c"][b])], 0)
        maps.append(m)
    return maps


def kernel(**inputs):
    nc, kb = get_nc()
    maps = make_in_maps(inputs)
    res = run_bass_kernel_spmd(nc, maps, core_ids=list(range(8)))
    R = res.results
    yp = np.concatenate([R[c]["yp"].reshape(NPS, TP, D) for c in range(8)], 0)
    ys = np.stack([R[0]["ys"], R[4]["ys"]], 0)
    ka = np.concatenate([R[c]["oka"].reshape(NPS, DEPTH, TP, 2, 64) for c in range(8)], 0)
    va = np.concatenate([R[c]["ova"].reshape(NPS, DEPTH, TP, 2, 64) for c in range(8)], 0)
    sb = np.concatenate([R[c]["osb"] for c in range(8)], 0)
    kc = np.concatenate([R[c]["okc"].reshape(NPS, DEPTH, TP, 6, 64) for c in range(8)], 0)
    vc = np.concatenate([R[c]["ovc"].reshape(NPS, DEPTH, TP, 6, 64) for c in range(8)], 0)
    return (yp.astype(np.float32), ys.astype(np.float32), ka.astype(np.float32), va.astype(np.float32),
            sb.astype(np.float32), kc.astype(np.float32), vc.astype(np.float32))
```

```python
import contextlib
import math
import os
import numpy as np
import concourse.bass as bass
import concourse.mybir as mybir
from concourse.bass_utils import run_bass_kernel_spmd

F32, BF16 = mybir.dt.float32, mybir.dt.bfloat16
AF = mybir.ActivationFunctionType
ALU = mybir.AluOpType
AX = mybir.AxisListType

D = 1024
DEPTH = 4
NPS = 4
TP = 256
TS = 2048
DFF = 2816
INW = 2832
EPS = 1e-6
C_AQ, C_AK, C_AV, C_BQKV, C_BG, C_BBETA, C_BALPHA, C_CQ, C_CK, C_CV = (
    0, 384, 512, 640, 1408, 1664, 1672, 1680, 2064, 2448)
NEG = -30000.0
_DBG = {}
POOLENG = _DBG.get('POOLENG', 'pool')

ENGS = ("pe", "act", "dve", "pool", "sp")


class Op:
    __slots__ = ("eng", "fn", "deps", "signal", "sem", "val", "dma", "epoch", "phase", "iname")

    def __init__(self, eng, fn, dma, epoch):
        self.eng = eng
        self.fn = fn
        self.deps = []
        self.signal = False
        self.sem = None
        self.val = 0
        self.dma = dma
        self.epoch = epoch
        self.phase = None
        self.iname = None


class Sched:
    def __init__(self, n_dma_sems=24, same_eng_sync=True):
        self.ops = {e: [] for e in ENGS}
        self.lastw = {}
        self.readers = {}
        self.n_dma_sems = n_dma_sems
        self.dma_rr = {e: 0 for e in ENGS}
        self.dma_last = {e: [None] * n_dma_sems for e in ENGS}
        self.epoch = 0
        self.final_ops = []
        self.same = same_eng_sync
        self.bar = {e: [] for e in ENGS}
        self.seq = []
        self.phase = "init"

    def _dep(self, op, other):
        if other is None or other is op:
            return
        if not other.dma and not op.dma and other.eng == op.eng:
            if op.eng == "pe" or not self.same:
                return
        op.deps.append(other)

    def barrier(self):
        lasts = [self.ops[e][-1] for e in ENGS if self.ops[e]]
        for e in ENGS:
            lasts += [d for d in self.dma_last[e] if d is not None]
        for e in ENGS:
            self.bar[e] = list(lasts)

    def add(self, eng, fn, reads=(), writes=(), dma=False):
        op = Op(eng, fn, dma, self.epoch)
        op.phase = self.phase
        if self.bar[eng]:
            for o in self.bar[eng]:
                if o.dma or dma or o.eng != eng:
                    op.deps.append(o)
            self.bar[eng] = []
        for r in reads:
            self._dep(op, self.lastw.get(r))
            if r[:2] in ("PF", "PT"):
                for rd in self.readers.get(r, ()):
                    if rd.eng != eng:
                        op.deps.append(rd)
        for w in writes:
            self._dep(op, self.lastw.get(w))
            for rd in self.readers.get(w, ()):
                self._dep(op, rd)
        for r in reads:
            self.readers.setdefault(r, []).append(op)
        for w in writes:
            self.lastw[w] = op
            self.readers[w] = []
        if dma:
            k = self.dma_rr[eng]
            self.dma_rr[eng] = (k + 1) % self.n_dma_sems
            prev = self.dma_last[eng][k]
            if prev is not None:
                op.deps.append(prev)
            self.dma_last[eng][k] = op
            op.sem = ("dma" + eng, k)
            op.signal = True
        self.ops[eng].append(op)
        self.seq.append(op)
        return op

    def emit(self, nc, stack):
        for e in ENGS:
            for op in self.ops[e]:
                for d in op.deps:
                    d.signal = True
        for op in self.final_ops:
            op.signal = True
        sems = {}
        counts = {}
        for op in self.seq:
            if not op.signal:
                continue
            if op.dma:
                key = op.sem
                counts[key] = counts.get(key, 0) + 16
            else:
                key = (op.eng, op.epoch)
                counts[key] = counts.get(key, 0) + 1
            op.sem = key
            op.val = counts[key]
            if key not in sems:
                sems[key] = stack.enter_context(nc.semaphore("s_%s_%s" % key))
        self.maxval = max(counts.values()) if counts else 0
        self.nsems = len(sems)
        block = stack.enter_context(nc.Block())
        engobj = {"pe": block.tensor, "act": block.scalar, "dve": block.vector,
                  "pool": block.gpsimd, "sp": block.sync}
        finals = list(self.final_ops)

        def make(e):
            ops = self.ops[e]

            def body(eng):
                waited = {}
                for op in ops:
                    need = {}
                    for d in op.deps:
                        if d.val > need.get(d.sem, 0):
                            need[d.sem] = d.val
                    for key, v in need.items():
                        if waited.get(key, 0) >= v:
                            continue
                        eng.wait_ge(sems[key], v)
                        waited[key] = v
                    ins = op.fn(eng)
                    try:
                        op.iname = ins.ins.name
                    except Exception:
                        pass
                    if op.signal:
                        ins.then_inc(sems[op.sem], 16 if op.dma else 1)
                if e == "sp":
                    for f in finals:
                        if waited.get(f.sem, 0) < f.val:
                            eng.wait_ge(sems[f.sem], f.val)
                            waited[f.sem] = f.val
            return body

        for e in ENGS:
            engobj[e](make(e))


class Arena:
    def __init__(self, ap, ncols):
        self.ap = ap
        self.n = ncols
        self.off = 0
        self.peak = 0

    def alloc(self, shape, dt=F32):
        n = int(np.prod(shape))
        cols = n if dt == F32 else (n + 1) // 2
        a = self.off
        self.off += cols
        self.peak = max(self.peak, self.off)
        assert self.off <= self.n, "arena overflow %d > %d" % (self.off, self.n)
        v = self.ap[:, a:a + cols]
        if dt != F32:
            v = v.bitcast(dt)
            if 2 * cols != n:
                v = v[:, 0:n]
        if len(shape) > 1:
            names = " ".join("d%d" % i for i in range(len(shape)))
            kw = {"d%d" % i: shape[i] for i in range(len(shape) - 1)}
            v = v.rearrange("p (%s) -> p %s" % (names, names), **kw)
        return v

    def mark(self):
        return self.off

    def release(self, m):
        self.off = m


class Job:
    pass


class KB:
    def __init__(self, nl=DEPTH, jobs=("p", "s"), dbg=(), same=True, stop=None):
        self.stop = stop
        self.nl = nl
        self.jobs = jobs
        self.dbg = set(dbg)
        self.nc = bass.Bass("TRN2", target_bir_lowering=False)
        self.S = Sched(same_eng_sync=same)
        self.st = contextlib.ExitStack()
        self.outs = []
        self.uid = 0

    def din(self, name, shape, dt=F32):
        return self.nc.dram_tensor(name, list(shape), dt, kind="ExternalInput").ap()

    def dout(self, name, shape, dt=F32):
        self.outs.append(name)
        return self.nc.dram_tensor(name, list(shape), dt, kind="ExternalOutput").ap()

    def add(self, eng, fn, r=(), w=(), dma=False):
        return self.S.add(eng, fn, reads=r, writes=w, dma=dma)

    def dma(self, q, out, in_, r=(), w=(), slow=False):
        if slow:
            return self.add(q, lambda e: e.dma_start(out=out, in_=in_, allow_slow_non_contiguous=True), r, w, True)
        return self.add(q, lambda e: e.dma_start(out=out, in_=in_), r, w, True)

    def store(self, out, in_, r=()):
        op = self.dma("sp", out, in_, r=r)
        self.S.final_ops.append(op)
        return op

    def mm(self, out, lhsT, rhs, start, stop, r=(), w=()):
        return self.add("pe", lambda e: e.matmul(out, lhsT=lhsT, rhs=rhs, start=start, stop=stop), r, w)

    def tr(self, out, in_, r=(), w=()):
        ident = self.ident_b if in_.dtype == BF16 else self.ident_f
        n = in_.shape[0]
        idn = ident[0:n, 0:n]
        return self.add("pe", lambda e: e.transpose(out=out, in_=in_, identity=idn), r, w)

    def act(self, out, in_, func, r=(), w=(), scale=None, bias=None, accum=None):
        kw = {}
        if scale is not None:
            kw["scale"] = scale
        if bias is not None:
            kw["bias"] = bias
        if accum is not None:
            kw["accum_out"] = accum
        return self.add("act", lambda e: e.activation(out=out, in_=in_, func=func, **kw), r, w)

    def tt(self, eng, out, in0, in1, op, r=(), w=()):
        return self.add(eng, lambda e: e.tensor_tensor(out=out, in0=in0, in1=in1, op=op), r, w)

    def ts(self, eng, out, in0, s1, s2, op0, op1=None, r=(), w=()):
        if op1 is None:
            return self.add(eng, lambda e: e.tensor_scalar(out=out, in0=in0, scalar1=s1, scalar2=None, op0=op0), r, w)
        return self.add(eng, lambda e: e.tensor_scalar(out=out, in0=in0, scalar1=s1, scalar2=s2, op0=op0, op1=op1), r, w)

    def stt(self, eng, out, in0, scalar, in1, op0, op1, r=(), w=()):
        return self.add(eng, lambda e: e.scalar_tensor_tensor(out=out, in0=in0, scalar=scalar, in1=in1, op0=op0, op1=op1), r, w)

    def cp(self, eng, out, in_, r=(), w=()):
        if eng == "act":
            return self.add(eng, lambda e: e.activation(out=out, in_=in_, func=AF.Copy), r, w)
        return self.add(eng, lambda e: e.tensor_copy(out=out, in_=in_), r, w)

    def red(self, eng, out, in_, r=(), w=()):
        return self.add(eng, lambda e: e.tensor_reduce(out=out, in_=in_, axis=AX.X, op=ALU.add), r, w)

    def memset(self, eng, out, val, w=()):
        return self.add(eng, lambda e: e.memset(out, val), (), w)

    def rstd(self, out, ssum, inv_n, r=(), w=()):
        self.act(out, ssum, AF.Ln, r=r, w=w, scale=inv_n, bias=EPS)
        self.act(out, out, AF.Exp, r=w, w=w, scale=-0.5)

    def tap(self, name, ap, r):
        if name not in self.dbg:
            return
        shape = list(ap.shape)
        d = self.dout("dbg_" + name, shape, ap.dtype)
        self.store(d, ap, r=r)

    def build(self):
        nc = self.nc
        nl = self.nl
        st = self.st
        self.xp = self.din("xp", [NPS * TP, D])
        self.xs = self.din("xs", [TS, D])
        self.cak = self.din("cak", [DEPTH, 256, 128])
        self.cav = self.din("cav", [DEPTH, 256, 128])
        self.sb0 = self.din("sb0", [DEPTH, 2, 4, 64, 64])
        self.cck = self.din("cck", [DEPTH, 256, 384])
        self.ccv = self.din("ccv", [DEPTH, 256, 384])
        self.cvec = self.din("cvec", [2, D])
        self.w_mod = self.din("w_mod", [DEPTH, D, 6 * D])
        self.b_mod = self.din("b_mod", [DEPTH, 6 * D])
        self.g_norm = self.din("g_norm", [DEPTH, 4, D])
        self.w_in = self.din("w_in", [DEPTH, D, INW])
        self.g_qk_a = self.din("g_qk_a", [DEPTH, 2, 64])
        self.g_out_a = self.din("g_out_a", [DEPTH, 384])
        self.conv_w = self.din("conv_w", [DEPTH, 3, 768])
        self.a_log = self.din("a_log", [DEPTH, 8])
        self.dt_bias = self.din("dt_bias", [DEPTH, 8])
        self.g_onorm_b = self.din("g_onorm_b", [DEPTH, 64])
        self.rpb = self.din("rpb", [DEPTH, 6, 15, 31])
        self.g_out_c = self.din("g_out_c", [DEPTH, 384])
        self.w_out = self.din("w_out", [DEPTH, D, D])
        self.w_gu = self.din("w_gu", [DEPTH, D, 2 * DFF])
        self.w_down = self.din("w_down", [DEPTH, DFF, D])
        self.cpack = self.din("cpack", [128, 7 * 128])
        self.ropet = self.din("ropet", [128, 16 * 64])
        self.qmask = self.din("qmask", [128, 14 * 128])
        self.rsel = self.din("rsel", [33, 102])
        self.ohc = self.din("ohc", [33, 4096])
        self.neghd = self.din("neghd", [128, 128])
        self.tabscr = self.nc.dram_tensor("tabscr", [102, 4096], BF16, kind="Internal").ap()
        self.yp = self.dout("yp", [NPS * TP, D])
        self.ys = self.dout("ys", [TS, D])
        self.oka = self.dout("oka", [NPS, DEPTH, TP, 128])
        self.ova = self.dout("ova", [NPS, DEPTH, TP, 128])
        self.osb = self.dout("osb", [NPS, DEPTH, 2, 4, 64, 64])
        self.okc = self.dout("okc", [NPS, DEPTH, TP, 384])
        self.ovc = self.dout("ovc", [NPS, DEPTH, TP, 384])

        NCOL = 53200
        arena_t = st.enter_context(nc.sbuf_tensor("arena", [128, NCOL], F32))
        self.A = Arena(arena_t[:], NCOL)
        A = self.A
        self.PF = [st.enter_context(nc.psum_tensor("pf%d" % i, [128, 512], F32))[:] for i in range(6)]
        self.PT = [st.enter_context(nc.psum_tensor("pt%d" % i, [128, 1024], BF16))[:] for i in range(2)]

        self.cF = A.alloc([7, 128])
        self.ident_f = self.cF[:, 0, :]
        self.tri = [self.cF[:, 1, :], self.cF[:, 2, :]]
        self.maft = [self.cF[:, 3, :], self.cF[:, 4, :]]
        self.ones_f = self.cF[:, 5, :]
        self.cB = A.alloc([3, 128], BF16)
        self.ident_b = self.cB[:, 0, :]
        self.ones_b = self.cB[:, 1, :]
        self.bd_b = self.cB[:, 2, :]
        self.rope = A.alloc([16, 64])
        self.negh = A.alloc([2, 64], BF16)
        self.dma("pool", self.negh, self.neghd.rearrange("p (a b) -> p a b", a=2), w=["negh"])
        self.dma("sp", self.cF, self.cpack.rearrange("p (a b) -> p a b", a=7), w=["cF"])
        self.dma("sp", self.rope, self.ropet.rearrange("p (a b) -> p a b", a=16), w=["rope"])
        self.cp("dve", self.cB[:, 0, :], self.cF[:, 0, :], r=["cF"], w=["cB"])
        self.cp("dve", self.cB[:, 1, :], self.cF[:, 5, :], r=["cF"], w=["cB"])
        self.cp("dve", self.cB[:, 2, :], self.cF[:, 6, :], r=["cF"], w=["cB"])
        self.CK = ["cF", "cB"]

        self.X = A.alloc([16, D])
        self.modraw = A.alloc([4, 8])
        self.modA = A.alloc([2, 8])
        self.gfm = A.alloc([2, 8])
        self.G1 = A.alloc([D])
        self.G2 = A.alloc([D])
        self.cfm = A.alloc([8])
        self.srep = A.alloc([8, 128], BF16)
        self.small = A.alloc([64])
        self.base_mark = A.mark()

        for jn in self.jobs:
            J = Job()
            J.name = jn
            if jn == "p":
                J.nt, J.T, J.nseq, J.ci, J.latent = 8, TP, NPS, 0, False
                J.xin, J.yout = self.xp, self.yp
            else:
                J.nt, J.T, J.nseq, J.ci, J.latent = 16, TS, 1, 1, True
                J.xin, J.yout = self.xs, self.ys
            self.run_job(J)

        self.S.emit(nc, st)
        return nc

    def run_job(self, J):
        S = self.S
        A = self.A
        S.barrier()
        xin = J.xin.rearrange("(t p) d -> p t d", p=128)
        for t in range(J.nt):
            self.dma("sp", self.X[:, t, :], xin[:, t, :], w=["X%d" % t])
        self.dma("sp", self.cfm, self.cvec[J.ci:J.ci + 1, :].rearrange("o (k p) -> p (o k)", p=128),
                 w=["cfm"], slow=True)
        sil = self.small[:, 0:8]
        self.act(sil, self.cfm, AF.Silu, r=["cfm"], w=["small"])
        self.cp("dve", self.srep, sil.unsqueeze(2).to_broadcast([128, 8, 128]), r=["small"], w=["srep"])
        for l in range(self.nl):
            S.epoch = (J.name, l)
            if self.stop == "load":
                break
            self.mod(l, J)
            if self.stop == "mod":
                break
            self.phase_b(l, J)
            if self.stop == "b":
                break
            self.phase_kvq(l, J)
            if self.stop == "kvq":
                break
            self.phase_f(l, J)
            self.tap("x2_%s%d" % (J.name, l), self.X[:, 0:J.nt, :], r=["X%d" % t for t in range(J.nt)])
        yout = J.yout.rearrange("(t p) d -> p t d", p=128)
        for t in range(J.nt):
            self.store(yout[:, t, :], self.X[:, t, :], r=["X%d" % t])

    def mod(self, l, J):
        self.S.phase = "mod"
        PF = self.PF
        self.S.barrier()
        self.A.release(self.base_mark)
        self.rowbuf = self.A.alloc([512])
        self.bbc = [self.A.alloc([512]) for _ in range(2)]
        self.wm = [self.A.alloc([8, 512], BF16) for _ in range(4)]
        wmv = self.w_mod[l].rearrange("(k p) n -> p k n", p=128)
        gn = self.g_norm[l]
        self.dma("sp", self.G1, gn[1:2, :].partition_broadcast(128), w=["G1"])
        self.dma("sp", self.G2, gn[3:4, :].partition_broadcast(128), w=["G2"])
        self.dma("sp", self.gfm[:, 0, :], gn[0:1, :].rearrange("o (k p) -> p (o k)", p=128), w=["gfm"], slow=True)
        self.dma("sp", self.gfm[:, 1, :], gn[2:3, :].rearrange("o (k p) -> p (o k)", p=128), w=["gfm"], slow=True)
        rawidx = {0: 0, 1: 1, 3: 2, 4: 3}
        for piece in range(12):
            s = piece % 2
            v, half = piece // 2, piece % 2
            ws = piece % 4
            self.dma("pool", self.wm[ws], wmv[:, :, piece * 512:(piece + 1) * 512], w=["wm%d" % ws])
            self.dma("sp", self.bbc[s], self.b_mod[l:l + 1, piece * 512:(piece + 1) * 512].partition_broadcast(128),
                     w=["bbc%d" % s])
            ps = PF[s]
            for k in range(8):
                self.mm(ps, self.srep[:, k, :], self.wm[ws][:, k, :], k == 0, k == 7,
                        r=["srep", "wm%d" % ws], w=["PF%d" % s])
            if v in (2, 5):
                G = self.G1 if v == 2 else self.G2
                gk = "G1" if v == 2 else "G2"
                gs = G[:, half * 512:(half + 1) * 512]
                self.tt("dve", self.bbc[s], ps, self.bbc[s], ALU.add, r=["PF%d" % s, "bbc%d" % s], w=["bbc%d" % s])
                self.tt("dve", gs, gs, self.bbc[s], ALU.mult, r=["bbc%d" % s, gk], w=[gk])
            elif _DBG.get("MODTEST") == "1":
                self.cp("dve", self.rowbuf[0:1, :], ps[0:1, :], r=["PF%d" % s], w=["rowbuf"])
            else:
                self.tt("dve", self.rowbuf[0:1, :], ps[0:1, :], self.bbc[s][0:1, :], ALU.add,
                        r=["PF%d" % s, "bbc%d" % s], w=["rowbuf"])
                pm = PF[2][:, 0:4]
                for j in range(4):
                    self.mm(pm[:, j:j + 1], self.rowbuf[0:1, j * 128:(j + 1) * 128], self.ones_f[0:1, 0:1],
                            True, True, r=["rowbuf", "cF"], w=["PF2"])
                self.cp("dve", self.modraw[:, rawidx[v], half * 4:(half + 1) * 4], pm, r=["PF2"], w=["modraw"])
        self.stt("dve", self.modA[:, 0, :], self.modraw[:, 1, :], 1.0, self.gfm[:, 0, :], ALU.add, ALU.mult,
                 r=["modraw", "gfm"], w=["modA"])
        self.stt("dve", self.modA[:, 1, :], self.modraw[:, 3, :], 1.0, self.gfm[:, 1, :], ALU.add, ALU.mult,
                 r=["modraw", "gfm"], w=["modA"])
        self.tap("G1_%s%d" % (J.name, l), self.G1, r=["G1"])
        self.tap("modraw_%s%d" % (J.name, l), self.modraw, r=["modraw"])

    def make_hT(self, J, tiles, which, dst, dkey, xn, tmpf):
        PT0 = self.PT[0]
        Avec = self.modA[:, which, :]
        shv = self.modraw[:, 0 if which == 0 else 2, :]
        for j, t in enumerate(tiles):
            xk = "X%d" % t
            ss = self.small[:, 8:9]
            rs = self.small[:, 9:10]
            self.act(xn, self.X[:, t, :], AF.Square, r=[xk], w=["xn", "small"], accum=ss)
            self.rstd(rs, ss, 1.0 / D, r=["small"], w=["small"])
            self.act(xn, self.X[:, t, :], AF.Copy, r=[xk, "small"], w=["xn"], scale=rs)
            for k in range(8):
                self.tr(PT0[:, k * 128:(k + 1) * 128], xn[:, k * 128:(k + 1) * 128], r=["xn", "cB"], w=["PT0"])
            pv = PT0.rearrange("p (k c) -> p k c", k=8)
            tv = tmpf.rearrange("p (k c) -> p k c", k=8)
            self.tt("dve", tv, pv, Avec.unsqueeze(2).to_broadcast([128, 8, 128]), ALU.mult,
                    r=["PT0", "modA"], w=["tmpf"])
            self.tt("pool", dst[:, :, j * 128:(j + 1) * 128], tv, shv.unsqueeze(2).to_broadcast([128, 8, 128]),
                    ALU.add, r=["tmpf", "modraw"], w=[dkey])

    def hT_pre(self, t, xn, xnk, sc0):
        xk = "X%d" % t
        ss = self.small[:, sc0:sc0 + 1]
        rs = self.small[:, sc0 + 1:sc0 + 2]
        self.act(xn, self.X[:, t, :], AF.Square, r=[xk], w=[xnk, "small"], accum=ss)
        self.rstd(rs, ss, 1.0 / D, r=["small"], w=["small"])
        self.act(xn, self.X[:, t, :], AF.Copy, r=[xk, "small"], w=[xnk], scale=rs)

    def hT_post(self, which, dst, dkey, j, xn, xnk, tmpf):
        PT0 = self.PT[0]
        Avec = self.modA[:, which, :]
        shv = self.modraw[:, 0 if which == 0 else 2, :]
        for k in range(8):
            self.tr(PT0[:, k * 128:(k + 1) * 128], xn[:, k * 128:(k + 1) * 128], r=[xnk, "cB"], w=["PT0"])
        for k in range(8):
            self.act(dst[:, k, j * 128:(j + 1) * 128], PT0[:, k * 128:(k + 1) * 128], AF.Identity,
                     r=["PT0", "modA", "modraw"], w=[dkey], scale=Avec[:, k:k + 1], bias=shv[:, k:k + 1])

    def load_w(self, dst, src2d, c0, c1, key, kchunks=8):
        v = src2d.rearrange("(k p) n -> p k n", p=128)
        return self.dma("pool", dst, v[:, :, c0:c1], w=[key])

    def phase_b(self, l, J):
        S = self.S
        A = self.A
        PF, PT = self.PF, self.PT
        S.barrier()
        A.release(self.base_mark)
        nt, T = J.nt, J.T
        ntq = T // 128
        self.YB = A.alloc([nt, 256], BF16)
        self.yb_mark = A.mark()
        WB = A.alloc([8, 1040], BF16)
        self.qm = A.alloc([14, 128], BF16)
        self.dma("pool", self.qm, self.qmask.rearrange("p (a b) -> p a b", a=14), w=["qm"])
        BT = A.alloc([6, T], BF16)
        GT = A.alloc([ntq, 256], BF16)
        OB = A.alloc([ntq, 256])
        bet = A.alloc([ntq, 8])
        nbet = A.alloc([ntq, 8])
        gl = A.alloc([ntq, 8])
        if J.nseq == 1:
            ctmp = WB.rearrange("p k c -> p (k c)").bitcast(F32)[:, 0:T]
        else:
            ctmp = A.alloc([T])
        hT0 = A.alloc([8, 128], BF16)
        xn = A.alloc([D], BF16)
        tmpf = A.alloc([D])
        cw = A.alloc([3, 6])
        dtb = A.alloc([8])
        nal = A.alloc([8])
        gon = A.alloc([64])
        if J.nseq == 1:
            wbf = WB.rearrange("p k c -> p (k c)").bitcast(F32)
            sq = wbf[:, 2048:2304].bitcast(BF16)
            rsb = wbf[:, 2304:2816]
        else:
            sq = A.alloc([512], BF16)
            rsb = A.alloc([512])
        def two(shape, dt=F32):
            return [A.alloc(shape, dt) for _ in range(2)]

        def one(shape, dt=F32):
            a = A.alloc(shape, dt)
            return [a, a]
        KVt = two([512], BF16)
        Gm = one([4, 128])
        dec = two([4, 128])
        decT = two([4, 128])
        if J.nseq == 1:
            hT1b = dec[0].rearrange("p h j -> p (h j)").bitcast(BF16).rearrange("p (k c) -> p k c", k=8)
            xn2 = dec[1].rearrange("p h j -> p (h j)").bitcast(BF16)
        else:
            hT1b = A.alloc([8, 128], BF16)
            xn2 = A.alloc([D], BF16)
        hT = [hT0, hT1b]
        xnb = [xn, xn2]
        Nb = two([4, 128], BF16)
        NTb = two([4, 128], BF16)
        Xb = two([4, 128], BF16)
        Yb = two([4, 128], BF16)
        Zb = two([4, 128], BF16)
        Zc2 = two([4, 128], BF16)
        Pm = one([4, 128], BF16)
        Qm = one([4, 128], BF16)
        ATb = two([4, 128], BF16)
        vb = two([4, 64], BF16)
        kbg = two([4, 64], BF16)
        ktil = two([4, 64], BF16)
        U = two([4, 64])
        WT = two([2, 128], BF16)
        vnew = two([4, 64], BF16)
        tmpo = two([4, 64])
        gsm = two([32])
        Sf = A.alloc([2, 64])
        Sb = A.alloc([2, 64], BF16)

        self.load_w(WB, self.w_in[l], C_BQKV, C_BQKV + 1040, "WB")
        for jc in range(3):
            self.dma("sp", cw[:, jc, :], self.conv_w[l, jc:jc + 1, :].rearrange("o (b p) -> p (o b)", p=128),
                     w=["cw"], slow=True)
        self.dma("sp", dtb, self.dt_bias[l:l + 1, :].partition_broadcast(128), w=["dtb"])
        self.dma("sp", nal, self.a_log[l:l + 1, :].partition_broadcast(128), w=["nal"])
        self.dma("sp", gon, self.g_onorm_b[l:l + 1, :].partition_broadcast(128), w=["gon"])
        self.ts("dve", gon, gon, 0.125, None, ALU.mult, r=["gon"], w=["gon"])
        self.act(nal, nal, AF.Exp, r=["nal"], w=["nal"])
        self.ts("dve", nal, nal, -1.0, None, ALU.mult, r=["nal"], w=["nal"])

        for s in range(J.nseq):
            t0 = s * ntq
            self.S.phase = "b1_proj"
            self.hT_pre(t0, xnb[0], "xnb0", 48)
            self.hT_post(0, hT[0], "hTb0", 0, xnb[0], "xnb0", tmpf)
            for j in range(ntq):
                t = t0 + j
                h = hT[j % 2]
                hk = "hTb%d" % (j % 2)
                q2 = (j + 1) % 2
                if j + 1 < ntq:
                    self.hT_pre(t + 1, xnb[q2], "xnb%d" % q2, 48 + 2 * q2)
                for b in range(6):
                    if b == 3 and j + 1 < ntq:
                        self.hT_post(0, hT[q2], "hTb%d" % q2, 0, xnb[q2], "xnb%d" % q2, tmpf)
                    ps = PF[b % 2][:, 0:128]
                    pk = "PF%d" % (b % 2)
                    for k in range(8):
                        self.mm(ps, WB[:, k, b * 128:(b + 1) * 128], h[:, k, :], k == 0, k == 7,
                                r=["WB", hk], w=[pk])
                    self.cp("act", BT[:, b, j * 128:(j + 1) * 128], ps, r=[pk], w=["BT"])
                pg = PF[2][:, 0:272]
                for k in range(8):
                    self.mm(pg, h[:, k, :], WB[:, k, 768:1040], k == 0, k == 7, r=["WB", hk], w=["PF2"])
                e1 = tmpf[:, 0:256]
                self.act(e1, pg[:, 0:256], AF.Exp, r=["PF2"], w=["tmpf"], scale=-1.0)
                self.ts("dve", e1, e1, 1.0, None, ALU.add, r=["tmpf"], w=["tmpf"])
                self.add("dve", lambda e, a=e1: e.reciprocal(out=a, in_=a), ["tmpf"], ["tmpf"])
                self.tt("dve", GT[:, j, :], e1, pg[:, 0:256], ALU.mult, r=["tmpf", "PF2"], w=["GT"])
                e2 = tmpf[:, 256:264]
                self.act(e2, pg[:, 256:264], AF.Exp, r=["PF2"], w=["tmpf"], scale=-1.0)
                self.ts("dve", e2, e2, 1.0, None, ALU.add, r=["tmpf"], w=["tmpf"])
                self.add("dve", lambda e, a=e2, o=bet[:, j, :]: e.reciprocal(out=o, in_=a), ["tmpf"], ["bet"])
                self.ts("dve", nbet[:, j, :], bet[:, j, :], -1.0, None, ALU.mult, r=["bet"], w=["nbet"])
                e3 = tmpf[:, 264:272]
                self.tt("dve", e3, pg[:, 264:272], dtb, ALU.add, r=["PF2", "dtb"], w=["tmpf"])
                self.act(e3, e3, AF.Exp, r=["tmpf"], w=["tmpf"])
                self.act(e3, e3, AF.Ln, r=["tmpf"], w=["tmpf"], bias=1.0)
                self.tt("dve", gl[:, j, :], e3, nal, ALU.mult, r=["tmpf", "nal"], w=["gl"])
            if _DBG.get("BSTOP") == "1":
                continue
            if J.nseq == 1:
                S.barrier()
            self.S.phase = "b2_conv"
            for b in range(6):
                src = BT[:, b, :]
                self.ts("dve", ctmp, src, cw[:, 1, b:b + 1], None, ALU.mult, r=["BT", "cw"], w=["ctmp"])
                self.stt("dve", ctmp[:, 1:T], src[:, 0:T - 1], cw[:, 0, b:b + 1], ctmp[:, 1:T], ALU.mult, ALU.add,
                         r=["BT", "cw", "ctmp"], w=["ctmp"])
                self.stt("dve", ctmp[:, 0:T - 1], src[:, 1:T], cw[:, 2, b:b + 1], ctmp[:, 0:T - 1], ALU.mult, ALU.add,
                         r=["BT", "cw", "ctmp"], w=["ctmp"])
                CW = min(512, T)
                for c0 in range(0, T, CW):
                    cs = slice(c0, c0 + CW)
                    ex = rsb[:, 0:CW]
                    self.act(ex, ctmp[:, cs], AF.Exp, r=["ctmp"], w=["rsb"], scale=-1.0)
                    self.ts("dve", ex, ex, 1.0, None, ALU.add, r=["rsb"], w=["rsb"])
                    self.add("dve", lambda e, a=ex: e.reciprocal(out=a, in_=a), ["rsb"], ["rsb"])
                    if b >= 4:
                        self.tt("pool", BT[:, b, cs], ctmp[:, cs], ex, ALU.mult, r=["ctmp", "rsb"], w=["BT"])
                    else:
                        self.tt("pool", ctmp[:, cs], ctmp[:, cs], ex, ALU.mult, r=["ctmp", "rsb"], w=["ctmp"])
                        self.tt("pool", sq[:, 0:CW], ctmp[:, cs], ctmp[:, cs], ALU.mult, r=["ctmp"], w=["sq"])
                        pn = PF[3][:, 0:CW]
                        self.mm(pn, self.bd_b, sq[:, 0:CW], True, True, r=["sq", "cB"], w=["PF3"])
                        self.act(ex, pn, AF.Ln, r=["PF3"], w=["rsb"], bias=EPS)
                        self.act(ex, ex, AF.Exp, r=["rsb"], w=["rsb"], scale=-0.5)
                        self.tt("dve", BT[:, b, cs], ctmp[:, cs], ex, ALU.mult, r=["ctmp", "rsb"], w=["BT"])
            if "bq" in self.dbg and s == 0:
                self.tap("bqkv", BT[:, :, 0:256], r=["BT"])
                self.tap("bgl", gl[:, 0:2, :], r=["gl"])
                self.tap("bbeta", bet[:, 0:2, :], r=["bet"])
            if _DBG.get("BSTOP") == "2":
                continue
            self.S.phase = "b3_chunks"
            for r in range(2):
                if J.latent:
                    self.dma("sp", Sf, self.sb0[l, r].rearrange("(j i) k v -> (i k) j v", i=2), w=["Sf"])
                else:
                    self.memset("pool", Sf, 0.0, w=["Sf"])
                self.cp("act", Sb, Sf, r=["Sf"], w=["Sb"])
                order = range(ntq) if r == 0 else range(ntq - 1, -1, -1)
                def chunk_gen(ci, c, p):
                    ba, bak = (PF[1], "PF1") if p == 0 else (PF[3], "PF3")
                    bb, bbk = (PF[2], "PF2") if p == 0 else (PF[4], "PF4")
                    sfx = "_%d" % p
                    cols = slice(c * 128, (c + 1) * 128)
                    for i, b in enumerate((2, 3, 4, 5)):
                        self.tr(PT[0][:, i * 128:(i + 1) * 128], BT[:, b, cols], r=["BT", "cB"], w=["PT0"])
                    self.cp("act", KVt[p], PT[0][:, 0:512], r=["PT0"], w=["KVt" + sfx])
                    ktok = KVt[p][:, 0:256].rearrange("p (h d) -> p h d", h=4)
                    vtok = KVt[p][:, 256:512].rearrange("p (h d) -> p h d", h=4)
                    g4 = gl[:, c, r * 4:(r + 1) * 4]
                    b4 = bet[:, c, r * 4:(r + 1) * 4]
                    nb4 = nbet[:, c, r * 4:(r + 1) * 4]
                    pg = PF[0][:, 0:8]
                    self.mm(pg[:, 0:4], self.tri[r], g4, True, True, r=["gl", "cF"], w=["PF0"])
                    self.mm(pg[:, 4:8], self.ones_f, g4, True, True, r=["gl", "cF"], w=["PF0"])
                    gs = gsm[p]
                    gk = "gsm" + sfx
                    gc, egc, eglast, dgl, ekt, bg = (gs[:, 0:4], gs[:, 4:8], gs[:, 8:12], gs[:, 12:16],
                                                     gs[:, 16:20], gs[:, 20:24])
                    self.cp("dve", gc, pg[:, 0:4], r=["PF0"], w=[gk])
                    self.act(egc, pg[:, 0:4], AF.Exp, r=["PF0"], w=[gk])
                    self.act(eglast, pg[:, 4:8], AF.Exp, r=["PF0"], w=[gk])
                    self.tt("dve", dgl, pg[:, 4:8], gc, ALU.subtract, r=["PF0", gk], w=[gk])
                    self.act(ekt, dgl, AF.Exp, r=[gk], w=[gk])
                    self.tt("dve", bg, b4, egc, ALU.mult, r=["bet", gk], w=[gk])
                    yield None
                    self.tt("dve", Gm[p], self.maft[r].unsqueeze(1).to_broadcast([128, 4, 128]),
                            g4.unsqueeze(2).to_broadcast([128, 4, 128]), ALU.mult, r=["cF", "gl"], w=["Gm"])
                    self.mm(ba, self.tri[r], Gm[p].rearrange("p h j -> p (h j)"), True, True,
                            r=["Gm", "cF"], w=[bak])
                    for h in range(4):
                        self.mm(bb[:, h * 128:(h + 1) * 128], Gm[p][:, h, :], self.tri[r], True, True,
                                r=["Gm", "cF"], w=[bbk])
                    d2 = dec[p].rearrange("p h j -> p (h j)")
                    dT2 = decT[p].rearrange("p h j -> p (h j)")
                    self.act(d2, ba, AF.Exp, r=[bak], w=["dec" + sfx])
                    self.act(dT2, bb, AF.Exp, r=[bbk], w=["decT" + sfx])
                    yield None
                    for h in range(4):
                        rows = slice((h % 2) * 64, (h % 2) * 64 + 64)
                        kT_h = BT[rows, 2 + h // 2, cols]
                        qT_h = BT[rows, h // 2, cols]
                        bank, bk = (ba, bak) if h % 2 == 0 else (bb, bbk)
                        self.mm(bank[:, (h // 2) * 128:(h // 2) * 128 + 128], kT_h, kT_h, True, True, r=["BT"], w=[bk])
                        self.mm(bank[:, 256 + (h // 2) * 128:256 + (h // 2) * 128 + 128], kT_h, qT_h, True, True,
                                r=["BT"], w=[bk])
                    yield None
                    mstrict = self.maft[r].unsqueeze(1).to_broadcast([128, 4, 128])
                    minclT = self.tri[r].unsqueeze(1).to_broadcast([128, 4, 128])
                    self.tt(POOLENG, dec[p], dec[p], mstrict, ALU.mult, r=["dec" + sfx, "cF"], w=["dec" + sfx])
                    self.tt(POOLENG, dec[p], dec[p], nb4.unsqueeze(2).to_broadcast([128, 4, 128]), ALU.mult,
                            r=["dec" + sfx, "nbet"], w=["dec" + sfx])
                    for par, (bank, bk) in enumerate(((ba, bak), (bb, bbk))):
                        self.tt("dve", Nb[p][:, par::2, :], bank[:, 0:256].rearrange("p (h j) -> p h j", h=2),
                                dec[p][:, par::2, :], ALU.mult, r=[bk, "dec" + sfx], w=["Nb" + sfx])
                    self.tt(POOLENG, decT[p], decT[p], minclT, ALU.mult, r=["decT" + sfx, "cF"], w=["decT" + sfx])
                    for par, (bank, bk) in enumerate(((ba, bak), (bb, bbk))):
                        self.tt("dve", ATb[p][:, par::2, :], bank[:, 256:512].rearrange("p (h j) -> p h j", h=2),
                                decT[p][:, par::2, :], ALU.mult, r=[bk, "decT" + sfx], w=["ATb" + sfx])
                    yield None
                    for h in range(4):
                        self.tr(PT[1][:, h * 128:(h + 1) * 128], Nb[p][:, h, :], r=["Nb" + sfx, "cB"], w=["PT1"])
                    self.cp("act", NTb[p].rearrange("p h j -> p (h j)"), PT[1][:, 0:512], r=["PT1"], w=["NTb" + sfx])
                    QT_ = lambda lv: self.qm[:, (0 if r == 0 else 7) + lv, :].unsqueeze(1).to_broadcast([128, 4, 128])
                    QZ_ = lambda lv: self.qm[:, (7 if r == 0 else 0) + lv, :].unsqueeze(1).to_broadcast([128, 4, 128])
                    idb = self.ident_b.unsqueeze(1).to_broadcast([128, 4, 128])
                    Tc, Tn_, tck, tnk = Xb[p], Yb[p], "Xb" + sfx, "Yb" + sfx
                    Zc, Zn_, zck, znk = Zb[p], Zc2[p], "Zb" + sfx, "Zc2" + sfx
                    self.tt("pool", Pm[p], Nb[p], QT_(0), ALU.mult, r=["Nb" + sfx, "qm"], w=["Pm"])
                    self.tt("pool", Tc, Pm[p], idb, ALU.add, r=["Pm", "cB"], w=[tck])
                    self.tt("pool", Qm[p], NTb[p], QZ_(0), ALU.mult, r=["NTb" + sfx, "qm"], w=["Qm"])
                    self.tt("pool", Zc, Qm[p], idb, ALU.add, r=["Qm", "cB"], w=[zck])
                    yield None
                    for lv in range(1, 7):
                        for h in range(4):
                            self.mm(ba[:, h * 128:(h + 1) * 128], NTb[p][:, h, :], Tc[:, h, :], True, True,
                                    r=["NTb" + sfx, tck], w=[bak])
                        for h in range(4):
                            self.mm(bb[:, h * 128:(h + 1) * 128], Nb[p][:, h, :], Zc[:, h, :], True, True,
                                    r=["Nb" + sfx, zck], w=[bbk])
                        self.tt("dve", Pm[p], ba.rearrange("p (h j) -> p h j", h=4), QT_(lv), ALU.mult,
                                r=[bak, "qm"], w=["Pm"])
                        self.tt("dve", Qm[p], bb.rearrange("p (h j) -> p h j", h=4), QZ_(lv), ALU.mult,
                                r=[bbk, "qm"], w=["Qm"])
                        if lv < 6:
                            for h in range(4):
                                self.mm(ba[:, h * 128:(h + 1) * 128], Zc[:, h, :], Pm[p][:, h, :], True, True,
                                        r=[zck, "Pm"], w=[bak])
                        for h in range(4):
                            self.mm(bb[:, h * 128:(h + 1) * 128], Tc[:, h, :], Qm[p][:, h, :], True, True,
                                    r=[tck, "Qm"], w=[bbk])
                        if lv < 6:
                            self.tt("dve", Tn_, ba.rearrange("p (h j) -> p h j", h=4), Tc, ALU.add,
                                    r=[bak, tck], w=[tnk])
                        self.tt("dve", Zn_, bb.rearrange("p (h j) -> p h j", h=4), Zc, ALU.add,
                                r=[bbk, zck], w=[znk])
                        Tc, Tn_, tck, tnk = Tn_, Tc, tnk, tck
                        Zc, Zn_, zck, znk = Zn_, Zc, znk, zck
                        yield None
                    Zf, zfk = Zc, zck
                    yield None
                    self.tt("pool", vb[p], vtok, b4.unsqueeze(2).to_broadcast([128, 4, 64]), ALU.mult,
                            r=["KVt" + sfx, "bet"], w=["vb" + sfx])
                    self.tt("pool", kbg[p], ktok, bg.unsqueeze(2).to_broadcast([128, 4, 64]), ALU.mult,
                            r=["KVt" + sfx, gk], w=["kbg" + sfx])
                    self.tt("pool", ktil[p], ktok, ekt.unsqueeze(2).to_broadcast([128, 4, 64]), ALU.mult,
                            r=["KVt" + sfx, gk], w=["ktil" + sfx])
                    for h in range(4):
                        self.mm(PF[0][:, h * 64:(h + 1) * 64], Zf[:, h, :], vb[p][:, h, :], True, True,
                                r=[zfk, "vb" + sfx], w=["PF0"])
                    for h in range(4):
                        rows = slice((h % 2) * 64, (h % 2) * 64 + 64)
                        self.mm(PF[5][rows, (h // 2) * 128:(h // 2) * 128 + 128], kbg[p][:, h, :], Zf[:, h, :],
                                True, True, r=[zfk, "kbg" + sfx], w=["PF5"])
                    self.cp("act", U[p].rearrange("p h d -> p (h d)"), PF[0][:, 0:256], r=["PF0"], w=["U" + sfx])
                    self.cp("act", WT[p].rearrange("p h j -> p (h j)"), PF[5][:, 0:256], r=["PF5"], w=["WT" + sfx])
                    egl2 = gs[:, 24:26]
                    self.cp("dve", egl2[0:64, :], eglast[0:64, 0:4:2], r=[gk], w=[gk])
                    self.cp("dve", egl2[64:128, :], eglast[64:128, 1:4:2], r=[gk], w=[gk])
                    if s == 0 and r == 1 and ci == 0 and "chk" in self.dbg:
                        self.dbg |= {"c_dec", "c_N", "c_AT", "c_Z", "c_U", "c_WT", "c_gs", "c_ktil", "c_vb"}
                        self.tap("c_dec", dec[p], r=["dec" + sfx])
                        self.tap("c_AT", ATb[p], r=["ATb" + sfx])
                        self.tap("c_Z", Zf, r=[zfk])
                        self.tap("c_U", U[p], r=["U" + sfx])
                        self.tap("c_WT", WT[p], r=["WT" + sfx])
                        self.tap("c_gs", gs[:, 0:26], r=[gk])
                        self.tap("c_ktil", ktil[p], r=["ktil" + sfx])
                        self.tap("c_vb", vb[p], r=["vb" + sfx])
                    yield "REC"
                    for h in range(4):
                        rows = slice((h % 2) * 64, (h % 2) * 64 + 64)
                        bank, bk = (PF[4][:, 0:128], "PF4") if h % 2 == 0 else (PF[5][:, 256:384], "PF5")
                        self.mm(bank[:, (h // 2) * 64:(h // 2) * 64 + 64], WT[p][rows, h // 2, :], Sb[rows, h // 2, :],
                                True, True, r=["WT" + sfx, "Sb"], w=[bk])
                    for par, (bank, bk) in enumerate(((PF[4][:, 0:128], "PF4"), (PF[5][:, 256:384], "PF5"))):
                        self.tt("dve", vnew[p][:, par::2, :], U[p][:, par::2, :],
                                bank.rearrange("p (h d) -> p h d", h=2), ALU.subtract,
                                r=["U" + sfx, bk], w=["vnew" + sfx])
                    for h in range(4):
                        rows = slice((h % 2) * 64, (h % 2) * 64 + 64)
                        bank, bk = (PF[0][:, 0:128], "PF0") if h % 2 == 0 else (PF[1][:, 0:128], "PF1")
                        self.mm(bank[:, (h // 2) * 64:(h // 2) * 64 + 64], BT[rows, h // 2, cols], Sb[rows, h // 2, :],
                                True, True, r=["BT", "Sb"], w=[bk])
                    for h in range(4):
                        self.mm(PF[5][:, h * 64:(h + 1) * 64], ATb[p][:, h, :], vnew[p][:, h, :], True, True,
                                r=["ATb" + sfx, "vnew" + sfx], w=["PF5"])
                    for par, (bank, bk) in enumerate(((PF[0][:, 0:128], "PF0"), (PF[1][:, 0:128], "PF1"))):
                        self.tt("dve", tmpo[p][:, par::2, :], bank.rearrange("p (h d) -> p h d", h=2),
                                egc[:, par::2].unsqueeze(2).to_broadcast([128, 2, 64]), ALU.mult,
                                r=[bk, gk], w=["tmpo" + sfx])
                    ob = OB[:, c, :].rearrange("p (h d) -> p h d", h=4)
                    if r == 0:
                        self.tt("dve", ob, PF[5][:, 0:256].rearrange("p (h d) -> p h d", h=4), tmpo[p], ALU.add,
                                r=["PF5", "tmpo" + sfx], w=["OB"])
                    else:
                        self.tt("dve", tmpo[p], PF[5][:, 0:256].rearrange("p (h d) -> p h d", h=4), tmpo[p], ALU.add,
                                r=["PF5", "tmpo" + sfx], w=["tmpo" + sfx])
                        self.tt("pool", ob, ob, tmpo[p], ALU.add, r=["tmpo" + sfx, "OB"], w=["OB"])
                    for h in range(4):
                        rows = slice((h % 2) * 64, (h % 2) * 64 + 64)
                        self.mm(PF[3][rows, (h // 2) * 64:(h // 2) * 64 + 64], ktil[p][:, h, :], vnew[p][:, h, :],
                                True, True, r=["ktil" + sfx, "vnew" + sfx], w=["PF3"])
                    self.tt("pool", Sf, Sf, egl2.unsqueeze(2).to_broadcast([128, 2, 64]), ALU.mult,
                            r=["Sf", gk], w=["Sf"])
                    self.tt("dve", Sf, Sf, PF[3][:, 0:128].rearrange("p (h d) -> p h d", h=2), ALU.add,
                            r=["Sf", "PF3"], w=["Sf"])
                    self.cp("act", Sb, Sf, r=["Sf"], w=["Sb"])
                order = list(order)
                for k0 in range(0, len(order), 2):
                    gens = [chunk_gen(k0 + i, order[k0 + i], i) for i in range(min(2, len(order) - k0))]
                    live = list(gens)
                    while live:
                        for g in list(live):
                            if next(g) == "REC":
                                live.remove(g)
                    for g in gens:
                        for _ in g:
                            pass
                if not J.latent:
                    self.store(self.osb[s, l, r].rearrange("(j i) k v -> (i k) j v", i=2), Sf, r=["Sf"])
            self.S.phase = "b4_out"
            for j0 in range(0, ntq, 4):
                nj = min(4, ntq - j0)
                ob = OB[:, j0:j0 + nj, :]
                ob4 = ob.rearrange("p t (h d) -> p (t h) d", h=4)
                ss = self.small[:, 16:16 + nj * 4]
                rs = self.small[:, 32:32 + nj * 4]
                sq4 = tmpf[:, 0:nj * 256].rearrange("p (a d) -> p a d", d=64)
                self.tt("pool", sq4, ob4, ob4, ALU.mult, r=["OB"], w=["tmpf"])
                self.red("dve", ss, sq4, r=["tmpf"], w=["small"])
                self.rstd(rs, ss, 1.0 / (64.0 * 64.0), r=["small"], w=["small"])
                self.tt("dve", sq4, ob4, rs.unsqueeze(2).to_broadcast([128, nj * 4, 64]), ALU.mult,
                        r=["OB", "small"], w=["tmpf"])
                self.tt("pool", sq4, sq4, gon.unsqueeze(1).to_broadcast([128, nj * 4, 64]), ALU.mult,
                        r=["tmpf", "gon"], w=["tmpf"])
                self.tt("pool", self.YB[:, t0 + j0:t0 + j0 + nj, :].rearrange("p t c -> p (t c)"),
                        tmpf[:, 0:nj * 256], GT[:, j0:j0 + nj, :].rearrange("p t c -> p (t c)"), ALU.mult,
                        r=["tmpf", "GT"], w=["YB"])
            if s == 0:
                self.tap("ob_%s%d" % (J.name, l), OB[:, 0:2, :], r=["OB"])
        self.tap("yb_%s%d" % (J.name, l), self.YB, r=["YB"])

    def rope_apply(self, eng, dst, src, t, H, scr, skey, r, w):
        cos = self.rope[:, t, 0:32].unsqueeze(1).to_broadcast([128, H, 32])
        sin = self.rope[:, t, 32:64].unsqueeze(1).to_broadcast([128, H, 32])
        x1, x2 = src[:, :, 0:32], src[:, :, 32:64]
        sc = scr[:, 0:H * 64].rearrange("p (h d) -> p h d", h=H)
        t1, t2 = sc[:, :, 0:32], sc[:, :, 32:64]
        self.tt(eng, t1, x1, cos, ALU.mult, r=r + ["rope"], w=[skey])
        self.tt(eng, t2, x2, sin, ALU.mult, r=r + ["rope"], w=[skey])
        self.tt(eng, dst[:, :, 0:32], t1, t2, ALU.subtract, r=[skey], w=w)
        self.tt(eng, t1, x1, sin, ALU.mult, r=r + ["rope"] + w, w=[skey])
        self.tt(eng, t2, x2, cos, ALU.mult, r=r + ["rope"], w=[skey])
        self.tt(eng, dst[:, :, 32:64], t1, t2, ALU.add, r=[skey], w=w)

    def head_norm(self, src, H, gbc, dst, scr, skey, r, w, sm0):
        sq = scr[:, 0:H * 64].rearrange("p (h d) -> p h d", h=H)
        ss = self.small[:, sm0:sm0 + H]
        rs = self.small[:, sm0 + 8:sm0 + 8 + H]
        self.tt("pool", sq, src, src, ALU.mult, r=r, w=[skey])
        self.red("dve", ss, sq, r=[skey], w=["small"])
        self.rstd(rs, ss, 1.0 / 64.0, r=["small"], w=["small"])
        self.tt("dve", sq, src, rs.unsqueeze(2).to_broadcast([128, H, 64]), ALU.mult, r=r + ["small"], w=[skey])
        self.tt("pool", dst, sq, gbc.unsqueeze(1).to_broadcast([128, H, 64]), ALU.mult, r=[skey, "gqk"], w=w)

    def attn(self, nq, qT, qkey, kts, O, h65, st):
        PF = self.PF
        nj = nq // 128
        n = len(kts)
        slots = []
        SB = (2, 3, 0, 1)
        NE = len(st["E"])
        LA = 3

        def scores(i):
            kT, V, bias, rk = kts[i]
            c = st["cnt"]
            st["cnt"] += 1
            bi_ = SB[c % 4]
            sp = PF[bi_][:, 0:nq]
            spk = "PF%d" % bi_
            self.mm(sp, kT, qT, True, bias is None, r=rk + [qkey], w=[spk])
            if bias is not None:
                for bi, (qb, bap) in enumerate(bias):
                    self.mm(sp[:, qb * 64:(qb + 1) * 64], self.ident_b, bap, False, bi == len(bias) - 1,
                            r=["BB2", "cB"], w=[spk])
            slots.append((c, sp, spk))

        for i in range(min(LA, n)):
            scores(i)
        for i in range(n):
            if i + LA < n:
                scores(i + LA)
            kT, V, bias, rk = kts[i]
            c, sp, spk = slots[i]
            E = st["E"][c % NE][:, 0:nq]
            ek = "E%d" % (c % NE)
            self.act(E, sp, AF.Exp, r=[spk], w=[ek], scale=0.125)
            for jj in range(nj):
                self.mm(O[jj][0][:, h65 * 65:h65 * 65 + 65], E[:, jj * 128:(jj + 1) * 128], V, i == 0,
                        i == n - 1, r=[ek] + rk, w=[O[jj][1]])

    def attn_out(self, Ops, okey, H, gbc, gkey, ydst, ykey, oscr, oskey, sm0):
        ov = Ops[:, 0:H * 65].rearrange("p (h d) -> p h d", h=H)
        rden = self.small[:, sm0:sm0 + H]
        self.add("dve", lambda e: e.reciprocal(out=rden.unsqueeze(2), in_=ov[:, :, 64:65]), [okey], ["small"])
        o3 = oscr[:, 0:H * 64].rearrange("p (h d) -> p h d", h=H)
        self.tt("dve", o3, ov[:, :, 0:64], rden.unsqueeze(2).to_broadcast([128, H, 64]), ALU.mult,
                r=[okey, "small"], w=[oskey])
        ss = self.small[:, sm0 + 8:sm0 + 9]
        rs = self.small[:, sm0 + 9:sm0 + 10]
        junk = self.junkb[:, 0:H * 64]
        self.act(junk, oscr[:, 0:H * 64], AF.Square, r=[oskey], w=["junkb", "small"], accum=ss)
        self.rstd(rs, ss, 1.0 / (H * 64.0), r=["small"], w=["small"])
        self.stt("dve", ydst, oscr[:, 0:H * 64], rs, gbc, ALU.mult, ALU.mult, r=[oskey, "small", gkey], w=[ykey])

    def residual_epilogue(self, t, Gbc, gkey, tmp, tkey):
        PF = self.PF
        ss = self.small[:, 40:42]
        rs = self.small[:, 42:43]
        junk = self.junkb[:, 0:512]
        self.act(junk, PF[0], AF.Square, r=["PF0"], w=["junkb", "small"], accum=ss[:, 0:1])
        self.act(junk, PF[1], AF.Square, r=["PF1"], w=["junkb", "small"], accum=ss[:, 1:2])
        self.tt("dve", ss[:, 0:1], ss[:, 0:1], ss[:, 1:2], ALU.add, r=["small"], w=["small"])
        self.rstd(rs, ss[:, 0:1], 1.0 / D, r=["small"], w=["small"])
        self.stt("dve", tmp[:, 0:512], PF[0], rs, Gbc[:, 0:512], ALU.mult, ALU.mult, r=["PF0", "small", gkey], w=[tkey])
        self.stt("dve", tmp[:, 512:1024], PF[1], rs, Gbc[:, 512:1024], ALU.mult, ALU.mult,
                 r=["PF1", "small", gkey], w=[tkey])
        self.tt("dve", self.X[:, t, :], self.X[:, t, :], tmp, ALU.add, r=[tkey, "X%d" % t], w=["X%d" % t])

    def phase_kvq(self, l, J):
        S = self.S
        A = self.A
        PF, PT = self.PF, self.PT
        S.barrier()
        A.release(self.yb_mark)
        nt, T = J.nt, J.T
        ntq = T // 128
        nctx = 2 if J.latent else 0
        nkt_seq = ntq + nctx
        NKT = J.nseq * nkt_seq
        Wkv = A.alloc([8, 1024], BF16)
        Wq = A.alloc([8, 768], BF16)
        Wo = Wkv
        kTA = A.alloc([NKT * 128], BF16)
        VA = A.alloc([NKT, 2, 65], BF16)
        kTC = A.alloc([3, NKT * 128], BF16)
        VC = A.alloc([NKT, 6, 65], BF16)
        gqk = A.alloc([2, 64])
        goa = A.alloc([384])
        goc = A.alloc([384])
        hTq = A.alloc([8, 256], BF16)
        hT1 = hTq[:, :, 0:128]
        xn = A.alloc([D], BF16)
        tmpf = A.alloc([D])
        self.junkb = A.alloc([512], BF16)
        ZA0 = A.alloc([256])
        ZA = [ZA0, ZA0]
        ZC0 = A.alloc([768])
        ZC = [ZC0, ZC0]
        knf0 = A.alloc([128])
        knf = [knf0, knf0]
        scr = A.alloc([768])
        kab = A.alloc([128], BF16)
        kcb = A.alloc([384], BF16)
        ZQ = A.alloc([384])
        qab = A.alloc([3, 2, 64], BF16)
        qT_all = A.alloc([6, 256], BF16)
        qTA = qT_all[:, 0:3, :]
        qTC = qT_all[:, 3:6, :]
        xnb = [xn, qT_all.rearrange("p a b -> p (a b)")[:, 0:D]]
        hbuf = [hTq[:, :, 0:128], hTq[:, :, 128:256]]
        Eb = [A.alloc([256], BF16) for _ in range(6)]
        oscr = A.alloc([384])
        qnf = oscr
        ya = [A.alloc([384], BF16) for _ in range(2)]
        yc = [A.alloc([384], BF16) for _ in range(2)]
        yT = A.alloc([8, 128], BF16)
        tmpx = tmpf

        win = self.w_in[l]
        wv = win.rearrange("(k p) n -> p k n", p=128)
        self.dma("pool", Wkv[:, :, 0:256], wv[:, :, C_AK:C_AK + 256], w=["Wkv"])
        self.dma("pool", Wkv[:, :, 256:1024], wv[:, :, C_CK:C_CK + 768], w=["Wkv"])
        self.dma("pool", Wq[:, :, 0:384], wv[:, :, C_AQ:C_AQ + 384], w=["Wq"])
        self.dma("pool", Wq[:, :, 384:768], wv[:, :, C_CQ:C_CQ + 384], w=["Wq"])
        self.dma("sp", gqk.rearrange("p a d -> p (a d)"),
                 self.g_qk_a[l:l + 1].rearrange("o a d -> o (a d)").partition_broadcast(128), w=["gqk"])
        self.dma("sp", goa, self.g_out_a[l:l + 1, :].partition_broadcast(128), w=["goa"])
        self.dma("sp", goc, self.g_out_c[l:l + 1, :].partition_broadcast(128), w=["goc"])
        if J.latent:
            self.kv_scr, self.kv_tmpf = scr, tmpf
            self.build_bias(l)
        self.memset("pool", VA[:, :, :, 64:65], 1.0, w=["VA"])
        self.memset("pool", VC[:, :, :, 64:65], 1.0, w=["VC"])

        if _DBG.get("KSTOP") == "0":
            return
        self.S.phase = "kv"
        self.hT_pre(0, xnb[0], "xnb0", 48)
        self.hT_post(0, hbuf[0], "hTq0", 0, xnb[0], "xnb0", tmpf)
        for t in range(nt):
            s, j = t // ntq, t % ntq
            kt = s * nkt_seq + j
            p2 = t % 2
            hT1 = hbuf[p2]
            for k in range(8):
                self.mm(PF[0], hT1[:, k, :], Wkv[:, k, 0:512], k == 0, k == 7, r=["hTq%d" % p2, "Wkv"], w=["PF0"])
            for k in range(8):
                self.mm(PF[1], hT1[:, k, :], Wkv[:, k, 512:1024], k == 0, k == 7, r=["hTq%d" % p2, "Wkv"], w=["PF1"])
            if t + 1 < nt:
                q2 = (t + 1) % 2
                self.hT_pre(t + 1, xnb[q2], "xnb%d" % q2, 48 + 2 * q2)
                self.hT_post(0, hbuf[q2], "hTq%d" % q2, 0, xnb[q2], "xnb%d" % q2, tmpf)
            if _DBG.get("KSTOP") == "0a":
                continue
            za, zc = ZA[p2], ZC[p2]
            zak, zck = "ZA", "ZC"
            self.cp("act", za, PF[0][:, 0:256], r=["PF0"], w=[zak])
            self.cp(_DBG.get("CPENG", "dve"), zc[:, 0:256], PF[0][:, 256:512], r=["PF0"], w=[zck])
            self.cp("act", zc[:, 256:768], PF[1], r=["PF1"], w=[zck])
            if _DBG.get("KSTOP") == "0d":
                continue
            kn = knf[p2]
            knk = "knf"
            kn3 = kn.rearrange("p (h d) -> p h d", h=2)
            self.head_norm(za[:, 0:128].rearrange("p (h d) -> p h d", h=2), 2, gqk[:, 1, :], kn3, scr, "scr",
                           [zak], [knk], 24)
            if _DBG.get("KSTOP") == "0e":
                continue
            if not J.latent and _DBG.get("KSTOP") == "0c":
                self.cp("pool", kab, kn, r=[knk], w=["kab"])
                continue
            if not J.latent:
                tok = slice(j * 128, (j + 1) * 128)
                self.store(self.oka[s, l, tok, :], kn, r=[knk])
                self.store(self.ova[s, l, tok, :], za[:, 128:256], r=[zak])
                self.store(self.okc[s, l, tok, :], zc[:, 0:384], r=[zck])
                self.store(self.ovc[s, l, tok, :], zc[:, 384:768], r=[zck])
                self.cp("pool", kab, kn, r=[knk], w=["kab"])
            else:
                self.rope_apply("pool", kab.rearrange("p (h d) -> p h d", h=2), kn3, j, 2, scr, "scr", [knk], ["kab"])
            if _DBG.get("KSTOP") == "0b":
                continue
            self.tr(PT[1][:, 0:128], kab, r=["kab", "cB"], w=["PT1"])
            self.cp("pool", VA[:, kt, :, 0:64], za[:, 128:256].rearrange("p (h d) -> p h d", h=2), r=[zak], w=["VA"])
            self.cp("pool", kcb, zc[:, 0:384], r=[zck], w=["kcb"])
            for p in range(3):
                self.tr(PT[1][:, 128 + p * 128:256 + p * 128], kcb[:, p * 128:(p + 1) * 128], r=["kcb", "cB"], w=["PT1"])
            self.cp("act", kTA[:, kt * 128:(kt + 1) * 128], PT[1][:, 0:128], r=["PT1"], w=["kTA"])
            self.cp("dve", kTC[:, :, kt * 128:(kt + 1) * 128], PT[1][:, 128:512].rearrange("p (a b) -> p a b", a=3),
                    r=["PT1"], w=["kTC"])
            self.cp("pool", VC[:, kt, :, 0:64], zc[:, 384:768].rearrange("p (h d) -> p h d", h=6), r=[zck], w=["VC"])
        if J.latent:
            for j in range(2):
                kt = ntq + j
                tok = slice(j * 128, (j + 1) * 128)
                self.dma("pool", kab, self.cak[l, tok, :], w=["kab"])
                self.dma("pool", kcb, self.cck[l, tok, :], w=["kcb"])
                self.dma("pool", VA[:, kt, :, 0:64], self.cav[l, tok, :].rearrange("p (h d) -> p h d", h=2), w=["VA"])
                self.dma("pool", VC[:, kt, :, 0:64], self.ccv[l, tok, :].rearrange("p (h d) -> p h d", h=6), w=["VC"])
                self.tr(PT[1][:, 0:128], kab, r=["kab", "cB"], w=["PT1"])
                for p in range(3):
                    self.tr(PT[1][:, 128 + p * 128:256 + p * 128], kcb[:, p * 128:(p + 1) * 128], r=["kcb", "cB"],
                            w=["PT1"])
                self.cp("act", kTA[:, kt * 128:(kt + 1) * 128], PT[1][:, 0:128], r=["PT1"], w=["kTA"])
                self.cp("dve", kTC[:, :, kt * 128:(kt + 1) * 128],
                        PT[1][:, 128:512].rearrange("p (a b) -> p a b", a=3), r=["PT1"], w=["kTC"])
        self.tap("kTA_%s%d" % (J.name, l), kTA, r=["kTA"])
        self.load_w(Wo, self.w_out[l], 0, D, "Wkv")

        if _DBG.get("KSTOP") == "1":
            return
        S.barrier()
        ast = {"cnt": 0, "E": Eb}
        for g in range(nt // 2):
            tl = [2 * g, 2 * g + 1]
            s = tl[0] // ntq
            self.make_hT(J, tl, 0, hTq, "hTq", xn, tmpf)
            self.S.phase = "q_proj"
            for jj, t in enumerate(tl):
                for k in range(8):
                    self.mm(PF[0][:, 0:384], hTq[:, k, jj * 128:(jj + 1) * 128], Wq[:, k, 0:384], k == 0, k == 7,
                            r=["hTq", "Wq"], w=["PF0"])
                self.cp("act", ZQ, PF[0][:, 0:384], r=["PF0"], w=["ZQ"])
                q3 = qnf.rearrange("p (h d) -> p h d", h=6)
                self.head_norm(ZQ.rearrange("p (h d) -> p h d", h=6), 6, gqk[:, 0, :], q3, scr, "scr", ["ZQ"], ["oscr"], 24)
                qdst = qab.rearrange("q p g d -> q g p d")
                qsrc = qnf.rearrange("q (g p d) -> q g p d", g=2, p=3)
                if J.latent:
                    for gg in range(2):
                        self.rope_apply("pool", qdst[:, gg], qsrc[:, gg], t % ntq, 3, scr, "scr", ["oscr"], ["qab"])
                else:
                    for gg in range(2):
                        self.cp("pool", qdst[:, gg], qsrc[:, gg], r=["oscr"], w=["qab"])
                for p in range(3):
                    self.tr(PT[1][:, jj * 384 + p * 128:jj * 384 + (p + 1) * 128],
                            qab[:, p].rearrange("q g d -> q (g d)"), r=["qab", "cB"], w=["PT1"])
                self.cp("act", qTA[:, :, jj * 128:(jj + 1) * 128],
                        PT[1][:, jj * 384:(jj + 1) * 384].rearrange("p (a b) -> p a b", a=3), r=["PT1"], w=["qTA"])
            for p in range(3):
                ps, pk = (PF[1][:, (p % 2) * 256:(p % 2) * 256 + 256], "PF1") if p < 2 else (PF[0][:, 0:256], "PF0")
                for k in range(8):
                    self.mm(ps, Wq[:, k, 384 + p * 128:384 + (p + 1) * 128], hTq[:, k, :], k == 0, k == 7,
                            r=["hTq", "Wq"], w=[pk])
                self.cp("dve" if p % 2 else "act", qTC[:, p, :], ps, r=[pk], w=["qTC"])
            if _DBG.get("KSTOP") == "2":
                continue
            self.S.phase = "attnA"
            O = [(PF[4], "PF4"), (PF[5], "PF5")]
            for h in range(6):
                p, gk_ = h % 3, h // 3
                rows = slice(gk_ * 64, gk_ * 64 + 64)
                kts = []
                for i in range(nkt_seq):
                    kt = s * nkt_seq + i
                    kts.append((kTA[rows, kt * 128:(kt + 1) * 128], VA[:, kt, gk_, :], None, ["kTA", "VA"]))
                self.attn(256, qTA[rows, p, :], "qTA", kts, O, h, ast)
            for jj in range(2):
                self.attn_out(O[jj][0], O[jj][1], 6, goa, "goa", ya[jj], "ya%d" % jj, oscr, "oscr", 24)
            if _DBG.get("KSTOP") == "3":
                continue
            self.S.phase = "attnC"
            if not J.latent:
                for h in range(6):
                    p = h // 2
                    rows = slice((h % 2) * 64, (h % 2) * 64 + 64)
                    kts = []
                    for i in range(nkt_seq):
                        kt = s * nkt_seq + i
                        kts.append((kTC[rows, p, kt * 128:(kt + 1) * 128], VC[:, kt, h, :], None, ["kTC", "VC"]))
                    self.attn(256, qTC[rows, p, :], "qTC", kts, O, h, ast)
            else:
                self.latent_c(l, J, tl, kTC, VC, qTC, O, ast)
            for jj in range(2):
                self.attn_out(O[jj][0], O[jj][1], 6, goc, "goc", yc[jj], "yc%d" % jj, oscr, "oscr", 24)
            if g == 0:
                self.tap("ya_%s%d" % (J.name, l), ya[0], r=["ya0"])
                self.tap("yc_%s%d" % (J.name, l), yc[0], r=["yc0"])
            if _DBG.get("KSTOP") == "4":
                continue
            self.S.phase = "merge"
            for jj, t in enumerate(tl):
                for kk in range(8):
                    if kk < 3:
                        src, rk = ya[jj][:, kk * 128:(kk + 1) * 128], "ya%d" % jj
                    elif kk < 5:
                        src, rk = self.YB[:, t, (kk - 3) * 128:(kk - 2) * 128], "YB"
                    else:
                        src, rk = yc[jj][:, (kk - 5) * 128:(kk - 4) * 128], "yc%d" % jj
                    self.tr(PT[0][:, kk * 128:(kk + 1) * 128], src, r=[rk, "cB"], w=["PT0"])
                self.cp("act", yT.rearrange("p k c -> p (k c)"), PT[0], r=["PT0"], w=["yT"])
                for hf in range(2):
                    for k in range(8):
                        self.mm(PF[hf], yT[:, k, :], Wo[:, k, hf * 512:(hf + 1) * 512], k == 0, k == 7,
                                r=["yT", "Wkv"], w=["PF%d" % hf])
                self.residual_epilogue(t, self.G1, "G1", tmpx, "tmpf")
        self.tap("x1_%s%d" % (J.name, l), self.X[:, 0:nt, :], r=["X%d" % t for t in range(nt)])

    def build_bias(self, l):
        A = self.A
        PF = self.PF
        BB2 = A.alloc([6, 17, 64], BF16)
        self.BB2 = BB2
        rT = self.kv_scr[:, 512:614]
        ohs = [self.kv_tmpf[:, 0:512], self.kv_tmpf[:, 512:1024]]
        tb = [self.kv_scr[:, 0:256].bitcast(BF16), self.kv_scr[:, 256:512].bitcast(BF16)]
        self.dma("sp", rT[0:33, :], self.rsel, w=["scr"])
        r3 = rT[0:31, :].rearrange("c (h d) -> c h d", h=6)
        for h in range(6):
            self.dma("sp", r3[:, h, 1:16], self.rpb[l, h].rearrange("d c -> c d"), r=["scr"], w=["scr"], slow=True)
        for ch in range(8):
            b2 = ch % 2
            self.dma("sp", ohs[b2][0:33, :], self.ohc[:, ch * 512:(ch + 1) * 512], w=["tmpf"])
            self.mm(PF[0][0:102, :], rT[0:33, :], ohs[b2][0:33, :], True, True, r=["scr", "tmpf"], w=["PF0"])
            self.act(tb[b2][0:102, :], PF[0][0:102, :], AF.Copy, r=["PF0"], w=["scr"], scale=8.0)
            self.dma("sp", self.tabscr[:, ch * 512:(ch + 1) * 512], tb[b2][0:102, :], r=["scr"], w=["tabscr"])
        tv = self.tabscr.rearrange("(h d) (k q) -> k h d q", h=6, k=64)
        for h in range(6):
            self.dma("sp", BB2[0:64, h, :, :], tv[:, h, :, :], r=["tabscr"], w=["BB2"])
            self.dma("sp", BB2[64:128, h, 0:16, :], tv[:, h, 1:17, :], r=["tabscr"], w=["BB2"])
            self.dma("sp", BB2[64:128, h, 16:17, :], tv[:, h, 16:17, :], r=["tabscr"], w=["BB2"])

    def latent_c(self, l, J, tl, kTC, VC, qTC, O, ast):
        clip = lambda v: min(max(v, 0), 24)
        for jj, t in enumerate(tl):
            s0, s1 = clip(2 * t - 4), clip(2 * t + 1 - 4)
            kt_lo, kt_hi = s0 // 2, (s1 + 7) // 2
            for h in range(6):
                p = h // 2
                rows = slice((h % 2) * 64, (h % 2) * 64 + 64)
                kts = []
                for kt in range(kt_lo, kt_hi + 1):
                    dl = kt - t
                    bl = []
                    for qb in range(2):
                        d = 2 * dl + 8 - qb
                        assert 0 <= d <= 16, d
                        bl.append((qb, self.BB2[:, h, d, :]))
                        sq_ = clip(2 * t + qb - 4)
                        for kb in range(2):
                            krow = 2 * kt + kb
                            if not (sq_ <= krow <= sq_ + 7):
                                bl.append((qb, self.negh[:, kb, :]))
                    kts.append((kTC[rows, p, kt * 128:(kt + 1) * 128], VC[:, kt, h, :], bl, ["kTC", "VC"]))
                for c in (16, 17):
                    kts.append((kTC[rows, p, c * 128:(c + 1) * 128], VC[:, c, h, :], None, ["kTC", "VC"]))
                self.attn(128, qTC[rows, p, jj * 128:(jj + 1) * 128], "qTC", kts, [O[jj]], h, ast)

    def phase_f(self, l, J):
        S = self.S
        A = self.A
        PF, PT = self.PF, self.PT
        S.barrier()
        A.release(self.base_mark)
        nt = J.nt
        GF = 4
        Wd = A.alloc([22, D], BF16)
        hT2s = [A.alloc([8, GF * 128], BF16) for _ in range(2)]
        actT = A.alloc([22, GF * 128], BF16)
        ring = [A.alloc([8, 256], BF16) for _ in range(3)]
        xn4 = [A.alloc([D], BF16) for _ in range(GF)]
        tmpf = A.alloc([D])
        self.junkb = A.alloc([512], BF16)
        sg = [A.alloc([512]) for _ in range(2)]
        tmpx = A.alloc([D])
        self.S.phase = "ffn"
        wg = self.w_gu[l].rearrange("(k p) n -> p k n", p=128)
        ng = nt // GF
        for j in range(GF):
            self.hT_pre(j, xn4[j], "xn4_%d" % j, 48 + 2 * j)
            self.hT_post(1, hT2s[0], "hT2_0", j, xn4[j], "xn4_%d" % j, tmpf)
        for g in range(ng):
            tl = list(range(g * GF, (g + 1) * GF))
            hT2 = hT2s[g % 2]
            hk = "hT2_%d" % (g % 2)
            nxt = list(range((g + 1) * GF, (g + 2) * GF)) if g + 1 < ng else []
            for c in range(22):
                if nxt and c == 1:
                    for j, t in enumerate(nxt):
                        self.hT_pre(t, xn4[j], "xn4_%d" % j, 48 + 2 * j)
                if nxt and c in (8, 11, 14, 17):
                    j = (c - 8) // 3
                    self.hT_post(1, hT2s[(g + 1) % 2], "hT2_%d" % ((g + 1) % 2), j, xn4[j], "xn4_%d" % j, tmpf)
                sl = c % 3
                rk = "ring%d" % sl
                self.dma("pool", ring[sl][:, :, 0:128], wg[:, :, c * 128:(c + 1) * 128], w=[rk])
                self.dma("pool", ring[sl][:, :, 128:256], wg[:, :, DFF + c * 128:DFF + (c + 1) * 128], w=[rk])
                if g == 0 and c == 2:
                    self.dma("pool", Wd, self.w_down[l].rearrange("(c p) n -> p c n", p=128), w=["Wd"])
                pg, pu = PF[2 + (c % 2) * 2], PF[3 + (c % 2) * 2]
                pgk, puk = "PF%d" % (2 + (c % 2) * 2), "PF%d" % (3 + (c % 2) * 2)
                for k in range(8):
                    self.mm(pg, ring[sl][:, k, 0:128], hT2[:, k, :], k == 0, k == 7, r=[rk, hk], w=[pgk])
                for k in range(8):
                    self.mm(pu, ring[sl][:, k, 128:256], hT2[:, k, :], k == 0, k == 7, r=[rk, hk], w=[puk])
                s2 = sg[c % 2]
                self.act(s2, pg, AF.Silu, r=[pgk], w=["sg%d" % (c % 2)])
                self.tt("dve", actT[:, c, :], s2, pu, ALU.mult, r=["sg%d" % (c % 2), puk], w=["actT"])
            for jj, t in enumerate(tl):
                for hf in range(2):
                    for c in range(22):
                        self.mm(PF[hf], actT[:, c, jj * 128:(jj + 1) * 128], Wd[:, c, hf * 512:(hf + 1) * 512],
                                c == 0, c == 21, r=["actT", "Wd"], w=["PF%d" % hf])
                self.residual_epilogue(t, self.G2, "G2", tmpx, "tmpx")


def make_consts():
    k = np.arange(128)[:, None]
    i = np.arange(128)[None, :]
    cp = np.zeros((128, 7, 128), np.float32)
    cp[:, 0] = (k == i)
    cp[:, 1] = (k <= i)
    cp[:, 2] = (k >= i)
    cp[:, 3] = (k > i)
    cp[:, 4] = (k < i)
    cp[:, 5] = 1.0
    cp[:, 6] = ((k // 64) == (i // 64))
    t = np.arange(TS)
    row = (t // 64).astype(np.float32)
    col = (t % 64).astype(np.float32)
    inv = (10000.0 ** (-np.arange(16, dtype=np.float32) / 16)).astype(np.float32)
    ang = np.concatenate([row[:, None] * inv, col[:, None] * inv], axis=-1).astype(np.float32)
    tab = np.concatenate([np.cos(ang), np.sin(ang)], axis=-1).astype(np.float32)
    ropet = tab.reshape(16, 128, 64).transpose(1, 0, 2).reshape(128, 16 * 64)
    qm = np.zeros((128, 14, 128), np.float32)
    for lv in range(7):
        b = 2 ** lv
        ll = ((k // (2 * b)) == (i // (2 * b))) & ((k % (2 * b)) >= b) & ((i % (2 * b)) < b)
        qm[:, lv] = ll
        qm[:, 7 + lv] = ll.T
    rsel = np.zeros((33, 6, 17), np.float32)
    rsel[31, :, 1:16] = 1.0
    rsel[32, :, 0] = 1.0
    rsel[32, :, 16] = 1.0
    kc = np.arange(64)[:, None]
    qc = np.arange(64)[None, :]
    cst = np.clip(qc - 8, 0, 48)
    inwin = (kc >= cst) & (kc < cst + 16)
    oh = np.zeros((33, 64, 64), np.float32)
    dcc = kc - qc + 15
    for c in range(31):
        oh[c] = ((dcc == c) & inwin)
    oh[31] = np.where(inwin, 0.0, NEG / 8.0)
    oh[32] = NEG / 8.0
    negh = np.zeros((128, 2, 64), np.float32)
    negh[0:64, 0, :] = NEG * 8.0
    negh[64:128, 1, :] = NEG * 8.0
    return (cp.reshape(128, 7 * 128), np.ascontiguousarray(ropet), qm.reshape(128, 14 * 128),
            rsel.reshape(33, 102), oh.reshape(33, 4096), negh.reshape(128, 128))


_CACHE = {}


def get_nc(nl=DEPTH, jobs=("p", "s"), dbg=(), same=True, stop=None):
    key = (nl, tuple(jobs), tuple(sorted(dbg)), same, stop)
    if key not in _CACHE:
        kb = KB(nl=nl, jobs=jobs, dbg=dbg, same=same, stop=stop)
        nc = kb.build()
        _CACHE[key] = (nc, kb)
    return _CACHE[key]


def make_in_maps(inp):
    f = lambda a: np.ascontiguousarray(np.asarray(a, dtype=np.float32))
    cpack, ropet, qmask, rsel, ohc, neghd = make_consts()
    shared = {
        "w_mod": f(inp["w_mod"]), "b_mod": f(inp["b_mod"]), "g_norm": f(inp["g_norm"]), "w_in": f(inp["w_in"]),
        "g_qk_a": f(inp["g_qk_a"]), "g_out_a": f(inp["g_out_a"]), "conv_w": f(inp["conv_w"]),
        "a_log": f(inp["a_log"]).reshape(DEPTH, 8), "dt_bias": f(inp["dt_bias"]).reshape(DEPTH, 8),
        "g_onorm_b": f(inp["g_onorm_b"]), "rpb": f(inp["rpb"]), "g_out_c": f(inp["g_out_c"]),
        "w_out": f(inp["w_out"]), "w_gu": f(inp["w_gu"]), "w_down": f(inp["w_down"]),
        "cpack": cpack, "ropet": ropet, "qmask": qmask, "rsel": rsel, "ohc": ohc, "neghd": neghd,
    }
    maps = []
    for c in range(8):
        b = c // 4
        m = dict(shared)
        m["xp"] = f(inp["x_prompt"][NPS * c:NPS * (c + 1)]).reshape(NPS * TP, D)
        m["xs"] = f(inp["x_sample"][b])
        m["cak"] = f(inp["cache_a_k"][b]).reshape(DEPTH, 256, 128)
        m["cav"] = f(inp["cache_a_v"][b]).reshape(DEPTH, 256, 128)
        m["sb0"] = f(inp["state_b"][b])
        m["cck"] = f(inp["cache_c_k"][b]).reshape(DEPTH, 256, 384)
        m["ccv"] = f(inp["cache_c_v"][b]).reshape(DEPTH, 256, 384)
        m["cvec"] = np.stack([f(inp["c_ctx"]), f(inp["c"][b])], 0)
        maps.append(m)
    return maps


def kernel(**inputs):
    nc, kb = get_nc()
    maps = make_in_maps(inputs)
    res = run_bass_kernel_spmd(nc, maps, core_ids=list(range(8)))
    R = res.results
    yp = np.concatenate([R[c]["yp"].reshape(NPS, TP, D) for c in range(8)], 0)
    ys = np.stack([R[0]["ys"], R[4]["ys"]], 0)
    ka = np.concatenate([R[c]["oka"].reshape(NPS, DEPTH, TP, 2, 64) for c in range(8)], 0)
    va = np.concatenate([R[c]["ova"].reshape(NPS, DEPTH, TP, 2, 64) for c in range(8)], 0)
    sb = np.concatenate([R[c]["osb"] for c in range(8)], 0)
    kc = np.concatenate([R[c]["okc"].reshape(NPS, DEPTH, TP, 6, 64) for c in range(8)], 0)
    vc = np.concatenate([R[c]["ovc"].reshape(NPS, DEPTH, TP, 6, 64) for c in range(8)], 0)
    return (yp.astype(np.float32), ys.astype(np.float32), ka.astype(np.float32), va.astype(np.float32),
            sb.astype(np.float32), kc.astype(np.float32), vc.astype(np.float32))
```
